# Optimizing a Trainium2 kernel written in Bass

```python
import jax, jax.numpy as jnp
from jax import lax
import numpy as np

D_MODEL = 2048
BATCH = 4
SEQ = 4096
DEPTH = 4

N_META = 16
CHUNK = 64
PAD = CHUNK - N_META
D_FF = 5632
EPS = 1e-6
N_BRANCH = 3

SSD_D_INNER = D_MODEL
SSD_HEAD_DIM = 64
SSD_HEADS = SSD_D_INNER // SSD_HEAD_DIM
SSD_GROUPS = 8
SSD_HPG = SSD_HEADS // SSD_GROUPS
SSD_STATE = 128
SSD_CONV = 4
SSD_GN = SSD_GROUPS * SSD_STATE
SSD_CONV_DIM = SSD_D_INNER + 2 * SSD_GN

GLA_HEADS = 4
GLA_DK = D_MODEL // 2
GLA_DV = D_MODEL
GLA_DK_HEAD = GLA_DK // GLA_HEADS
GLA_DV_HEAD = GLA_DV // GLA_HEADS
GLA_GATE_RANK = 16
GLA_TAU = 16.0

S5_WIDTH = D_MODEL
S5_GROUP = 16
S5_GROUPS = S5_WIDTH // S5_GROUP
S5_STATE = 64

IN_SIZES = (SSD_D_INNER, SSD_CONV_DIM, SSD_HEADS,
            GLA_DK, GLA_DK, GLA_DV, GLA_DV, GLA_GATE_RANK,
            S5_WIDTH, N_BRANCH * D_MODEL)
D_IN = sum(IN_SIZES)

kernel_name = "hybrid_ssd_gla_s5_macaron_trunk"


def rms_norm(x, w):
    xf = x.astype(jnp.float32)
    y = xf * lax.rsqrt(jnp.mean(xf * xf, axis=-1, keepdims=True) + EPS)
    return (y * w.astype(jnp.float32)).astype(x.dtype)


def group_rms_norm(x, w, groups):
    xf = x.astype(jnp.float32)
    sh = xf.shape
    xg = xf.reshape(sh[:-1] + (groups, sh[-1] // groups))
    xg = xg * lax.rsqrt(jnp.mean(xg * xg, axis=-1, keepdims=True) + EPS)
    return xg.reshape(sh) * w.astype(jnp.float32)


def swiglu(x, w_gu, w_down):
    g, u = jnp.split(x @ w_gu, 2, axis=-1)
    return (jax.nn.silu(g) * u) @ w_down


def split_cols(p):
    offs = [int(o) for o in np.cumsum(IN_SIZES)[:-1]]
    return jnp.split(p, offs, axis=-1)


def causal_depthwise_conv(x, w, b):
    c = x.shape[-1]
    out = lax.conv_general_dilated(x, w[:, None, :], window_strides=(1,),
                                   padding=((w.shape[0] - 1, 0),),
                                   dimension_numbers=("NWC", "WIO", "NWC"),
                                   feature_group_count=c)
    return out + b


def pad_front(t):
    return jnp.pad(t, [(0, 0), (PAD, 0)] + [(0, 0)] * (t.ndim - 2))


def to_chunks(t):
    return t.reshape((t.shape[0], t.shape[1] // CHUNK, CHUNK) + t.shape[2:])


def ssd_chunked(xdt, a, bm, cm):
    bsz, lp = xdt.shape[:2]
    xdt, a, bm, cm = to_chunks(xdt), to_chunks(a), to_chunks(bm), to_chunks(cm)
    a_cs = jnp.cumsum(a, axis=2)
    causal = jnp.tril(jnp.ones((CHUNK, CHUNK), dtype=bool))
    seg = a_cs[:, :, :, None] - a_cs[:, :, None, :]
    decay = jnp.exp(jnp.where(causal[:, :, None, None], seg, -jnp.inf))
    cb = jnp.einsum("bclgn,bcsgn->bclsg", cm, bm)
    y_diag = jnp.einsum("bclsgk,bcsgkp->bclgkp", cb[..., None] * decay, xdt)
    x_end = xdt * jnp.exp(a_cs[:, :, -1:] - a_cs)[..., None]

    def step(s, inp):
        c_q, b_k, x_w, a_c = inp
        y_off = jnp.einsum("blgn,bgkpn->blgkp", c_q, s) * jnp.exp(a_c)[..., None]
        s = s * jnp.exp(a_c[:, -1])[..., None, None] + jnp.einsum("blgn,blgkp->bgkpn", b_k, x_w)
        return s, y_off

    s0 = jnp.zeros((bsz, SSD_GROUPS, SSD_HPG, SSD_HEAD_DIM, SSD_STATE), xdt.dtype)
    xs = (jnp.moveaxis(cm, 1, 0), jnp.moveaxis(bm, 1, 0),
          jnp.moveaxis(x_end, 1, 0), jnp.moveaxis(a_cs, 1, 0))
    _, y_off = lax.scan(step, s0, xs)
    y = y_diag + jnp.moveaxis(y_off, 0, 1)
    return y.reshape((bsz, lp) + y.shape[3:])


def gla_chunked(q, k, v, g):
    bsz, lp = q.shape[:2]
    q, k, v, g = to_chunks(q), to_chunks(k), to_chunks(v), to_chunks(g)
    g_cs = jnp.cumsum(g, axis=2)
    g_end = g_cs[:, :, -1:]
    g_mid = g_cs[:, :, CHUNK // 2 - 1:CHUNK // 2]
    causal = jnp.tril(jnp.ones((CHUNK, CHUNK), dtype=bool))
    att = jnp.einsum("bclhd,bcshd->bchls", q * jnp.exp(g_cs - g_mid), k * jnp.exp(g_mid - g_cs))
    att = jnp.where(causal, att, 0.0)
    o_intra = jnp.einsum("bchls,bcshv->bclhv", att, v)
    q_dec = q * jnp.exp(g_cs)
    k_dec = k * jnp.exp(g_end - g_cs)
    c_dec = jnp.exp(g_end[:, :, 0])

    def step(s, inp):
        qd, kd, vc, dc = inp
        o = jnp.einsum("blhd,bhdv->blhv", qd, s)
        s = s * dc[..., None] + jnp.einsum("blhd,blhv->bhdv", kd, vc)
        return s, o

    s0 = jnp.zeros((bsz, GLA_HEADS, GLA_DK_HEAD, GLA_DV_HEAD), q.dtype)
    xs = (jnp.moveaxis(q_dec, 1, 0), jnp.moveaxis(k_dec, 1, 0),
          jnp.moveaxis(v, 1, 0), jnp.moveaxis(c_dec, 1, 0))
    _, o_inter = lax.scan(step, s0, xs)
    o = o_intra + jnp.moveaxis(o_inter, 0, 1)
    return o.reshape((bsz, lp) + o.shape[3:])


def complex_affine_combine(e1, e2):
    a1r, a1i, b1r, b1i = e1
    a2r, a2i, b2r, b2i = e2
    return (a1r * a2r - a1i * a2i,
            a1r * a2i + a1i * a2r,
            a2r * b1r - a2i * b1i + b2r,
            a2r * b1i + a2i * b1r + b2i)


def s5_scan(u, lam_re, lam_im, log_dt, b_re, b_im, c_re, c_im):
    dt = jnp.exp(log_dt)[:, None]
    mag = jnp.exp(lam_re * dt)
    ang = lam_im * dt
    lb_re, lb_im = mag * jnp.cos(ang), mag * jnp.sin(ang)
    den = lam_re * lam_re + lam_im * lam_im
    nr = lb_re - 1.0
    co_re = (nr * lam_re + lb_im * lam_im) / den
    co_im = (lb_im * lam_re - nr * lam_im) / den
    bb_re = co_re[..., None] * b_re - co_im[..., None] * b_im
    bb_im = co_re[..., None] * b_im + co_im[..., None] * b_re
    bu_re = jnp.einsum("blgi,gpi->blgp", u, bb_re)
    bu_im = jnp.einsum("blgi,gpi->blgp", u, bb_im)
    seq_len = u.shape[1]
    a_re = jnp.broadcast_to(lb_re, (1, seq_len) + lb_re.shape)
    a_im = jnp.broadcast_to(lb_im, (1, seq_len) + lb_im.shape)
    _, _, x_re, x_im = lax.associative_scan(complex_affine_combine, (a_re, a_im, bu_re, bu_im), axis=1)
    return jnp.einsum("gip,blgp->blgi", c_re, x_re) - jnp.einsum("gip,blgp->blgi", c_im, x_im)


def token_mixer(u, w_in, conv_w, conv_b, dt_bias, a_log, ssd_d, ssd_norm_w,
                gla_w2, gla_b, gla_norm_w,
                s5_lam_re, s5_lam_im, s5_log_dt, s5_b_re, s5_b_im, s5_c_re, s5_c_im,
                s5_d, s5_glu_w, s5_glu_b, w_branch, w_out):
    f32 = jnp.float32
    bsz, seq_len, _ = u.shape
    z, xbc, dt_raw, q, k, v, r, g_lr, s5_u, gate_raw = split_cols(u @ w_in)

    xbc = jax.nn.silu(causal_depthwise_conv(xbc, conv_w, conv_b)).astype(f32)
    xs = xbc[..., :SSD_D_INNER].reshape(bsz, seq_len, SSD_GROUPS, SSD_HPG, SSD_HEAD_DIM)
    bs = xbc[..., SSD_D_INNER:SSD_D_INNER + SSD_GN].reshape(bsz, seq_len, SSD_GROUPS, SSD_STATE)
    cs = xbc[..., SSD_D_INNER + SSD_GN:].reshape(bsz, seq_len, SSD_GROUPS, SSD_STATE)
    dt = jax.nn.softplus(dt_raw.astype(f32) + dt_bias.astype(f32)).reshape(bsz, seq_len, SSD_GROUPS, SSD_HPG)
    a_head = -jnp.exp(a_log.astype(f32)).reshape(SSD_GROUPS, SSD_HPG)
    y = ssd_chunked(pad_front(xs * dt[..., None]), pad_front(dt * a_head),
                    pad_front(bs), pad_front(cs))[:, PAD:]
    y = y + ssd_d.astype(f32).reshape(SSD_GROUPS, SSD_HPG)[:, :, None] * xs
    y = y.reshape(bsz, seq_len, SSD_D_INNER) * jax.nn.silu(z.astype(f32))
    y_ssd = group_rms_norm(y, ssd_norm_w, SSD_GROUPS).astype(u.dtype)

    qh = q.astype(f32).reshape(bsz, seq_len, GLA_HEADS, GLA_DK_HEAD) * (GLA_DK_HEAD ** -0.5)
    kh = k.astype(f32).reshape(bsz, seq_len, GLA_HEADS, GLA_DK_HEAD)
    vh = v.astype(f32).reshape(bsz, seq_len, GLA_HEADS, GLA_DV_HEAD)
    log_a = jax.nn.log_sigmoid((g_lr @ gla_w2 + gla_b).astype(f32)) / GLA_TAU
    log_a = log_a.reshape(bsz, seq_len, GLA_HEADS, GLA_DK_HEAD)
    o = gla_chunked(pad_front(qh), pad_front(kh), pad_front(vh), pad_front(log_a))[:, PAD:]
    o = group_rms_norm(o.reshape(bsz, seq_len, GLA_DV), gla_norm_w, GLA_HEADS)
    y_gla = (o * jax.nn.silu(r.astype(f32))).astype(u.dtype)

    s5_in = s5_u.astype(f32)
    ys = s5_scan(s5_in.reshape(bsz, seq_len, S5_GROUPS, S5_GROUP),
                 s5_lam_re.astype(f32), s5_lam_im.astype(f32), s5_log_dt.astype(f32),
                 s5_b_re.astype(f32), s5_b_im.astype(f32), s5_c_re.astype(f32), s5_c_im.astype(f32))
    ys = ys.reshape(bsz, seq_len, S5_WIDTH) + s5_d.astype(f32) * s5_in
    hg = jax.nn.gelu(ys)
    y_s5 = (hg * jax.nn.sigmoid(hg @ s5_glu_w.astype(f32) + s5_glu_b.astype(f32))).astype(u.dtype)

    branches = jnp.einsum("blnc,ncd->blnd", jnp.stack([y_ssd, y_gla, y_s5], axis=2), w_branch)
    gates = jax.nn.sigmoid(gate_raw.reshape(bsz, seq_len, N_BRANCH, D_MODEL))
    merged = jnp.sum(gates * branches, axis=2)
    return merged @ w_out


def setup_inputs(seed: int = 0) -> dict:
    key = jax.random.key(seed)
    ks = iter(jax.random.split(key, 48))
    f32 = jnp.float32

    def nrm(shape, scale):
        return jax.random.normal(next(ks), shape, f32) * scale

    def gain(shape):
        return 1.0 + 0.05 * jax.random.normal(next(ks), shape, f32)

    def log_uniform(shape, lo, hi):
        return jax.random.uniform(next(ks), shape, f32, np.log(lo), np.log(hi))

    d = DEPTH
    dt0 = jnp.exp(log_uniform((d, SSD_HEADS), 1e-3, 1e-1))
    inp = {}
    inp["x"] = nrm((BATCH, SEQ, D_MODEL), 1.0)
    inp["meta"] = nrm((N_META, D_MODEL), 1.0)
    inp["ln_f1_pre"] = gain((d, D_MODEL))
    inp["w_f1_gu"] = nrm((d, D_MODEL, 2 * D_FF), D_MODEL ** -0.5)
    inp["w_f1_down"] = nrm((d, D_FF, D_MODEL), D_FF ** -0.5)
    inp["ln_f1_post"] = gain((d, D_MODEL))
    inp["ln_mix_pre"] = gain((d, D_MODEL))
    inp["w_in"] = nrm((d, D_MODEL, D_IN), D_MODEL ** -0.5)
    inp["conv_w"] = nrm((d, SSD_CONV, SSD_CONV_DIM), SSD_CONV ** -0.5)
    inp["conv_b"] = nrm((d, SSD_CONV_DIM), 0.02)
    inp["dt_bias"] = dt0 + jnp.log(-jnp.expm1(-dt0))
    inp["a_log"] = jnp.log(jax.random.uniform(next(ks), (d, SSD_HEADS), f32, 1.0, 16.0))
    inp["ssd_d"] = gain((d, SSD_HEADS))
    inp["ssd_norm_w"] = gain((d, SSD_D_INNER))
    inp["gla_w2"] = nrm((d, GLA_GATE_RANK, GLA_DK), GLA_GATE_RANK ** -0.5)
    inp["gla_b"] = nrm((d, GLA_DK), 0.02)
    inp["gla_norm_w"] = gain((d, GLA_DV))
    inp["s5_lam_re"] = -0.5 + nrm((d, S5_GROUPS, S5_STATE), 0.01)
    inp["s5_lam_im"] = jnp.pi * jnp.arange(S5_STATE, dtype=f32) + nrm((d, S5_GROUPS, S5_STATE), 0.01)
    inp["s5_log_dt"] = log_uniform((d, S5_GROUPS), 1e-3, 1e-1)
    inp["s5_b_re"] = nrm((d, S5_GROUPS, S5_STATE, S5_GROUP), (2 * S5_GROUP) ** -0.5)
    inp["s5_b_im"] = nrm((d, S5_GROUPS, S5_STATE, S5_GROUP), (2 * S5_GROUP) ** -0.5)
    inp["s5_c_re"] = nrm((d, S5_GROUPS, S5_GROUP, S5_STATE), (2 * S5_STATE) ** -0.5)
    inp["s5_c_im"] = nrm((d, S5_GROUPS, S5_GROUP, S5_STATE), (2 * S5_STATE) ** -0.5)
    inp["s5_d"] = nrm((d, S5_WIDTH), 1.0)
    inp["s5_glu_w"] = nrm((d, S5_WIDTH, S5_WIDTH), S5_WIDTH ** -0.5)
    inp["s5_glu_b"] = nrm((d, S5_WIDTH), 0.02)
    inp["w_branch"] = nrm((d, N_BRANCH, D_MODEL, D_MODEL), D_MODEL ** -0.5)
    inp["w_out"] = nrm((d, D_MODEL, D_MODEL), D_MODEL ** -0.5)
    inp["ln_mix_post"] = gain((d, D_MODEL))
    inp["ln_f2_pre"] = gain((d, D_MODEL))
    inp["w_f2_gu"] = nrm((d, D_MODEL, 2 * D_FF), D_MODEL ** -0.5)
    inp["w_f2_down"] = nrm((d, D_FF, D_MODEL), D_FF ** -0.5)
    inp["ln_f2_post"] = gain((d, D_MODEL))
    return inp


def reference(x, meta, ln_f1_pre, w_f1_gu, w_f1_down, ln_f1_post, ln_mix_pre, w_in,
              conv_w, conv_b, dt_bias, a_log, ssd_d, ssd_norm_w, gla_w2, gla_b, gla_norm_w,
              s5_lam_re, s5_lam_im, s5_log_dt, s5_b_re, s5_b_im, s5_c_re, s5_c_im,
              s5_d, s5_glu_w, s5_glu_b, w_branch, w_out, ln_mix_post,
              ln_f2_pre, w_f2_gu, w_f2_down, ln_f2_post):
    bsz = x.shape[0]
    h = jnp.concatenate([jnp.broadcast_to(meta.astype(x.dtype)[None], (bsz, N_META, D_MODEL)), x], axis=1)
    for i in range(DEPTH):
        h = h + 0.5 * rms_norm(swiglu(rms_norm(h, ln_f1_pre[i]), w_f1_gu[i], w_f1_down[i]), ln_f1_post[i])
        mix = token_mixer(rms_norm(h, ln_mix_pre[i]), w_in[i], conv_w[i], conv_b[i], dt_bias[i], a_log[i],
                          ssd_d[i], ssd_norm_w[i], gla_w2[i], gla_b[i], gla_norm_w[i],
                          s5_lam_re[i], s5_lam_im[i], s5_log_dt[i], s5_b_re[i], s5_b_im[i],
                          s5_c_re[i], s5_c_im[i], s5_d[i], s5_glu_w[i], s5_glu_b[i],
                          w_branch[i], w_out[i])
        h = h + rms_norm(mix, ln_mix_post[i])
        h = h + 0.5 * rms_norm(swiglu(rms_norm(h, ln_f2_pre[i]), w_f2_gu[i], w_f2_down[i]), ln_f2_post[i])
    return h[:, N_META:]
```

```python
import numpy as np
import concourse.bass as bass
import concourse.mybir as mybir
from concourse.bass_utils import run_bass_kernel_spmd
from contextlib import ExitStack

F32 = mybir.dt.float32; BF16 = mybir.dt.bfloat16; I32 = mybir.dt.int32
AF = mybir.ActivationFunctionType; ALU = mybir.AluOpType

D = 2048; NKB = 16; DFF = 5632; NFB = 44; NMETA = 16; EPS = 1e-6
SEGS = [("z", 2048), ("xbc", 4096), ("dt", 32), ("q", 1024), ("k", 1024), ("v", 2048), ("r", 2048),
        ("glr", 16), ("s5u", 2048), ("gate", 6144)]
DIN = sum(w for _, w in SEGS)
SEGOFF = {}
_o = 0
for _n, _w in SEGS:
    SEGOFF[_n] = _o; _o += _w


class Buf:
    def __init__(s, name, t):
        s.name = name; s.t = t
        s.last_w = None
        s.readers = {}
        s.sem = None; s.dma_cnt = 0

    def __getitem__(s, idx):
        return V(s, s.t[idx])


class V:
    def __init__(s, buf, ap):
        s.buf = buf; s.ap = ap

    def __getitem__(s, idx):
        return V(s.buf, s.ap[idx])

    def re(s, pat, **kw):
        return V(s.buf, s.ap.rearrange(pat, **kw))


class Op:
    __slots__ = ("eng", "fn", "deps", "dma_dst", "needed", "tok")

    def __init__(s, eng, fn):
        s.eng = eng; s.fn = fn; s.deps = []; s.dma_dst = None; s.needed = False; s.tok = None


class Prog:
    ENGS = ("pe", "dve", "act", "pool", "sp")

    def __init__(s, nc, es):
        s.nc = nc; s.es = es
        s.ops = {e: [] for e in s.ENGS}
        s.esem = {e: es.enter_context(nc.semaphore("s_" + e)) for e in ("pe", "dve", "act", "pool")}
        s.nsem = 4
        s.bufs = {}

    def sbuf(s, name, shape, dt):
        return Buf(name, s.es.enter_context(s.nc.sbuf_tensor(name, list(shape), dt)))

    def psum(s, name, shape, dt=F32):
        return Buf(name, s.es.enter_context(s.nc.psum_tensor(name, list(shape), dt)))

    def dram(s, name, shape, dt, kind="Internal"):
        return Buf(name, s.nc.dram_tensor(name, list(shape), dt, kind=kind).ap())

    def region(s, key, ap):
        if key not in s.bufs:
            s.bufs[key] = Buf(str(key), None)
        return V(s.bufs[key], ap)

    def op(s, eng, fn, reads=(), writes=(), dma=False):
        o = Op(eng, fn)
        deps = {}

        def add(d):
            if d is not None and d is not o:
                deps[id(d)] = d
        for v in reads:
            add(v.buf.last_w)
        for v in writes:
            add(v.buf.last_w)
            for r in v.buf.readers.values():
                add(r)
        for v in writes:
            v.buf.last_w = o; v.buf.readers = {}
        for v in reads:
            if v.buf.last_w is o:
                continue
            key = ("dma", len(v.buf.readers)) if dma else eng
            v.buf.readers[key] = o
        for d in deps.values():
            if d.eng == "pe" and eng == "pe" and d.dma_dst is None and not dma:
                continue
            d.needed = True
            o.deps.append(d)
        if dma:
            b = writes[0].buf
            if b.sem is None:
                b.sem = s.es.enter_context(s.nc.semaphore("d_%d" % s.nsem)); s.nsem += 1
            b.dma_cnt += 1
            o.dma_dst = b
            o.tok = (b.sem, 16 * b.dma_cnt)
        s.ops[eng].append(o)
        return o

    def dma(s, out, in_, eng="sp", extra_reads=()):
        return s.op(eng, lambda e: e.dma_start(out=out.ap, in_=in_.ap), reads=[in_] + list(extra_reads), writes=[out], dma=True)

    def fence(s, bufs):
        vs = [V(b, None) for b in bufs]
        return s.op("dve", lambda e: e.memset(s.fence_ap, 0.0), reads=vs, writes=vs + [V(s.fence_buf, None)])

    def mm(s, out, lhsT, rhs, start=True, stop=True):
        rw = [] if start else [out]
        return s.op("pe", lambda e: e.matmul(out.ap, lhsT.ap, rhs.ap, start=start, stop=stop),
                    reads=[lhsT, rhs] + rw, writes=[out])

    def transpose(s, out, in_, ident):
        return s.op("pe", lambda e: e.transpose(out.ap, in_.ap, ident.ap), reads=[in_, ident], writes=[out])

    def act(s, out, in_, func, bias=None, scale=1.0):
        kw = {}
        rd = [in_]
        if bias is not None:
            if isinstance(bias, V):
                kw["bias"] = bias.ap; rd.append(bias)
            else:
                kw["bias"] = bias
        if isinstance(scale, V):
            rd.append(scale); sc = scale.ap
        else:
            sc = scale
        return s.op("act", lambda e: e.activation(out=out.ap, in_=in_.ap, func=func, scale=sc, **kw),
                    reads=rd, writes=[out])

    def tt(s, out, a, b, op, eng="dve"):
        return s.op(eng, lambda e: e.tensor_tensor(out=out.ap, in0=a.ap, in1=b.ap, op=op), reads=[a, b], writes=[out])

    def ts(s, out, a, s1, op0, s2=None, op1=None, eng="dve"):
        rd = [a]
        s1a = s1.ap if isinstance(s1, V) else s1
        s2a = s2.ap if isinstance(s2, V) else s2
        if isinstance(s1, V): rd.append(s1)
        if isinstance(s2, V): rd.append(s2)
        if op1 is not None:
            return s.op(eng, lambda e: e.tensor_scalar(out=out.ap, in0=a.ap, scalar1=s1a, scalar2=s2a, op0=op0, op1=op1),
                        reads=rd, writes=[out])
        return s.op(eng, lambda e: e.tensor_scalar(out=out.ap, in0=a.ap, scalar1=s1a, scalar2=None, op0=op0),
                    reads=rd, writes=[out])

    def stt(s, out, a, sc, b, op0, op1):
        rd = [a, b]
        sca = sc.ap if isinstance(sc, V) else sc
        if isinstance(sc, V): rd.append(sc)
        return s.op("dve", lambda e: e.scalar_tensor_tensor(out=out.ap, in0=a.ap, scalar=sca, in1=b.ap, op0=op0, op1=op1),
                    reads=rd, writes=[out])

    def scan(s, out, d0, d1, init, op0=ALU.mult, op1=ALU.add):
        rd = [d0, d1]
        ia = init.ap if isinstance(init, V) else init
        if isinstance(init, V): rd.append(init)
        return s.op("dve", lambda e: e.tensor_tensor_scan(out=out.ap, data0=d0.ap, data1=d1.ap, initial=ia, op0=op0, op1=op1),
                    reads=rd, writes=[out])

    def recip(s, out, in_):
        return s.op("dve", lambda e: e.reciprocal(out=out.ap, in_=in_.ap), reads=[in_], writes=[out])

    def copy(s, out, in_, eng="dve"):
        if eng == "act":
            return s.op("act", lambda e: e.copy(out=out.ap, in_=in_.ap), reads=[in_], writes=[out])
        return s.op(eng, lambda e: e.tensor_copy(out=out.ap, in_=in_.ap), reads=[in_], writes=[out])

    def memset(s, out, val, eng="pool"):
        return s.op(eng, lambda e: e.memset(out.ap, val), reads=[], writes=[out])

    def emit(s, final_ops):
        nc = s.nc
        for e in ("pe", "dve", "act", "pool"):
            c = 0
            for o in s.ops[e]:
                if o.dma_dst is None and o.needed:
                    c += 1
                    o.tok = (s.esem[e], c)
        engmap = {"pe": "tensor", "dve": "vector", "act": "scalar", "pool": "gpsimd", "sp": "sync"}
        stats = {}
        with nc.Block() as block:
            for e in s.ENGS:
                def body(eng, ops=s.ops[e], e=e):
                    waited = {}
                    nw = 0
                    for o in ops:
                        need = {}
                        for d in o.deps:
                            sem, val = d.tok
                            k = id(sem)
                            if waited.get(k, 0) >= val:
                                continue
                            if k not in need or need[k][1] < val:
                                need[k] = (sem, val)
                        for k, (sem, val) in need.items():
                            eng.wait_ge(sem, val); waited[k] = val; nw += 1
                        ins = o.fn(eng)
                        if o.dma_dst is not None:
                            ins.then_inc(o.dma_dst.sem, 16)
                        elif o.needed:
                            ins.then_inc(s.esem[e], 1)
                    if e == "sp":
                        for o in final_ops:
                            sem, val = o.tok
                            eng.wait_ge(sem, val)
                    stats[e] = (len(ops), nw)
                getattr(block, engmap[e])(body)
        return stats


class WMat:
    def __init__(s, P, name, src_ap, K, N, grp):
        s.K = K; s.N = N
        s.nkb = K // 128
        s.NC = (s.nkb + 15) // 16
        s.NP = (N + 511) // 512
        s.grp = grp
        s.t = P.nc.dram_tensor(name, [s.NP, s.NC, 128, 16, 512], BF16, kind="Internal").ap()
        for pn in range(s.NP):
            w = min(512, N - pn * 512)
            for kc in range(s.NC):
                nk = min(16, s.nkb - kc * 16)
                src = src_ap[kc * 2048:kc * 2048 + nk * 128, pn * 512:pn * 512 + w].rearrange("(kb p) c -> p kb c", p=128)
                P.dma(V(grp, s.t[pn, kc, :, 0:nk, 0:w]), V(P.wsrc, src), eng="pool")


def build_program(seq, depth, dbg=None, stop_after=None, skip_ffn=False, skip_gla=False):
    T = NMETA + seq
    tiles = [(0, NMETA)] + [(NMETA + 512 * i, 512) for i in range(seq // 512)]
    assert seq % 512 == 0
    NT = len(tiles)
    nc = bass.Bass("TRN2", target_bir_lowering=False)
    es = ExitStack()
    with es:
        P = Prog(nc, es)
        dbg = dbg or []

        def din(name, shape):
            return nc.dram_tensor(name, list(shape), F32, kind="ExternalInput").ap()
        x_in = din("x", [seq, D]); meta_in = din("meta", [NMETA, D])
        wi = {}
        for nm, shp in [("ln_f1_pre", [depth, D]), ("w_f1_gu", [depth, D, 2 * DFF]), ("w_f1_down", [depth, DFF, D]),
                        ("ln_f1_post", [depth, D]), ("ln_mix_pre", [depth, D]), ("w_in", [depth, D, DIN]),
                        ("conv_w", [depth, 4, 4096]), ("conv_b", [depth, 4096]), ("dt_bias", [depth, 32]),
                        ("a_log", [depth, 32]), ("ssd_d", [depth, 32]), ("ssd_norm_w", [depth, D]),
                        ("gla_w2", [depth, 16, 1024]), ("gla_b", [depth, 1024]), ("gla_norm_w", [depth, D]),
                        ("s5_lam_re", [depth, 128, 64]), ("s5_lam_im", [depth, 128, 64]), ("s5_log_dt", [depth, 128]),
                        ("s5_b_re", [depth, 128, 64, 16]), ("s5_b_im", [depth, 128, 64, 16]),
                        ("s5_c_re", [depth, 128, 16, 64]), ("s5_c_im", [depth, 128, 16, 64]),
                        ("s5_d", [depth, D]), ("s5_glu_w", [depth, D, D]), ("s5_glu_b", [depth, D]),
                        ("w_branch", [depth, 3, D, D]), ("w_out", [depth, D, D]), ("ln_mix_post", [depth, D]),
                        ("ln_f2_pre", [depth, D]), ("w_f2_gu", [depth, D, 2 * DFF]), ("w_f2_down", [depth, DFF, D]),
                        ("ln_f2_post", [depth, D])]:
            wi[nm] = din(nm, shp)
        out_d = nc.dram_tensor("out", [seq, D], F32, kind="ExternalOutput").ap()
        P.wsrc = Buf("wsrc", None)
        ext = P.wsrc

        def scratch(name, shape, dt):
            kind = "ExternalOutput" if name in dbg else "Internal"
            return nc.dram_tensor(name, list(shape), dt, kind=kind).ap()
        hT = scratch("hT", [D, T], F32)

        def hreg(j, kb=None):
            t0, n = tiles[j]
            if kb is None:
                return P.region(("h", j), hT[:, t0:t0 + n].rearrange("(kb p) n -> p kb n", p=128))
            return P.region(("h", j), hT[kb * 128:(kb + 1) * 128, t0:t0 + n])

        WT = [P.sbuf("wt%d" % i, [128, 16, 512], BF16) for i in range(2)]
        XA = P.sbuf("xa", [128, 16, 512], F32)
        XN = P.sbuf("xn", [128, 16, 512], BF16)
        BIG = P.sbuf("big", [128, 44, 512], BF16)
        SQ = [P.sbuf("sq%d" % i, [128, 512], BF16) for i in range(2)]
        T1 = [P.sbuf("t1_%d" % i, [128, 512], F32) for i in range(3)]
        RSTD = P.sbuf("rstd", [128, 512], F32)
        HB = [P.sbuf("hb%d" % i, [128, 512], F32) for i in range(2)]
        ONES = P.sbuf("ones", [128, 128], BF16)
        IDF = P.sbuf("idf", [128, 128], F32)
        NAT = P.sbuf("nat", [128, 3, 128], F32)
        CHV = P.sbuf("chv", [128, depth, 384], F32)
        PS = [P.psum("ps%d" % i, [128, 512]) for i in range(8)]
        IDB = P.sbuf("idb", [128, 128], BF16)
        CV = [P.sbuf("cv%d" % i, [128, 520], F32) for i in range(2)]
        YB = [P.sbuf("yb%d" % i, [128, 512], BF16) for i in range(2)]
        ST32 = P.sbuf("st32", [128, 4096], F32)
        STB = P.sbuf("stb", [128, 4096], BF16)
        MASK64 = P.sbuf("mask64", [128, 512], F32)
        MASKNEG = P.sbuf("maskneg", [64, 64], F32)
        CAUS = P.sbuf("caus", [64, 64], F32)
        SMALL = P.sbuf("small", [128, 256], F32)
        DTK = P.sbuf("dtk", [64, 64], F32)
        EAEND = P.sbuf("eaend", [128, 32], F32)
        VST = P.sbuf("vst", [128, 1024], F32)

        def subbuf(name, base, e0, e1, dt, shape=None):
            ap = base.t[:, :, :].rearrange("p a b -> p (a b)")[:, e0:e1]
            if dt == F32:
                ap = ap.bitcast(F32)
            return Buf(name, ap)
        wt_i = [0]

        P.memset(ONES[:, :], 1.0)
        P.memset(MASK64[:, :], 1.0)
        P.memset(V(MASK64, MASK64.t[:, :].rearrange("p (c l) -> p c l", l=64)[:, :, 0:1]), 0.0)
        P.memset(MASKNEG[:, :], 0.0)
        P.op("pool", lambda e: e.affine_select(out=MASKNEG.t[:, :], in_=MASKNEG.t[:, :], pattern=[[1, 64]],
                                               compare_op=ALU.is_ge, fill=-30000.0, base=0, channel_multiplier=-1),
             reads=[MASKNEG[:, :]], writes=[MASKNEG[:, :]])
        P.memset(CAUS[:, :], 1.0)
        P.op("pool", lambda e: e.affine_select(out=CAUS.t[:, :], in_=CAUS.t[:, :], pattern=[[1, 64]],
                                               compare_op=ALU.is_ge, fill=0.0, base=0, channel_multiplier=-1),
             reads=[CAUS[:, :]], writes=[CAUS[:, :]])
        P.memset(IDF[:, :], 1.0)
        P.op("pool", lambda e: e.affine_select(out=IDF.t[:, :], in_=IDF.t[:, :], pattern=[[-1, 128]],
                                               compare_op=ALU.is_equal, fill=0.0, base=0, channel_multiplier=1),
             reads=[IDF[:, :]], writes=[IDF[:, :]])
        CH_LAYOUT = [("ln_f1_pre", 16), ("ln_f1_post", 16), ("ln_mix_pre", 16), ("ln_mix_post", 16), ("ln_f2_pre", 16),
                     ("ln_f2_post", 16), ("ssd_norm_w", 16), ("gla_norm_w", 16), ("s5_d", 16), ("s5_glu_b", 16),
                     ("conv_b", 32), ("conv_w0", 32), ("conv_w1", 32), ("conv_w2", 32), ("conv_w3", 32), ("gla_b", 8)]
        CHO = {}
        _c = 0
        for nm, m_ in CH_LAYOUT:
            CHO[nm] = _c; _c += m_
        NCH = 384
        for L in range(depth):
            nat = NAT
            P.memset(nat[:, :, :], 0.0)
            for nm, m_ in CH_LAYOUT:
                if nm.startswith("conv_w"):
                    src = wi["conv_w"][L, int(nm[-1])]
                else:
                    src = wi[nm][L]
                c0 = CHO[nm]
                r = 0
                while r < m_:
                    g, p0 = divmod(c0 + r, 128)
                    cnt = min(m_ - r, 128 - p0)
                    P.dma(nat[p0:p0 + cnt, g, :], V(ext, src[r * 128:(r + cnt) * 128].rearrange("(kb p) -> kb p", p=128)))
                    r += cnt
            for g in range(3):
                P.transpose(PS[g][:, 0:128], nat[:, g, :], IDF[:, :])
                P.copy(CHV[:, L, g * 128:(g + 1) * 128], PS[g][:, 0:128])

        P.copy(IDB[:, :], IDF[:, :])

        def chv(L, nm, kb):
            c = CHO[nm] + kb
            return CHV[:, L, c:c + 1]
        WM = {}
        for L in range(depth):
            g1 = Buf("wg_f1_%d" % L, None); gm = Buf("wg_mx_%d" % L, None); g2 = Buf("wg_f2_%d" % L, None)
            WM[L, "f1_gu"] = WMat(P, "wb_f1gu%d" % L, wi["w_f1_gu"][L], D, 2 * DFF, g1)
            WM[L, "f1_dn"] = WMat(P, "wb_f1dn%d" % L, wi["w_f1_down"][L], DFF, D, g1)
            if stop_after != "ffn1":
                for nm, w in SEGS:
                    o = SEGOFF[nm]
                    WM[L, "in_" + nm] = WMat(P, "wb_in_%s%d" % (nm, L), wi["w_in"][L][:, o:o + w], D, w, gm)
                for b in range(3):
                    WM[L, "br%d" % b] = WMat(P, "wb_br%d_%d" % (b, L), wi["w_branch"][L, b], D, D, gm)
                WM[L, "out"] = WMat(P, "wb_out%d" % L, wi["w_out"][L], D, D, gm)
                WM[L, "glu"] = WMat(P, "wb_glu%d" % L, wi["s5_glu_w"][L], D, D, gm)
                WM[L, "f2_gu"] = WMat(P, "wb_f2gu%d" % L, wi["w_f2_gu"][L], D, 2 * DFF, g2)
                WM[L, "f2_dn"] = WMat(P, "wb_f2dn%d" % L, wi["w_f2_down"][L], DFF, D, g2)

        def load_transposed(j):
            t0, n = tiles[j]
            for s0 in range(0, n, 128):
                m = min(128, n - s0)
                if j == 0:
                    src = meta_in[s0:s0 + m, :]
                else:
                    src = x_in[t0 - NMETA + s0:t0 - NMETA + s0 + m, :]
                tm = V(XA, XA.t[0:m, 0:4, :].rearrange("p a b -> p (a b)"))
                P.dma(tm, V(ext, src))
                for kb in range(NKB):
                    ps = PS[kb % 8]
                    P.transpose(ps[:, 0:m], tm[:, kb * 128:(kb + 1) * 128], IDF[0:m, 0:m])
                    hb = HB[kb % 2]
                    P.copy(hb[:, 0:m], ps[:, 0:m], eng="act" if kb % 2 else "dve")
                    P.dma(P.region(("h", j), hT[kb * 128:(kb + 1) * 128, t0 + s0:t0 + s0 + m]), hb[:, 0:m], eng="sp")
        for j in range(NT):
            load_transposed(j)

        def rstd_from_blocks(blocks, n, nelem, out_rstd):
            ps = PS[7]
            nb = len(blocks)
            for i, b in enumerate(blocks):
                sq = SQ[i % 2]
                P.act(sq[:, 0:n], b, AF.Square)
                P.mm(ps[:, 0:n], ONES[:, :], sq[:, 0:n], start=(i == 0), stop=(i == nb - 1))
            P.act(out_rstd[:, 0:n], ps[:, 0:n], AF.Sqrt, bias=EPS, scale=1.0 / nelem)
            P.recip(out_rstd[:, 0:n], out_rstd[:, 0:n])

        def linear(wm, rhs_of_kb, n, consume, K_nkb=None):
            bank = 0
            for pn in range(wm.NP):
                w = min(512, wm.N - pn * 512)
                ncb = (w + 127) // 128
                banks = [PS[(pn % 2) * 4 + c] for c in range(ncb)] if wm.NC > 1 else None
                for kc in range(wm.NC):
                    nk = min(16, wm.nkb - kc * 16)
                    wt = WT[wt_i[0] % 2]; wt_i[0] += 1
                    P.dma(wt[:, 0:nk, 0:w], V(wm.grp, wm.t[pn, kc, :, 0:nk, 0:w]))
                    for c in range(ncb):
                        cw = min(128, w - c * 128)
                        if wm.NC > 1:
                            ps = banks[c]
                        else:
                            ps = PS[bank % 7]
                        for kb in range(nk):
                            P.mm(ps[0:cw, 0:n], wt[:, kb, c * 128:c * 128 + cw], rhs_of_kb(kc * 16 + kb),
                                 start=(kc == 0 and kb == 0), stop=(kc == wm.NC - 1 and kb == nk - 1))
                        if wm.NC == 1:
                            consume(pn * 4 + c, ps[0:cw, 0:n], cw)
                            bank += 1
                if wm.NC > 1:
                    for c in range(ncb):
                        cw = min(128, w - c * 128)
                        consume(pn * 4 + c, banks[c][0:cw, 0:n], cw)

        def ffn(L, which, ln_pre, ln_post):
            wgu = WM[L, which + "_gu"]; wdn = WM[L, which + "_dn"]
            for j in range(NT):
                t0, n = tiles[j]
                P.dma(XA[:, :, 0:n], hreg(j))
                rstd_from_blocks([XA[:, kb, 0:n] for kb in range(NKB)], n, D, RSTD)
                for kb in range(NKB):
                    P.stt(XN[:, kb, 0:n], XA[:, kb, 0:n], chv(L, ln_pre, kb), RSTD[:, 0:n], ALU.mult, ALU.mult)
                def cons_gate(cb, ps, cw):
                    P.act(BIG[:, cb, 0:n], ps, AF.Silu)

                def cons_up(cb, ps, cw):
                    P.tt(BIG[:, cb, 0:n], BIG[:, cb, 0:n], ps, ALU.mult)
                linear(_sub(wgu, 0, NFB // 4), lambda kb: XN[:, kb, 0:n], n, cons_gate)
                linear(_sub(wgu, NFB // 4, NFB // 2), lambda kb: XN[:, kb, 0:n], n, cons_up)
                def cons_dn(cb, ps, cw):
                    P.copy(XA[:, cb, 0:n], ps, eng="act")
                linear(wdn, lambda kb: BIG[:, kb, 0:n], n, cons_dn)
                rstd_from_blocks([XA[:, kb, 0:n] for kb in range(NKB)], n, D, RSTD)
                for kb in range(NKB):
                    hb = HB[kb % 2]
                    P.dma(hb[:, 0:n], hreg(j, kb))
                    t1 = T1[kb % 3]
                    P.stt(t1[:, 0:n], XA[:, kb, 0:n], chv(L, ln_post, kb), RSTD[:, 0:n], ALU.mult, ALU.mult)
                    P.stt(hb[:, 0:n], t1[:, 0:n], 0.5, hb[:, 0:n], ALU.mult, ALU.add)
                    P.dma(hreg(j, kb), hb[:, 0:n], eng="sp")


        projS = {nm: scratch("proj_" + nm, [w, T], F32) for nm, w in SEGS}

        def proj_ap(r0, r1, a, b):
            for nm, w in SEGS:
                o = SEGOFF[nm]
                if o <= r0 < o + w:
                    assert r1 <= o + w
                    return projS[nm][r0 - o:r1 - o, a:b]
            raise AssertionError
        ybr = scratch("ybr", [3, D, T], BF16)
        ysT = scratch("ysT", [D, T], F32)
        FEN = P.sbuf("fence", [128, 8], F32)
        P.fence_ap = FEN.t[:, 0:1]; P.fence_buf = FEN

        def preg(j, r0, r1, c0=None, c1=None):
            t0, n = tiles[j]
            a = t0 if c0 is None else c0
            b = t0 + n if c1 is None else c1
            return P.region(("proj", j), proj_ap(r0, r1, a, b))

        stage_i = [0]

        def stage():
            bufs = T1 + HB
            b = bufs[stage_i[0] % 5]; stage_i[0] += 1
            return b

        def layer_smalls(L):
            c0 = L * 64
            P.dma(SMALL[0:32, c0:c0 + 1], V(ext, wi["dt_bias"][L].rearrange("(k o) -> k o", o=1)))
            P.dma(SMALL[0:32, c0 + 1:c0 + 2], V(ext, wi["a_log"][L].rearrange("(k o) -> k o", o=1)))
            P.act(SMALL[0:32, c0 + 1:c0 + 2], SMALL[0:32, c0 + 1:c0 + 2], AF.Exp)
            P.ts(SMALL[0:32, c0 + 1:c0 + 2], SMALL[0:32, c0 + 1:c0 + 2], -1.0, ALU.mult)
            P.dma(SMALL[:, c0 + 18:c0 + 50], V(ext, wi["ssd_d"][L].partition_broadcast(128)))
            raw = SMALL.t[:, c0 + 18:c0 + 50].rearrange("p (kb two) -> p kb two", two=2)
            P.copy(SMALL[0:64, c0 + 2:c0 + 18], V(SMALL, raw[0:64, :, 0]))
            P.copy(SMALL[64:128, c0 + 2:c0 + 18], V(SMALL, raw[64:128, :, 1]))

        def inproj(L):
            c0 = L * 64
            for j in range(NT):
                t0, n = tiles[j]
                P.dma(XA[:, :, 0:n], hreg(j))
                rstd_from_blocks([XA[:, kb, 0:n] for kb in range(NKB)], n, D, RSTD)
                for kb in range(NKB):
                    P.stt(XN[:, kb, 0:n], XA[:, kb, 0:n], chv(L, "ln_mix_pre", kb), RSTD[:, 0:n], ALU.mult, ALU.mult)
                for nm, w in SEGS:
                    off = SEGOFF[nm]
                    cnt = [0]

                    def cons(cb, ps, cw, nm=nm, off=off, cnt=cnt):
                        st = stage()
                        sv = st[0:cw, 0:n]
                        if nm in ("z", "r"):
                            P.act(sv, ps, AF.Silu)
                        elif nm == "gate":
                            P.act(sv, ps, AF.Sigmoid)
                        elif nm == "dt":
                            P.act(sv, ps, AF.Exp, bias=SMALL[0:32, c0:c0 + 1])
                            P.act(sv, sv, AF.Ln, bias=1.0)
                        elif nm == "q":
                            P.ts(sv, ps, 1.0 / 16.0, ALU.mult)
                        else:
                            cnt[0] += 1
                            P.copy(sv, ps, eng="act" if cnt[0] % 2 else "dve")
                        P.dma(preg(j, off + cb * 128, off + cb * 128 + cw), sv)
                    linear(WM[L, "in_" + nm], lambda kb: XN[:, kb, 0:n], n, cons)

        def ssd(L):
            c0 = L * 64
            AHEAD = SMALL[0:32, c0 + 1:c0 + 2]
            XDT = subbuf("xdt", BIG, 0, 4096, BF16)
            XW = subbuf("xw", BIG, 4096, 8192, BF16)
            BTK = subbuf("btk", BIG, 8192, 10240, BF16)
            ES = subbuf("es", BIG, 10240, 12288, BF16)
            STT = subbuf("stt", BIG, 12288, 14336, BF16)
            CDEC = subbuf("cdec", BIG, 14336, 16384, BF16)
            ACS = subbuf("acs", BIG, 16384, 17408, F32)
            NACS = subbuf("nacs", BIG, 17408, 18432, F32)
            EA = subbuf("ea", BIG, 18432, 19456, F32)
            DTE = subbuf("dte", BIG, 19456, 20480, F32)
            DTs = subbuf("dts", BIG, 20480, 21504, F32)
            subs = [XDT, XW, BTK, ES, STT, CDEC, ACS, NACS, EA, DTE, DTs]
            P.fence([BIG] + subs)
            S32 = V(ST32, ST32.t[:, 0:2048]); SB = V(STB, STB.t[:, 0:2048])
            P.memset(S32, 0.0); P.memset(SB, 0.0)
            xo = SEGOFF["xbc"]
            for j in range(NT):
                t0, n = tiles[j]
                Lc = min(64, n); NC = n // Lc
                for cb in range(32):
                    cv = CV[cb % 2]
                    r0 = xo + cb * 128
                    if j == 0:
                        P.memset(cv[:, 0:3], 0.0)
                        P.dma(cv[:, 3:3 + n], preg(j, r0, r0 + 128))
                    else:
                        P.dma(cv[:, 0:n + 3], preg(j, r0, r0 + 128, t0 - 3, t0 + n), extra_reads=[preg(j - 1, r0, r0 + 128)])
                    acc = stage()
                    P.ts(acc[:, 0:n], cv[:, 3:3 + n], chv(L, "conv_w3", cb), ALU.mult, chv(L, "conv_b", cb), ALU.add)
                    for k in (2, 1, 0):
                        P.stt(acc[:, 0:n], cv[:, k:k + n], chv(L, "conv_w%d" % k, cb), acc[:, 0:n], ALU.mult, ALU.add)
                    if cb < 16:
                        P.act(XA[:, cb, 0:n], acc[:, 0:n], AF.Silu)
                    else:
                        P.act(XN[:, cb - 16, 0:n], acc[:, 0:n], AF.Silu)
                do = SEGOFF["dt"]
                P.dma(DTs[0:32, 0:n], preg(j, do, do + 32))
                P.ts(NACS[0:32, 0:n], DTs[0:32, 0:n], AHEAD, ALU.mult)
                P.scan(ACS[0:32, 0:n], MASK64[0:32, 0:n], NACS[0:32, 0:n], 0.0)
                P.ts(NACS[0:32, 0:n], ACS[0:32, 0:n], -1.0, ALU.mult)
                P.act(EA[0:32, 0:n], ACS[0:32, 0:n], AF.Exp)
                for c in range(NC):
                    sl = slice(c * Lc, (c + 1) * Lc)
                    P.act(DTE[0:32, sl], ACS[0:32, sl], AF.Exp, bias=ACS[0:32, (c + 1) * Lc - 1:(c + 1) * Lc], scale=-1.0)
                for c in range(NC):
                    sl = slice(c * Lc, (c + 1) * Lc)
                    P.transpose(PS[7][0:Lc, 0:32], DTs[0:32, sl], IDF[0:32, 0:32])
                    P.transpose(PS[7][0:Lc, 32:64], DTE[0:32, sl], IDF[0:32, 0:32])
                    P.copy(DTK[0:Lc, :], PS[7][0:Lc, 0:64])
                    for k in range(32):
                        o_ = PS[k // 8][0:Lc, (k % 8) * Lc:(k % 8 + 1) * Lc]
                        sel_s = V(IDF, IDF.t[0:32, k:k + 1].to_broadcast([32, Lc]))
                        P.mm(o_, sel_s, ACS[0:32, sl], start=True, stop=False)
                        P.mm(o_, NACS[0:32, sl], sel_s, start=False, stop=False)
                        P.mm(o_, IDF[0:Lc, 0:Lc], MASKNEG[0:Lc, 0:Lc], start=False, stop=True)
                    for q in range(4):
                        P.act(ES[0:Lc, q * 512:q * 512 + 8 * Lc], PS[q][0:Lc, 0:8 * Lc], AF.Exp)
                    for g in range(8):
                        P.mm(PS[6][0:Lc, g * Lc:(g + 1) * Lc], XN[:, g, sl], XN[:, 8 + g, sl])
                    for q in range(4):
                        es4 = V(ES, ES.t[0:Lc, q * 512:q * 512 + 8 * Lc].rearrange("p (g k l) -> p g k l", g=2, k=4))
                        st4 = V(STT, STT.t[0:Lc, q * 512:q * 512 + 8 * Lc].rearrange("p (g k l) -> p g k l", g=2, k=4))
                        cb4 = V(PS[6], PS[6].t[0:Lc, q * 2 * Lc:(q * 2 + 2) * Lc].rearrange("p (g l) -> p g l", g=2).unsqueeze(2).to_broadcast([Lc, 2, 4, Lc]))
                        P.tt(st4, es4, cb4, ALU.mult)
                    for kb in range(16):
                        P.transpose(PS[kb // 4][0:Lc, (kb % 4) * 128:(kb % 4 + 1) * 128], XA[:, kb, sl], IDF[:, :])
                    for q in range(4):
                        px = V(PS[q], PS[q].t[0:Lc, :].rearrange("p (k d) -> p k d", k=8))
                        xd = V(XDT, XDT.t[0:Lc, q * 512:(q + 1) * 512].rearrange("p (k d) -> p k d", k=8))
                        xw = V(XW, XW.t[0:Lc, q * 512:(q + 1) * 512].rearrange("p (k d) -> p k d", k=8))
                        dtb = V(DTK, DTK.t[0:Lc, q * 8:(q + 1) * 8].unsqueeze(2).to_broadcast([Lc, 8, 64]))
                        deb = V(DTK, DTK.t[0:Lc, 32 + q * 8:32 + (q + 1) * 8].unsqueeze(2).to_broadcast([Lc, 8, 64]))
                        P.tt(xd, px, dtb, ALU.mult)
                        P.tt(xw, xd, deb, ALU.mult)
                    for k in range(32):
                        sel_p = V(IDF, IDF.t[0:32, k:k + 1].to_broadcast([32, 128]))
                        P.mm(PS[k // 8][:, (k % 8) * Lc:(k % 8 + 1) * Lc], sel_p, EA[0:32, sl])
                    for q in range(4):
                        pe4 = V(PS[q], PS[q].t[:, 0:8 * Lc].rearrange("p (g k l) -> p g k l", g=2, k=4))
                        cd4 = V(CDEC, CDEC.t[:, q * 512:q * 512 + 8 * Lc].rearrange("p (g k l) -> p g k l", g=2, k=4))
                        cc4 = V(XN, XN.t[:, 8 + 2 * q:10 + 2 * q, sl].unsqueeze(2).to_broadcast([128, 2, 4, Lc]))
                        P.tt(cd4, cc4, pe4, ALU.mult)
                        P.copy(EAEND[:, q * 8:(q + 1) * 8],
                               V(PS[q], PS[q].t[:, 0:8 * Lc].rearrange("p (k l) -> p k l", k=8)[:, :, Lc - 1]))
                    for g in range(8):
                        P.mm(PS[4 + g // 4][0:Lc, (g % 4) * 128:(g % 4 + 1) * 128], XN[:, g, sl], IDB[:, :])
                    P.copy(BTK[0:Lc, 0:512], PS[4][0:Lc, :], eng="act")
                    P.copy(BTK[0:Lc, 512:1024], PS[5][0:Lc, :], eng="act")
                    for k in range(32):
                        kb = k // 2
                        o_ = PS[4 + kb // 8][(k % 2) * 64:(k % 2) * 64 + 64, (kb % 8) * Lc:(kb % 8 + 1) * Lc]
                        col = (k // 8) * 512 + (k % 8) * Lc
                        P.mm(o_, XDT[0:Lc, k * 64:(k + 1) * 64], STT[0:Lc, col:col + Lc], start=True, stop=False)
                        P.mm(o_, SB[:, k * 64:(k + 1) * 64], CDEC[:, col:col + Lc], start=False, stop=True)
                    for kb in range(16):
                        P.stt(XA[:, kb, sl], XA[:, kb, sl], SMALL[:, c0 + 2 + kb:c0 + 3 + kb],
                              PS[4 + kb // 8][:, (kb % 8) * Lc:(kb % 8 + 1) * Lc], ALU.mult, ALU.add)
                    for g in range(8):
                        P.mm(PS[g // 2][:, (g % 2) * 256:(g % 2 + 1) * 256], BTK[0:Lc, g * 128:(g + 1) * 128],
                             XW[0:Lc, g * 256:(g + 1) * 256])
                    s3 = V(ST32, ST32.t[:, 0:2048].rearrange("p (k d) -> p k d", k=32))
                    P.tt(s3, s3, V(EAEND, EAEND.t[:, :].unsqueeze(2).to_broadcast([128, 32, 64])), ALU.mult)
                    for q in range(4):
                        P.tt(S32[:, q * 512:(q + 1) * 512], S32[:, q * 512:(q + 1) * 512], PS[q][:, :], ALU.add)
                    P.copy(SB, S32, eng="act")
                zo = SEGOFF["z"]
                for kb in range(16):
                    st = stage()
                    P.dma(st[:, 0:n], preg(j, zo + kb * 128, zo + (kb + 1) * 128))
                    P.tt(XA[:, kb, 0:n], XA[:, kb, 0:n], st[:, 0:n], ALU.mult)
                for g in range(8):
                    rstd_from_blocks([XA[:, 2 * g, 0:n], XA[:, 2 * g + 1, 0:n]], n, 256, RSTD)
                    for kb in (2 * g, 2 * g + 1):
                        yb = YB[kb % 2]
                        P.stt(yb[:, 0:n], XA[:, kb, 0:n], chv(L, "ssd_norm_w", kb), RSTD[:, 0:n], ALU.mult, ALU.mult)
                        P.dma(P.region(("ybr", j), ybr[0, kb * 128:(kb + 1) * 128, t0:t0 + n]), yb[:, 0:n])
            P.fence([BIG] + subs)


        GSM = P.sbuf("gsm", [128, 272], F32)

        def gla(L):
            QD = subbuf("qd", BIG, 0, 4096, BF16)
            KD = subbuf("kd", BIG, 4096, 8192, BF16)
            QDEC = subbuf("qdec", BIG, 8192, 12288, BF16)
            KDEC = subbuf("kdec", BIG, 12288, 16384, BF16)
            VTK = subbuf("vtk", BIG, 16384, 18432, BF16)
            ATT = subbuf("att", BIG, 18432, 18688, BF16)
            KDT = subbuf("kdt", BIG, 18688, 19712, BF16)
            W2 = subbuf("w2", BIG, 19712, 21760, F32)
            subs = [QD, KD, QDEC, KDEC, VTK, ATT, KDT, W2]
            P.fence([BIG] + subs)
            GCS = Buf("gcs", XN.t[:, :, :].rearrange("p a b -> p (a b)").bitcast(F32).rearrange("p (a b) -> p a b", a=8))
            P.fence([XN, GCS])
            P.dma(W2[0:16, :], V(ext, wi["gla_w2"][L]))
            for d in range(8):
                P.ts(GSM[:, 256 + d:257 + d], chv(L, "gla_b", d), -1.0, ALU.mult)
            S3 = V(ST32, ST32.t[:, :].rearrange("p (d v) -> p d v", d=8))
            SB3 = V(STB, STB.t[:, :].rearrange("p (d v) -> p d v", d=8))
            P.memset(ST32[:, :], 0.0); P.memset(STB[:, :], 0.0)
            gsm4 = GSM.t[:, 0:256].rearrange("p (a d c) -> p a d c", a=4, d=8)
            qo, ko, vo, ro, go = (SEGOFF[x_] for x_ in ("q", "k", "v", "r", "glr"))

            def q4(ap3, n):
                return ap3[:, :, 0:n].rearrange("p d (c l) -> p d c l", l=min(64, n))
            for j in range(NT):
                t0, n = tiles[j]
                Lc = min(64, n); NC = n // Lc
                gl = CV[0]
                P.dma(gl[0:16, 0:n], preg(j, go, go + 16))
                for d in range(8):
                    ps = PS[d % 4]
                    P.mm(ps[:, 0:n], W2[0:16, d * 128:(d + 1) * 128], gl[0:16, 0:n])
                    st = stage()
                    P.act(st[:, 0:n], ps[:, 0:n], AF.Exp, bias=GSM[:, 256 + d:257 + d], scale=-1.0)
                    P.act(st[:, 0:n], st[:, 0:n], AF.Ln, bias=1.0)
                    P.ts(st[:, 0:n], st[:, 0:n], -1.0 / 16.0, ALU.mult)
                    P.scan(GCS[:, d, 0:n], MASK64[:, 0:n], st[:, 0:n], 0.0)
                g4 = q4(GCS.t, n)
                if n >= 64:
                    P.ts(V(GSM, gsm4[:, 0, :, 0:NC]), V(GCS, g4[:, :, :, 31]), -1.0, ALU.mult)
                    P.copy(V(GSM, gsm4[:, 1, :, 0:NC]), V(GCS, g4[:, :, :, 31]))
                else:
                    P.memset(V(GSM, gsm4[:, 0:2, :, 0:NC]), 0.0, eng="dve")
                P.copy(V(GSM, gsm4[:, 2, :, 0:NC]), V(GCS, g4[:, :, :, Lc - 1]))
                P.act(V(GSM, gsm4[:, 3, :, 0:NC]), V(GSM, gsm4[:, 2, :, 0:NC]), AF.Exp)
                for d in range(8):
                    qs = stage()
                    P.dma(qs[:, 0:n], preg(j, qo + d * 128, qo + (d + 1) * 128))
                    e1 = stage()
                    P.act(e1[:, 0:n], GCS[:, d, 0:n], AF.Exp)
                    P.tt(QDEC[:, d * 512:d * 512 + n], qs[:, 0:n], e1[:, 0:n], ALU.mult)
                    e2 = stage()
                    for c in range(NC):
                        sl = slice(c * Lc, (c + 1) * Lc)
                        P.act(e2[:, sl], GCS[:, d, sl], AF.Exp, bias=V(GSM, gsm4[:, 0, d, c:c + 1]))
                    P.tt(QD[:, d * 512:d * 512 + n], qs[:, 0:n], e2[:, 0:n], ALU.mult)
                    ks = stage()
                    P.dma(ks[:, 0:n], preg(j, ko + d * 128, ko + (d + 1) * 128))
                    e3 = stage()
                    for c in range(NC):
                        sl = slice(c * Lc, (c + 1) * Lc)
                        P.act(e3[:, sl], GCS[:, d, sl], AF.Exp, bias=V(GSM, gsm4[:, 1, d, c:c + 1]), scale=-1.0)
                    P.tt(KD[:, d * 512:d * 512 + n], ks[:, 0:n], e3[:, 0:n], ALU.mult)
                    e4 = stage()
                    for c in range(NC):
                        sl = slice(c * Lc, (c + 1) * Lc)
                        P.act(e4[:, sl], GCS[:, d, sl], AF.Exp, bias=V(GSM, gsm4[:, 2, d, c:c + 1]), scale=-1.0)
                    P.tt(KDEC[:, d * 512:d * 512 + n], ks[:, 0:n], e4[:, 0:n], ALU.mult)
                for c in range(NC):
                    sl = slice(c * Lc, (c + 1) * Lc)
                    vst3 = V(VST, VST.t[:, :].rearrange("p (kb l) -> p kb l", kb=16)[:, :, 0:Lc])
                    P.dma(vst3, P.region(("proj", j), projS["v"][:, t0 + c * Lc:t0 + (c + 1) * Lc].rearrange("(kb p) l -> p kb l", p=128)))
                    for kb in range(16):
                        P.transpose(PS[kb // 4][0:Lc, (kb % 4) * 128:(kb % 4 + 1) * 128], vst3[:, kb, :], IDF[:, :])
                    for q in range(4):
                        P.copy(VTK[0:Lc, q * 512:(q + 1) * 512], PS[q][0:Lc, :], eng="act" if q % 2 else "dve")
                    for h in range(4):
                        for d2 in range(2):
                            d = h * 2 + d2
                            P.mm(PS[4][0:Lc, h * Lc:(h + 1) * Lc], KD[:, d * 512 + c * Lc:d * 512 + (c + 1) * Lc],
                                 QD[:, d * 512 + c * Lc:d * 512 + (c + 1) * Lc], start=(d2 == 0), stop=(d2 == 1))
                    P.tt(V(ATT, ATT.t[0:Lc, 0:4 * Lc].rearrange("p (h l) -> p h l", h=4)),
                         V(PS[4], PS[4].t[0:Lc, 0:4 * Lc].rearrange("p (h l) -> p h l", h=4)),
                         V(CAUS, CAUS.t[0:Lc, 0:Lc].unsqueeze(1).to_broadcast([Lc, 4, Lc])), ALU.mult)
                    for d in range(8):
                        P.mm(PS[5 + d // 4][0:Lc, (d % 4) * 128:(d % 4 + 1) * 128], KDEC[:, d * 512 + c * Lc:d * 512 + (c + 1) * Lc], IDB[:, :])
                    P.copy(KDT[0:Lc, 0:512], PS[5][0:Lc, :], eng="act")
                    P.copy(KDT[0:Lc, 512:1024], PS[6][0:Lc, :])
                    for vb in range(16):
                        h = vb // 4
                        o_ = PS[vb // 8][:, (vb % 8) * Lc:(vb % 8 + 1) * Lc]
                        P.mm(o_, VTK[0:Lc, vb * 128:(vb + 1) * 128], ATT[0:Lc, h * Lc:(h + 1) * Lc], start=True, stop=False)
                        for d2 in range(2):
                            d = h * 2 + d2
                            P.mm(o_, SB3[:, d, (vb % 4) * 128:(vb % 4 + 1) * 128], QDEC[:, d * 512 + c * Lc:d * 512 + (c + 1) * Lc],
                                 start=False, stop=(d2 == 1))
                    for hf in range(2):
                        P.copy(XA[:, hf * 8:(hf + 1) * 8, sl], V(PS[hf], PS[hf].t[:, 0:8 * Lc].rearrange("p (a l) -> p a l", a=8)),
                               eng="act" if hf else "dve")
                    for d in range(8):
                        h = d // 2
                        ps = PS[2 + d % 2]
                        P.mm(ps[:, :], KDT[0:Lc, d * 128:(d + 1) * 128], VTK[0:Lc, h * 512:(h + 1) * 512])
                        P.stt(S3[:, d, :], S3[:, d, :], V(GSM, gsm4[:, 3, d, c:c + 1]), ps[:, :], ALU.mult, ALU.add)
                        P.copy(SB3[:, d, :], S3[:, d, :], eng="act")
                for h in range(4):
                    rstd_from_blocks([XA[:, 4 * h + i, 0:n] for i in range(4)], n, 512, RSTD)
                    for vb in range(4 * h, 4 * h + 4):
                        rs = stage()
                        P.dma(rs[:, 0:n], preg(j, ro + vb * 128, ro + (vb + 1) * 128))
                        tm = stage()
                        P.stt(tm[:, 0:n], XA[:, vb, 0:n], chv(L, "gla_norm_w", vb), RSTD[:, 0:n], ALU.mult, ALU.mult)
                        yb = YB[vb % 2]
                        P.tt(yb[:, 0:n], tm[:, 0:n], rs[:, 0:n], ALU.mult)
                        P.dma(P.region(("ybr", j), ybr[1, vb * 128:(vb + 1) * 128, t0:t0 + n]), yb[:, 0:n])
            P.fence([BIG] + subs)
            P.fence([XN, GCS])


        IOTA = P.sbuf("iota", [128, 512], F32)
        S5S = P.sbuf("s5s", [128, 16, 64], F32)
        TWO_PI = float(2 * np.pi); PI = float(np.pi)
        P.op("pool", lambda e: e.iota(IOTA.t[:, :], pattern=[[1, 512]], base=1, channel_multiplier=0,
                                      allow_small_or_imprecise_dtypes=True), reads=[], writes=[IOTA[:, :]])

        def wrap_pm_pi(r, tmp, lower=True):
            P.ts(tmp, r, PI, ALU.is_gt)
            P.stt(r, tmp, -TWO_PI, r, ALU.mult, ALU.add)
            if lower:
                P.ts(tmp, r, -PI, ALU.is_lt)
                P.stt(r, tmp, TWO_PI, r, ALU.mult, ALU.add)

        def sincos(sin_out, cos_out, ang, W):
            ti = stage(); tf = stage(); tm = stage()
            tiv = V(ti, ti.t[:, 0:W].bitcast(I32))
            P.ts(tiv, ang, 1.0 / TWO_PI, ALU.mult)
            P.copy(tf[:, 0:W], tiv)
            P.stt(tf[:, 0:W], tf[:, 0:W], -TWO_PI, ang, ALU.mult, ALU.add)
            wrap_pm_pi(tf[:, 0:W], tm[:, 0:W])
            P.act(sin_out, tf[:, 0:W], AF.Sin)
            P.ts(tf[:, 0:W], tf[:, 0:W], PI / 2, ALU.add)
            wrap_pm_pi(tf[:, 0:W], tm[:, 0:W], lower=False)
            P.act(cos_out, tf[:, 0:W], AF.Sin)
            return tf

        def s5(L):
            names = ["cos", "sin", "are", "aim", "vre", "vim", "wre", "wim", "xre", "xim", "magt"]
            sb = {}
            for i, nm in enumerate(names):
                sb[nm] = subbuf("s5_" + nm, BIG, i * 1024, (i + 1) * 1024, F32)
            XREb = subbuf("s5_xreb", BIG, 11264, 11776, BF16)
            XIMb = subbuf("s5_ximb", BIG, 11776, 12288, BF16)
            UB = subbuf("s5_ub", BIG, 12288, 12800, BF16)
            BTR = subbuf("s5_btr", BIG, 13312, 15360, BF16)
            BTI = subbuf("s5_bti", BIG, 15360, 17408, BF16)
            CTR = subbuf("s5_ctr", BIG, 17408, 19456, BF16)
            CTI = subbuf("s5_cti", BIG, 19456, 21504, BF16)
            subs = list(sb.values()) + [XREb, XIMb, UB, BTR, BTI, CTR, CTI]
            P.fence([BIG] + subs)
            st16 = ST32.t[:, :].bitcast(BF16)
            BT3R = Buf("s5_bt3r", st16[:, 0:2048]); BT3I = Buf("s5_bt3i", st16[:, 2048:4096])
            P.fence([ST32, BT3R, BT3I])
            LAMRE, LAMIM, DTV, MAG, THR, CORE, COIM, CARRE, CARIM, TA, TB, TC, TD, TE = (S5S[:, i, :] for i in range(14))
            for dst, nm in ((LAMRE, "s5_lam_re"), (LAMIM, "s5_lam_im")):
                src = wi[nm][L].rearrange("(q two) p -> q two p", two=2)
                P.dma(NAT[0:64, 0, 0:64], V(ext, src[:, 0, :]))
                P.dma(NAT[0:64, 0, 64:128], V(ext, src[:, 1, :]))
                P.transpose(PS[0][:, 0:64], NAT[0:64, 0, :], IDF[0:64, 0:64])
                P.copy(dst, PS[0][:, 0:64])
            P.dma(NAT[0:64, 1, 0:2], V(ext, wi["s5_log_dt"][L].rearrange("(q two) -> q two", two=2)))
            P.copy(V(NAT, NAT.t[0:64, 2, :].rearrange("p (two d) -> p two d", two=2)),
                   V(NAT, NAT.t[0:64, 1, 0:2].unsqueeze(2).to_broadcast([64, 2, 64])))
            P.transpose(PS[0][:, 0:64], NAT[0:64, 2, :], IDF[0:64, 0:64])
            P.act(DTV, PS[0][:, 0:64], AF.Exp)
            P.tt(TA, LAMRE, DTV, ALU.mult)
            P.act(MAG, TA, AF.Exp)
            P.tt(THR, LAMIM, DTV, ALU.mult)
            tf = sincos(TA, TB, THR, 64)
            ti = stage(); tm = stage()
            tiv = V(ti, ti.t[:, 0:64].bitcast(I32))
            P.ts(tiv, THR, 1.0 / TWO_PI, ALU.mult)
            P.copy(TC, tiv)
            P.stt(THR, TC, -TWO_PI, THR, ALU.mult, ALU.add)
            wrap_pm_pi(THR, tm[:, 0:64])
            P.tt(TC, MAG, TB, ALU.mult)
            P.tt(TD, MAG, TA, ALU.mult)
            P.ts(TC, TC, -1.0, ALU.add)
            P.tt(TA, LAMRE, LAMRE, ALU.mult)
            P.tt(TB, LAMIM, LAMIM, ALU.mult)
            P.tt(TA, TA, TB, ALU.add)
            P.recip(TA, TA)
            P.tt(CORE, TC, LAMRE, ALU.mult)
            P.tt(TB, TD, LAMIM, ALU.mult)
            P.tt(CORE, CORE, TB, ALU.add)
            P.tt(CORE, CORE, TA, ALU.mult)
            P.tt(COIM, TD, LAMRE, ALU.mult)
            P.tt(TB, TC, LAMIM, ALU.mult)
            P.tt(COIM, COIM, TB, ALU.subtract)
            P.tt(COIM, COIM, TA, ALU.mult)
            P.memset(CARRE, 0.0, eng="dve"); P.memset(CARIM, 0.0, eng="dve")
            xaf = XA.t[:, :, :].rearrange("p a b -> p (a b)")
            P.memset(XA[:, :, :], 0.0, eng="dve")
            for i, nm in enumerate(("s5_b_re", "s5_b_im")):
                bn = xaf[:, i * 2048:(i + 1) * 2048].rearrange("p (q c) -> p q c", c=32)
                src = wi[nm][L].rearrange("(q two) p j -> two p q j", two=2)
                P.dma(V(XA, bn[0:64, :, 0:16]), V(ext, src[0]))
                P.dma(V(XA, bn[64:128, :, 16:32]), V(ext, src[1]))
            for i, nm in enumerate(("s5_c_re", "s5_c_im")):
                cn = xaf[:, (2 + i) * 2048:(3 + i) * 2048].rearrange("p (blk c) -> p blk c", c=128)
                src = wi[nm][L].rearrange("(blk q4 two) i p -> q4 two i blk p", q4=4, two=2)
                for q4 in range(4):
                    for g2 in range(2):
                        r0 = q4 * 32 + g2 * 16
                        P.dma(V(XA, cn[r0:r0 + 16, :, g2 * 64:(g2 + 1) * 64]), V(ext, src[q4, g2]))
            for blk in range(16):
                for i, dstb in enumerate((BTR, BTI)):
                    bn = xaf[:, i * 2048 + blk * 128:i * 2048 + (blk + 1) * 128]
                    ps = PS[(2 * blk + i) % 8]
                    P.transpose(ps[:, 0:128], V(XA, bn), IDF[:, :])
                    P.copy(dstb[:, blk * 128:(blk + 1) * 128], ps[:, 0:128], eng="act" if i else "dve")
                    ps3 = PS[(2 * blk + i + 2) % 8]
                    P.transpose(ps3[0:32, 0:128], V(XA, bn[:, 96:128]), IDF[:, :])
                    P.copy((BT3R, BT3I)[i][0:32, blk * 128:(blk + 1) * 128], ps3[0:32, 0:128], eng="dve" if i else "act")
                for i, dstb in enumerate((CTR, CTI)):
                    cn = xaf[:, (2 + i) * 2048 + blk * 128:(2 + i) * 2048 + (blk + 1) * 128]
                    ps = PS[(2 * blk + i + 4) % 8]
                    P.transpose(ps[:, 0:128], V(XA, cn), IDF[:, :])
                    if i == 0:
                        P.copy(dstb[:, blk * 128:(blk + 1) * 128], ps[:, 0:128], eng="act")
                    else:
                        P.ts(dstb[:, blk * 128:(blk + 1) * 128], ps[:, 0:128], -1.0, ALU.mult)
            so = SEGOFF["s5u"]
            COS, SIN, ARE, AIM = sb["cos"], sb["sin"], sb["are"], sb["aim"]
            VRE, VIM, WRE, WIM, XRE, XIM, MAGT = sb["vre"], sb["vim"], sb["wre"], sb["wim"], sb["xre"], sb["xim"], sb["magt"]
            for q in range(64):
                blk, q4 = divmod(q, 4)
                rows = slice(q4 * 32, q4 * 32 + 32) if q4 < 3 else slice(0, 32)
                btr = BTR[rows, blk * 128:(blk + 1) * 128] if q4 < 3 else BT3R[0:32, blk * 128:(blk + 1) * 128]
                bti = BTI[rows, blk * 128:(blk + 1) * 128] if q4 < 3 else BT3I[0:32, blk * 128:(blk + 1) * 128]
                ang = stage()
                P.ts(ang[:, :], IOTA[:, :], THR[:, q:q + 1], ALU.mult)
                sincos(SIN[:, :], COS[:, :], ang[:, :], 512)
                P.ts(ARE[:, :], COS[:, :], CORE[:, q:q + 1], ALU.mult)
                P.stt(ARE[:, :], SIN[:, :], COIM[:, q:q + 1], ARE[:, :], ALU.mult, ALU.add)
                P.ts(AIM[:, :], COS[:, :], COIM[:, q:q + 1], ALU.mult)
                P.ts(ang[:, :], SIN[:, :], CORE[:, q:q + 1], ALU.mult)
                P.tt(AIM[:, :], AIM[:, :], ang[:, :], ALU.subtract)
                P.ts(MAGT[:, :], IOTA[:, :], 0.0, ALU.mult, MAG[:, q:q + 1], ALU.add)
                for j in range(NT):
                    t0, n = tiles[j]
                    us = stage()
                    P.dma(us[rows, 0:n], preg(j, so + blk * 128 + q4 * 32, so + blk * 128 + q4 * 32 + 32))
                    P.copy(UB[rows, 0:n], us[rows, 0:n], eng="act")
                    pr, pi_ = PS[(2 * q) % 4], PS[(2 * q + 1) % 4]
                    P.mm(pr[:, 0:n], btr, UB[rows, 0:n])
                    P.mm(pi_[:, 0:n], bti, UB[rows, 0:n])
                    ta = stage(); tb = stage()
                    P.tt(VRE[:, 0:n], pr[:, 0:n], ARE[:, 0:n], ALU.mult)
                    P.tt(ta[:, 0:n], pi_[:, 0:n], AIM[:, 0:n], ALU.mult)
                    P.tt(VRE[:, 0:n], VRE[:, 0:n], ta[:, 0:n], ALU.subtract)
                    P.tt(VIM[:, 0:n], pi_[:, 0:n], ARE[:, 0:n], ALU.mult)
                    P.tt(tb[:, 0:n], pr[:, 0:n], AIM[:, 0:n], ALU.mult)
                    P.tt(VIM[:, 0:n], VIM[:, 0:n], tb[:, 0:n], ALU.add)
                    P.scan(WRE[:, 0:n], MAGT[:, 0:n], VRE[:, 0:n], CARRE[:, q:q + 1])
                    P.scan(WIM[:, 0:n], MAGT[:, 0:n], VIM[:, 0:n], CARIM[:, q:q + 1])
                    tc = stage(); td = stage()
                    P.tt(XRE[:, 0:n], WRE[:, 0:n], COS[:, 0:n], ALU.mult, eng="pool")
                    P.tt(tc[:, 0:n], WIM[:, 0:n], SIN[:, 0:n], ALU.mult, eng="pool")
                    P.tt(XRE[:, 0:n], XRE[:, 0:n], tc[:, 0:n], ALU.subtract, eng="pool")
                    P.tt(XIM[:, 0:n], WRE[:, 0:n], SIN[:, 0:n], ALU.mult, eng="pool")
                    P.tt(td[:, 0:n], WIM[:, 0:n], COS[:, 0:n], ALU.mult, eng="pool")
                    P.tt(XIM[:, 0:n], XIM[:, 0:n], td[:, 0:n], ALU.add, eng="pool")
                    P.copy(CARRE[:, q:q + 1], XRE[:, n - 1:n], eng="pool")
                    P.copy(CARIM[:, q:q + 1], XIM[:, n - 1:n], eng="pool")
                    P.copy(XREb[:, 0:n], XRE[:, 0:n], eng="act")
                    P.copy(XIMb[:, 0:n], XIM[:, 0:n], eng="act")
                    py = PS[4 + q % 4]
                    P.mm(py[0:32, 0:n], CTR[:, blk * 128 + q4 * 32:blk * 128 + q4 * 32 + 32], XREb[:, 0:n], start=True, stop=False)
                    P.mm(py[0:32, 0:n], CTI[:, blk * 128 + q4 * 32:blk * 128 + q4 * 32 + 32], XIMb[:, 0:n], start=False, stop=True)
                    yst = stage()
                    P.copy(yst[0:32, 0:n], py[0:32, 0:n])
                    P.dma(P.region(("ys", j), ysT[blk * 128 + q4 * 32:blk * 128 + q4 * 32 + 32, t0:t0 + n]), yst[0:32, 0:n])
            P.fence([BIG] + subs)
            P.fence([ST32, BT3R, BT3I])
            for j in range(NT):
                t0, n = tiles[j]
                for kb in range(16):
                    ysb = stage(); ub = stage(); t2 = stage()
                    P.dma(ysb[:, 0:n], P.region(("ys", j), ysT[kb * 128:(kb + 1) * 128, t0:t0 + n]))
                    P.dma(ub[:, 0:n], preg(j, so + kb * 128, so + (kb + 1) * 128))
                    P.stt(ysb[:, 0:n], ub[:, 0:n], chv(L, "s5_d", kb), ysb[:, 0:n], ALU.mult, ALU.add)
                    P.act(t2[:, 0:n], ysb[:, 0:n], AF.Square)
                    P.ts(t2[:, 0:n], t2[:, 0:n], 0.044715, ALU.mult, 1.0, ALU.add)
                    P.tt(t2[:, 0:n], t2[:, 0:n], ysb[:, 0:n], ALU.mult)
                    P.act(t2[:, 0:n], t2[:, 0:n], AF.Sigmoid, scale=1.5957691216057308)
                    P.tt(XA[:, kb, 0:n], ysb[:, 0:n], t2[:, 0:n], ALU.mult)
                    P.copy(XN[:, kb, 0:n], XA[:, kb, 0:n], eng="act")

                def cons(cb, ps, cw):
                    sg = stage()
                    P.act(sg[:, 0:n], ps, AF.Sigmoid, bias=chv(L, "s5_glu_b", cb))
                    yb = YB[cb % 2]
                    P.tt(yb[:, 0:n], XA[:, cb, 0:n], sg[:, 0:n], ALU.mult)
                    P.dma(P.region(("ybr", j), ybr[2, cb * 128:(cb + 1) * 128, t0:t0 + n]), yb[:, 0:n])
                linear(WM[L, "glu"], lambda kb: XN[:, kb, 0:n], n, cons)


        def merge(L):
            go = SEGOFF["gate"]
            for j in range(NT):
                t0, n = tiles[j]
                for b in range(3):
                    P.dma(XN[:, :, 0:n], P.region(("ybr", j), ybr[b, :, t0:t0 + n].rearrange("(kb p) n -> p kb n", p=128)))

                    def cons(cb, ps, cw, b=b):
                        g = stage()
                        r0 = go + b * 2048 + cb * 128
                        P.dma(g[:, 0:n], preg(j, r0, r0 + 128))
                        if b == 0:
                            P.tt(XA[:, cb, 0:n], ps, g[:, 0:n], ALU.mult)
                        else:
                            P.tt(g[:, 0:n], ps, g[:, 0:n], ALU.mult)
                            P.tt(XA[:, cb, 0:n], XA[:, cb, 0:n], g[:, 0:n], ALU.add)
                    linear(WM[L, "br%d" % b], lambda kb: XN[:, kb, 0:n], n, cons)
                for kb in range(NKB):
                    P.copy(XN[:, kb, 0:n], XA[:, kb, 0:n], eng="act" if kb % 2 else "dve")

                def cons_o(cb, ps, cw):
                    P.copy(XA[:, cb, 0:n], ps, eng="act")
                linear(WM[L, "out"], lambda kb: XN[:, kb, 0:n], n, cons_o)
                rstd_from_blocks([XA[:, kb, 0:n] for kb in range(NKB)], n, D, RSTD)
                for kb in range(NKB):
                    hb = HB[kb % 2]
                    P.dma(hb[:, 0:n], hreg(j, kb))
                    t1 = T1[kb % 3]
                    P.stt(t1[:, 0:n], XA[:, kb, 0:n], chv(L, "ln_mix_post", kb), RSTD[:, 0:n], ALU.mult, ALU.mult)
                    P.tt(hb[:, 0:n], hb[:, 0:n], t1[:, 0:n], ALU.add)
                    P.dma(hreg(j, kb), hb[:, 0:n], eng="sp")

        class _sub:
            def __init__(s, wm, p0, p1):
                s.wm = wm; s.NP = p1 - p0; s.NC = wm.NC; s.nkb = wm.nkb; s.N = (p1 - p0) * 512; s.grp = wm.grp
                s.t = wm.t[p0:p1]

        for L in range(depth):
            if not skip_ffn:
                ffn(L, "f1", "ln_f1_pre", "ln_f1_post")
            if stop_after == "ffn1":
                break
            layer_smalls(L)
            inproj(L)
            if stop_after == "inproj":
                break
            ssd(L)
            if stop_after == "ssd":
                break
            if not skip_gla:
                gla(L)
            if stop_after == "gla":
                break
            s5(L)
            if stop_after == "s5":
                break
            merge(L)
            if stop_after == "mix":
                break
            if not skip_ffn:
                ffn(L, "f2", "ln_f2_pre", "ln_f2_post")

        finals = []
        for j in range(1, NT):
            t0, n = tiles[j]
            P.dma(XA[:, :, 0:n], hreg(j))
            for s0 in range(0, n, 128):
                tm = V(XN, XN.t[:, 0:8, :].bitcast(F32).rearrange("p a b -> p (a b)"))
                for kb in range(NKB):
                    ps = PS[kb % 8]
                    P.transpose(ps[:, 0:128], XA[:, kb, s0:s0 + 128], IDF[:, :])
                    P.copy(tm[:, kb * 128:(kb + 1) * 128], ps[:, 0:128], eng="act" if kb % 2 else "dve")
                r0 = t0 - NMETA + s0
                finals.append(P.dma(P.region(("out", j), out_d[r0:r0 + 128, :]), tm, eng="sp"))
        st = P.emit(finals)
        print("ops/waits per engine:", st, "sems:", P.nsem)
    return nc


N_CORES = 8


def kernel(**inputs):
    seq = inputs["x"].shape[1]
    depth = inputs["w_in"].shape[0]
    nc = build_program(seq, depth)
    in_maps = []
    for c in range(N_CORES):
        b = c % inputs["x"].shape[0]
        m = {k: np.ascontiguousarray(v) for k, v in inputs.items() if k != "x"}
        m["x"] = np.ascontiguousarray(inputs["x"][b])
        in_maps.append(m)
    res = run_bass_kernel_spmd(nc, in_maps, core_ids=list(range(N_CORES)))
    out = np.stack([res.results[b]["out"] for b in range(inputs["x"].shape[0])], axis=0)
    return out.astype(np.float32)
```

```python
import numpy as np
import concourse.bass as bass
import concourse.mybir as mybir
from concourse.bass_utils import run_bass_kernel_spmd
from contextlib import ExitStack

F32 = mybir.dt.float32; BF16 = mybir.dt.bfloat16; I32 = mybir.dt.int32
AF = mybir.ActivationFunctionType; ALU = mybir.AluOpType

D = 2048; NKB = 16; DFF = 5632; NFB = 44; NMETA = 16; EPS = 1e-6
SEGS = [("z", 2048), ("xbc", 4096), ("dt", 32), ("q", 1024), ("k", 1024), ("v", 2048), ("r", 2048),
        ("glr", 16), ("s5u", 2048), ("gate", 6144)]
DIN = sum(w for _, w in SEGS)
SEGOFF = {}
_o = 0
for _n, _w in SEGS:
    SEGOFF[_n] = _o; _o += _w


class Buf:
    def __init__(s, name, t):
        s.name = name; s.t = t
        s.last_w = None
        s.readers = {}
        s.sem = None; s.dma_cnt = 0

    def __getitem__(s, idx):
        return V(s, s.t[idx])


class V:
    def __init__(s, buf, ap):
        s.buf = buf; s.ap = ap

    def __getitem__(s, idx):
        return V(s.buf, s.ap[idx])

    def re(s, pat, **kw):
        return V(s.buf, s.ap.rearrange(pat, **kw))


class Op:
    __slots__ = ("eng", "fn", "deps", "dma_dst", "needed", "tok")

    def __init__(s, eng, fn):
        s.eng = eng; s.fn = fn; s.deps = []; s.dma_dst = None; s.needed = False; s.tok = None


class Prog:
    ENGS = ("pe", "dve", "act", "pool", "sp")

    def __init__(s, nc, es):
        s.nc = nc; s.es = es
        s.ops = {e: [] for e in s.ENGS}
        s.esem = {e: es.enter_context(nc.semaphore("s_" + e)) for e in ("pe", "dve", "act", "pool")}
        s.nsem = 4
        s.bufs = {}

    def sbuf(s, name, shape, dt):
        return Buf(name, s.es.enter_context(s.nc.sbuf_tensor(name, list(shape), dt)))

    def psum(s, name, shape, dt=F32):
        return Buf(name, s.es.enter_context(s.nc.psum_tensor(name, list(shape), dt)))

    def dram(s, name, shape, dt, kind="Internal"):
        return Buf(name, s.nc.dram_tensor(name, list(shape), dt, kind=kind).ap())

    def region(s, key, ap):
        if key not in s.bufs:
            s.bufs[key] = Buf(str(key), None)
        return V(s.bufs[key], ap)

    def op(s, eng, fn, reads=(), writes=(), dma=False):
        o = Op(eng, fn)
        deps = {}

        def add(d):
            if d is not None and d is not o:
                deps[id(d)] = d
        for v in reads:
            add(v.buf.last_w)
        for v in writes:
            add(v.buf.last_w)
            for r in v.buf.readers.values():
                add(r)
        for v in writes:
            v.buf.last_w = o; v.buf.readers = {}
        for v in reads:
            if v.buf.last_w is o:
                continue
            key = ("dma", len(v.buf.readers)) if dma else eng
            v.buf.readers[key] = o
        for d in deps.values():
            if d.eng == "pe" and eng == "pe" and d.dma_dst is None and not dma:
                continue
            d.needed = True
            o.deps.append(d)
        if dma:
            b = writes[0].buf
            if b.sem is None:
                b.sem = s.es.enter_context(s.nc.semaphore("d_%d" % s.nsem)); s.nsem += 1
            b.dma_cnt += 1
            o.dma_dst = b
            o.tok = (b.sem, 16 * b.dma_cnt)
        s.ops[eng].append(o)
        return o

    def dma(s, out, in_, eng="sp", extra_reads=()):
        return s.op(eng, lambda e: e.dma_start(out=out.ap, in_=in_.ap), reads=[in_] + list(extra_reads), writes=[out], dma=True)

    def fence(s, bufs):
        vs = [V(b, None) for b in bufs]
        return s.op("dve", lambda e: e.memset(s.fence_ap, 0.0), reads=vs, writes=vs + [V(s.fence_buf, None)])

    def mm(s, out, lhsT, rhs, start=True, stop=True):
        rw = [] if start else [out]
        return s.op("pe", lambda e: e.matmul(out.ap, lhsT.ap, rhs.ap, start=start, stop=stop),
                    reads=[lhsT, rhs] + rw, writes=[out])

    def transpose(s, out, in_, ident):
        return s.op("pe", lambda e: e.transpose(out.ap, in_.ap, ident.ap), reads=[in_, ident], writes=[out])

    def act(s, out, in_, func, bias=None, scale=1.0):
        kw = {}
        rd = [in_]
        if bias is not None:
            if isinstance(bias, V):
                kw["bias"] = bias.ap; rd.append(bias)
            else:
                kw["bias"] = bias
        if isinstance(scale, V):
            rd.append(scale); sc = scale.ap
        else:
            sc = scale
        return s.op("act", lambda e: e.activation(out=out.ap, in_=in_.ap, func=func, scale=sc, **kw),
                    reads=rd, writes=[out])

    def tt(s, out, a, b, op, eng="dve"):
        return s.op(eng, lambda e: e.tensor_tensor(out=out.ap, in0=a.ap, in1=b.ap, op=op), reads=[a, b], writes=[out])

    def ts(s, out, a, s1, op0, s2=None, op1=None, eng="dve"):
        rd = [a]
        s1a = s1.ap if isinstance(s1, V) else s1
        s2a = s2.ap if isinstance(s2, V) else s2
        if isinstance(s1, V): rd.append(s1)
        if isinstance(s2, V): rd.append(s2)
        if op1 is not None:
            return s.op(eng, lambda e: e.tensor_scalar(out=out.ap, in0=a.ap, scalar1=s1a, scalar2=s2a, op0=op0, op1=op1),
                        reads=rd, writes=[out])
        return s.op(eng, lambda e: e.tensor_scalar(out=out.ap, in0=a.ap, scalar1=s1a, scalar2=None, op0=op0),
                    reads=rd, writes=[out])

    def stt(s, out, a, sc, b, op0, op1):
        rd = [a, b]
        sca = sc.ap if isinstance(sc, V) else sc
        if isinstance(sc, V): rd.append(sc)
        return s.op("dve", lambda e: e.scalar_tensor_tensor(out=out.ap, in0=a.ap, scalar=sca, in1=b.ap, op0=op0, op1=op1),
                    reads=rd, writes=[out])

    def scan(s, out, d0, d1, init, op0=ALU.mult, op1=ALU.add):
        rd = [d0, d1]
        ia = init.ap if isinstance(init, V) else init
        if isinstance(init, V): rd.append(init)
        return s.op("dve", lambda e: e.tensor_tensor_scan(out=out.ap, data0=d0.ap, data1=d1.ap, initial=ia, op0=op0, op1=op1),
                    reads=rd, writes=[out])

    def recip(s, out, in_):
        return s.op("dve", lambda e: e.reciprocal(out=out.ap, in_=in_.ap), reads=[in_], writes=[out])

    def copy(s, out, in_, eng="dve"):
        if eng == "act":
            return s.op("act", lambda e: e.copy(out=out.ap, in_=in_.ap), reads=[in_], writes=[out])
        return s.op(eng, lambda e: e.tensor_copy(out=out.ap, in_=in_.ap), reads=[in_], writes=[out])

    def memset(s, out, val, eng="pool"):
        return s.op(eng, lambda e: e.memset(out.ap, val), reads=[], writes=[out])

    def emit(s, final_ops):
        nc = s.nc
        for e in ("pe", "dve", "act", "pool"):
            c = 0
            for o in s.ops[e]:
                if o.dma_dst is None and o.needed:
                    c += 1
                    o.tok = (s.esem[e], c)
        engmap = {"pe": "tensor", "dve": "vector", "act": "scalar", "pool": "gpsimd", "sp": "sync"}
        stats = {}
        with nc.Block() as block:
            for e in s.ENGS:
                def body(eng, ops=s.ops[e], e=e):
                    waited = {}
                    nw = 0
                    for o in ops:
                        need = {}
                        for d in o.deps:
                            sem, val = d.tok
                            k = id(sem)
                            if waited.get(k, 0) >= val:
                                continue
                            if k not in need or need[k][1] < val:
                                need[k] = (sem, val)
                        for k, (sem, val) in need.items():
                            eng.wait_ge(sem, val); waited[k] = val; nw += 1
                        ins = o.fn(eng)
                        if o.dma_dst is not None:
                            ins.then_inc(o.dma_dst.sem, 16)
                        elif o.needed:
                            ins.then_inc(s.esem[e], 1)
                    if e == "sp":
                        for o in final_ops:
                            sem, val = o.tok
                            eng.wait_ge(sem, val)
                    stats[e] = (len(ops), nw)
                getattr(block, engmap[e])(body)
        return stats


class WMat:
    def __init__(s, P, name, src_ap, K, N, grp):
        s.K = K; s.N = N
        s.nkb = K // 128
        s.NC = (s.nkb + 15) // 16
        s.NP = (N + 511) // 512
        s.grp = grp
        s.t = P.nc.dram_tensor(name, [s.NP, s.NC, 128, 16, 512], BF16, kind="Internal").ap()
        for pn in range(s.NP):
            w = min(512, N - pn * 512)
            for kc in range(s.NC):
                nk = min(16, s.nkb - kc * 16)
                src = src_ap[kc * 2048:kc * 2048 + nk * 128, pn * 512:pn * 512 + w].rearrange("(kb p) c -> p kb c", p=128)
                P.dma(V(grp, s.t[pn, kc, :, 0:nk, 0:w]), V(P.wsrc, src), eng="pool")


def build_program(seq, depth, dbg=None, stop_after=None, skip_ffn=False, skip_gla=False, skip_ssd=False):
    T = NMETA + seq
    tiles = [(0, NMETA)] + [(NMETA + 512 * i, 512) for i in range(seq // 512)]
    assert seq % 512 == 0
    NT = len(tiles)
    nc = bass.Bass("TRN2", target_bir_lowering=False)
    es = ExitStack()
    with es:
        P = Prog(nc, es)
        dbg = dbg or []

        def din(name, shape):
            return nc.dram_tensor(name, list(shape), F32, kind="ExternalInput").ap()
        x_in = din("x", [seq, D]); meta_in = din("meta", [NMETA, D])
        wi = {}
        for nm, shp in [("ln_f1_pre", [depth, D]), ("w_f1_gu", [depth, D, 2 * DFF]), ("w_f1_down", [depth, DFF, D]),
                        ("ln_f1_post", [depth, D]), ("ln_mix_pre", [depth, D]), ("w_in", [depth, D, DIN]),
                        ("conv_w", [depth, 4, 4096]), ("conv_b", [depth, 4096]), ("dt_bias", [depth, 32]),
                        ("a_log", [depth, 32]), ("ssd_d", [depth, 32]), ("ssd_norm_w", [depth, D]),
                        ("gla_w2", [depth, 16, 1024]), ("gla_b", [depth, 1024]), ("gla_norm_w", [depth, D]),
                        ("s5_lam_re", [depth, 128, 64]), ("s5_lam_im", [depth, 128, 64]), ("s5_log_dt", [depth, 128]),
                        ("s5_b_re", [depth, 128, 64, 16]), ("s5_b_im", [depth, 128, 64, 16]),
                        ("s5_c_re", [depth, 128, 16, 64]), ("s5_c_im", [depth, 128, 16, 64]),
                        ("s5_d", [depth, D]), ("s5_glu_w", [depth, D, D]), ("s5_glu_b", [depth, D]),
                        ("w_branch", [depth, 3, D, D]), ("w_out", [depth, D, D]), ("ln_mix_post", [depth, D]),
                        ("ln_f2_pre", [depth, D]), ("w_f2_gu", [depth, D, 2 * DFF]), ("w_f2_down", [depth, DFF, D]),
                        ("ln_f2_post", [depth, D])]:
            wi[nm] = din(nm, shp)
        out_d = nc.dram_tensor("out", [seq, D], F32, kind="ExternalOutput").ap()
        P.wsrc = Buf("wsrc", None)
        ext = P.wsrc

        def scratch(name, shape, dt):
            kind = "ExternalOutput" if name in dbg else "Internal"
            return nc.dram_tensor(name, list(shape), dt, kind=kind).ap()
        hT = scratch("hT", [D, T], F32)

        def hreg(j, kb=None):
            t0, n = tiles[j]
            if kb is None:
                return P.region(("h", j), hT[:, t0:t0 + n].rearrange("(kb p) n -> p kb n", p=128))
            return P.region(("h", j), hT[kb * 128:(kb + 1) * 128, t0:t0 + n])

        WT = [P.sbuf("wt%d" % i, [128, 16, 512], BF16) for i in range(2)]
        XA = P.sbuf("xa", [128, 16, 512], F32)
        XN = P.sbuf("xn", [128, 16, 512], BF16)
        BIG = P.sbuf("big", [128, 44, 512], BF16)
        SQ = [P.sbuf("sq%d" % i, [128, 512], BF16) for i in range(2)]
        T1 = [P.sbuf("t1_%d" % i, [128, 512], F32) for i in range(3)]
        RSTD = P.sbuf("rstd", [128, 512], F32)
        HB = [P.sbuf("hb%d" % i, [128, 512], F32) for i in range(2)]
        ONES = P.sbuf("ones", [128, 128], BF16)
        IDF = P.sbuf("idf", [128, 128], F32)
        NAT = P.sbuf("nat", [128, 3, 128], F32)
        CHV = P.sbuf("chv", [128, depth, 384], F32)
        PS = [P.psum("ps%d" % i, [128, 512]) for i in range(8)]
        IDB = P.sbuf("idb", [128, 128], BF16)
        CV = [P.sbuf("cv%d" % i, [128, 520], F32) for i in range(2)]
        YB = [P.sbuf("yb%d" % i, [128, 512], BF16) for i in range(2)]
        ST32 = P.sbuf("st32", [128, 4096], F32)
        STB = P.sbuf("stb", [128, 4096], BF16)
        MASK64 = P.sbuf("mask64", [128, 512], F32)
        MASKNEG = P.sbuf("maskneg", [64, 64], F32)
        CAUS = P.sbuf("caus", [64, 64], F32)
        SMALL = P.sbuf("small", [128, 256], F32)
        DTK = P.sbuf("dtk", [64, 64], F32)
        EAEND = P.sbuf("eaend", [128, 32], F32)
        VST = P.sbuf("vst", [128, 1024], F32)

        def subbuf(name, base, e0, e1, dt, shape=None):
            ap = base.t[:, :, :].rearrange("p a b -> p (a b)")[:, e0:e1]
            if dt == F32:
                ap = ap.bitcast(F32)
            return Buf(name, ap)
        wt_i = [0]

        P.memset(ONES[:, :], 1.0)
        P.memset(MASK64[:, :], 1.0)
        P.memset(V(MASK64, MASK64.t[:, :].rearrange("p (c l) -> p c l", l=64)[:, :, 0:1]), 0.0)
        P.memset(MASKNEG[:, :], 0.0)
        P.op("pool", lambda e: e.affine_select(out=MASKNEG.t[:, :], in_=MASKNEG.t[:, :], pattern=[[1, 64]],
                                               compare_op=ALU.is_ge, fill=-30000.0, base=0, channel_multiplier=-1),
             reads=[MASKNEG[:, :]], writes=[MASKNEG[:, :]])
        P.memset(CAUS[:, :], 1.0)
        P.op("pool", lambda e: e.affine_select(out=CAUS.t[:, :], in_=CAUS.t[:, :], pattern=[[1, 64]],
                                               compare_op=ALU.is_ge, fill=0.0, base=0, channel_multiplier=-1),
             reads=[CAUS[:, :]], writes=[CAUS[:, :]])
        P.memset(IDF[:, :], 1.0)
        P.op("pool", lambda e: e.affine_select(out=IDF.t[:, :], in_=IDF.t[:, :], pattern=[[-1, 128]],
                                               compare_op=ALU.is_equal, fill=0.0, base=0, channel_multiplier=1),
             reads=[IDF[:, :]], writes=[IDF[:, :]])
        CH_LAYOUT = [("ln_f1_pre", 16), ("ln_f1_post", 16), ("ln_mix_pre", 16), ("ln_mix_post", 16), ("ln_f2_pre", 16),
                     ("ln_f2_post", 16), ("ssd_norm_w", 16), ("gla_norm_w", 16), ("s5_d", 16), ("s5_glu_b", 16),
                     ("conv_b", 32), ("conv_w0", 32), ("conv_w1", 32), ("conv_w2", 32), ("conv_w3", 32), ("gla_b", 8)]
        CHO = {}
        _c = 0
        for nm, m_ in CH_LAYOUT:
            CHO[nm] = _c; _c += m_
        NCH = 384
        for L in range(depth):
            nat = NAT
            P.memset(nat[:, :, :], 0.0)
            for nm, m_ in CH_LAYOUT:
                if nm.startswith("conv_w"):
                    src = wi["conv_w"][L, int(nm[-1])]
                else:
                    src = wi[nm][L]
                c0 = CHO[nm]
                r = 0
                while r < m_:
                    g, p0 = divmod(c0 + r, 128)
                    cnt = min(m_ - r, 128 - p0)
                    P.dma(nat[p0:p0 + cnt, g, :], V(ext, src[r * 128:(r + cnt) * 128].rearrange("(kb p) -> kb p", p=128)))
                    r += cnt
            for g in range(3):
                P.transpose(PS[g][:, 0:128], nat[:, g, :], IDF[:, :])
                P.copy(CHV[:, L, g * 128:(g + 1) * 128], PS[g][:, 0:128])

        P.copy(IDB[:, :], IDF[:, :])

        def chv(L, nm, kb):
            c = CHO[nm] + kb
            return CHV[:, L, c:c + 1]
        WM = {}
        for L in range(depth):
            g1 = Buf("wg_f1_%d" % L, None); gm = Buf("wg_mx_%d" % L, None); g2 = Buf("wg_f2_%d" % L, None)
            WM[L, "f1_gu"] = WMat(P, "wb_f1gu%d" % L, wi["w_f1_gu"][L], D, 2 * DFF, g1)
            WM[L, "f1_dn"] = WMat(P, "wb_f1dn%d" % L, wi["w_f1_down"][L], DFF, D, g1)
            if stop_after != "ffn1":
                for nm, w in SEGS:
                    o = SEGOFF[nm]
                    WM[L, "in_" + nm] = WMat(P, "wb_in_%s%d" % (nm, L), wi["w_in"][L][:, o:o + w], D, w, gm)
                for b in range(3):
                    WM[L, "br%d" % b] = WMat(P, "wb_br%d_%d" % (b, L), wi["w_branch"][L, b], D, D, gm)
                WM[L, "out"] = WMat(P, "wb_out%d" % L, wi["w_out"][L], D, D, gm)
                WM[L, "glu"] = WMat(P, "wb_glu%d" % L, wi["s5_glu_w"][L], D, D, gm)
                WM[L, "f2_gu"] = WMat(P, "wb_f2gu%d" % L, wi["w_f2_gu"][L], D, 2 * DFF, g2)
                WM[L, "f2_dn"] = WMat(P, "wb_f2dn%d" % L, wi["w_f2_down"][L], DFF, D, g2)

        def load_transposed(j):
            t0, n = tiles[j]
            for s0 in range(0, n, 128):
                m = min(128, n - s0)
                if j == 0:
                    src = meta_in[s0:s0 + m, :]
                else:
                    src = x_in[t0 - NMETA + s0:t0 - NMETA + s0 + m, :]
                tm = V(XA, XA.t[0:m, 0:4, :].rearrange("p a b -> p (a b)"))
                P.dma(tm, V(ext, src))
                for kb in range(NKB):
                    ps = PS[kb % 8]
                    P.transpose(ps[:, 0:m], tm[:, kb * 128:(kb + 1) * 128], IDF[0:m, 0:m])
                    hb = HB[kb % 2]
                    P.copy(hb[:, 0:m], ps[:, 0:m], eng="act" if kb % 2 else "dve")
                    P.dma(P.region(("h", j), hT[kb * 128:(kb + 1) * 128, t0 + s0:t0 + s0 + m]), hb[:, 0:m], eng="sp")
        for j in range(NT):
            load_transposed(j)

        def rstd_from_blocks(blocks, n, nelem, out_rstd):
            ps = PS[7]
            nb = len(blocks)
            for i, b in enumerate(blocks):
                sq = SQ[i % 2]
                P.act(sq[:, 0:n], b, AF.Square)
                P.mm(ps[:, 0:n], ONES[:, :], sq[:, 0:n], start=(i == 0), stop=(i == nb - 1))
            P.act(out_rstd[:, 0:n], ps[:, 0:n], AF.Sqrt, bias=EPS, scale=1.0 / nelem)
            P.recip(out_rstd[:, 0:n], out_rstd[:, 0:n])

        def linear(wm, rhs_of_kb, n, consume, K_nkb=None):
            bank = 0
            for pn in range(wm.NP):
                w = min(512, wm.N - pn * 512)
                ncb = (w + 127) // 128
                banks = [PS[(pn % 2) * 4 + c] for c in range(ncb)] if wm.NC > 1 else None
                for kc in range(wm.NC):
                    nk = min(16, wm.nkb - kc * 16)
                    wt = WT[wt_i[0] % 2]; wt_i[0] += 1
                    P.dma(wt[:, 0:nk, 0:w], V(wm.grp, wm.t[pn, kc, :, 0:nk, 0:w]))
                    for c in range(ncb):
                        cw = min(128, w - c * 128)
                        if wm.NC > 1:
                            ps = banks[c]
                        else:
                            ps = PS[bank % 7]
                        for kb in range(nk):
                            P.mm(ps[0:cw, 0:n], wt[:, kb, c * 128:c * 128 + cw], rhs_of_kb(kc * 16 + kb),
                                 start=(kc == 0 and kb == 0), stop=(kc == wm.NC - 1 and kb == nk - 1))
                        if wm.NC == 1:
                            consume(pn * 4 + c, ps[0:cw, 0:n], cw)
                            bank += 1
                if wm.NC > 1:
                    for c in range(ncb):
                        cw = min(128, w - c * 128)
                        consume(pn * 4 + c, banks[c][0:cw, 0:n], cw)

        def ffn(L, which, ln_pre, ln_post):
            wgu = WM[L, which + "_gu"]; wdn = WM[L, which + "_dn"]
            for j in range(NT):
                t0, n = tiles[j]
                P.dma(XA[:, :, 0:n], hreg(j))
                rstd_from_blocks([XA[:, kb, 0:n] for kb in range(NKB)], n, D, RSTD)
                for kb in range(NKB):
                    P.stt(XN[:, kb, 0:n], XA[:, kb, 0:n], chv(L, ln_pre, kb), RSTD[:, 0:n], ALU.mult, ALU.mult)
                def cons_gate(cb, ps, cw):
                    P.act(BIG[:, cb, 0:n], ps, AF.Silu)

                def cons_up(cb, ps, cw):
                    P.tt(BIG[:, cb, 0:n], BIG[:, cb, 0:n], ps, ALU.mult)
                linear(_sub(wgu, 0, NFB // 4), lambda kb: XN[:, kb, 0:n], n, cons_gate)
                linear(_sub(wgu, NFB // 4, NFB // 2), lambda kb: XN[:, kb, 0:n], n, cons_up)
                def cons_dn(cb, ps, cw):
                    P.copy(XA[:, cb, 0:n], ps, eng="act")
                linear(wdn, lambda kb: BIG[:, kb, 0:n], n, cons_dn)
                rstd_from_blocks([XA[:, kb, 0:n] for kb in range(NKB)], n, D, RSTD)
                for kb in range(NKB):
                    hb = HB[kb % 2]
                    P.dma(hb[:, 0:n], hreg(j, kb))
                    t1 = T1[kb % 3]
                    P.stt(t1[:, 0:n], XA[:, kb, 0:n], chv(L, ln_post, kb), RSTD[:, 0:n], ALU.mult, ALU.mult)
                    P.stt(hb[:, 0:n], t1[:, 0:n], 0.5, hb[:, 0:n], ALU.mult, ALU.add)
                    P.dma(hreg(j, kb), hb[:, 0:n], eng="sp")


        projS = {nm: scratch("proj_" + nm, [w, T], F32) for nm, w in SEGS}

        def proj_ap(r0, r1, a, b):
            for nm, w in SEGS:
                o = SEGOFF[nm]
                if o <= r0 < o + w:
                    assert r1 <= o + w
                    return projS[nm][r0 - o:r1 - o, a:b]
            raise AssertionError
        ybr = scratch("ybr", [3, D, T], BF16)
        ysT = scratch("ysT", [D, T], F32)
        FEN = P.sbuf("fence", [128, 8], F32)
        P.fence_ap = FEN.t[:, 0:1]; P.fence_buf = FEN

        def preg(j, r0, r1, c0=None, c1=None):
            t0, n = tiles[j]
            a = t0 if c0 is None else c0
            b = t0 + n if c1 is None else c1
            return P.region(("proj", j), proj_ap(r0, r1, a, b))

        stage_i = [0]

        def stage():
            bufs = T1 + HB
            b = bufs[stage_i[0] % 5]; stage_i[0] += 1
            return b

        def layer_smalls(L):
            c0 = L * 64
            P.dma(SMALL[0:32, c0:c0 + 1], V(ext, wi["dt_bias"][L].rearrange("(k o) -> k o", o=1)))
            P.dma(SMALL[0:32, c0 + 1:c0 + 2], V(ext, wi["a_log"][L].rearrange("(k o) -> k o", o=1)))
            P.act(SMALL[0:32, c0 + 1:c0 + 2], SMALL[0:32, c0 + 1:c0 + 2], AF.Exp)
            P.ts(SMALL[0:32, c0 + 1:c0 + 2], SMALL[0:32, c0 + 1:c0 + 2], -1.0, ALU.mult)
            P.dma(SMALL[:, c0 + 18:c0 + 50], V(ext, wi["ssd_d"][L].partition_broadcast(128)))
            raw = SMALL.t[:, c0 + 18:c0 + 50].rearrange("p (kb two) -> p kb two", two=2)
            P.copy(SMALL[0:64, c0 + 2:c0 + 18], V(SMALL, raw[0:64, :, 0]))
            P.copy(SMALL[64:128, c0 + 2:c0 + 18], V(SMALL, raw[64:128, :, 1]))

        def inproj(L):
            c0 = L * 64
            for j in range(NT):
                t0, n = tiles[j]
                P.dma(XA[:, :, 0:n], hreg(j))
                rstd_from_blocks([XA[:, kb, 0:n] for kb in range(NKB)], n, D, RSTD)
                for kb in range(NKB):
                    P.stt(XN[:, kb, 0:n], XA[:, kb, 0:n], chv(L, "ln_mix_pre", kb), RSTD[:, 0:n], ALU.mult, ALU.mult)
                for nm, w in SEGS:
                    off = SEGOFF[nm]
                    cnt = [0]

                    def cons(cb, ps, cw, nm=nm, off=off, cnt=cnt):
                        st = stage()
                        sv = st[0:cw, 0:n]
                        if nm in ("z", "r"):
                            P.act(sv, ps, AF.Silu)
                        elif nm == "gate":
                            P.act(sv, ps, AF.Sigmoid)
                        elif nm == "dt":
                            P.act(sv, ps, AF.Exp, bias=SMALL[0:32, c0:c0 + 1])
                            P.act(sv, sv, AF.Ln, bias=1.0)
                        elif nm == "q":
                            P.ts(sv, ps, 1.0 / 16.0, ALU.mult)
                        else:
                            cnt[0] += 1
                            P.copy(sv, ps, eng="act" if cnt[0] % 2 else "dve")
                        P.dma(preg(j, off + cb * 128, off + cb * 128 + cw), sv)
                    linear(WM[L, "in_" + nm], lambda kb: XN[:, kb, 0:n], n, cons)

        def ssd(L):
            c0 = L * 64
            AHEAD = SMALL[0:32, c0 + 1:c0 + 2]
            XDT = subbuf("xdt", BIG, 0, 4096, BF16)
            XW = subbuf("xw", BIG, 4096, 8192, BF16)
            BTK = subbuf("btk", BIG, 8192, 10240, BF16)
            ES = subbuf("es", BIG, 10240, 12288, BF16)
            STT = subbuf("stt", BIG, 12288, 14336, BF16)
            CDEC = subbuf("cdec", BIG, 14336, 16384, BF16)
            ACS = subbuf("acs", BIG, 16384, 17408, F32)
            NACS = subbuf("nacs", BIG, 17408, 18432, F32)
            EA = subbuf("ea", BIG, 18432, 19456, F32)
            DTE = subbuf("dte", BIG, 19456, 20480, F32)
            DTs = subbuf("dts", BIG, 20480, 21504, F32)
            subs = [XDT, XW, BTK, ES, STT, CDEC, ACS, NACS, EA, DTE, DTs]
            P.fence([BIG] + subs)
            S32 = V(ST32, ST32.t[:, 0:2048]); SB = V(STB, STB.t[:, 0:2048])
            P.memset(S32, 0.0); P.memset(SB, 0.0)
            xo = SEGOFF["xbc"]
            for j in range(NT):
                t0, n = tiles[j]
                Lc = min(64, n); NC = n // Lc
                for cb in range(32):
                    cv = CV[cb % 2]
                    r0 = xo + cb * 128
                    if j == 0:
                        P.memset(cv[:, 0:3], 0.0)
                        P.dma(cv[:, 3:3 + n], preg(j, r0, r0 + 128))
                    else:
                        P.dma(cv[:, 0:n + 3], preg(j, r0, r0 + 128, t0 - 3, t0 + n), extra_reads=[preg(j - 1, r0, r0 + 128)])
                    acc = stage()
                    P.ts(acc[:, 0:n], cv[:, 3:3 + n], chv(L, "conv_w3", cb), ALU.mult, chv(L, "conv_b", cb), ALU.add)
                    for k in (2, 1, 0):
                        P.stt(acc[:, 0:n], cv[:, k:k + n], chv(L, "conv_w%d" % k, cb), acc[:, 0:n], ALU.mult, ALU.add)
                    if cb < 16:
                        P.act(XA[:, cb, 0:n], acc[:, 0:n], AF.Silu)
                    else:
                        P.act(XN[:, cb - 16, 0:n], acc[:, 0:n], AF.Silu)
                do = SEGOFF["dt"]
                P.dma(DTs[0:32, 0:n], preg(j, do, do + 32))
                P.ts(NACS[0:32, 0:n], DTs[0:32, 0:n], AHEAD, ALU.mult)
                P.scan(ACS[0:32, 0:n], MASK64[0:32, 0:n], NACS[0:32, 0:n], 0.0)
                P.ts(NACS[0:32, 0:n], ACS[0:32, 0:n], -1.0, ALU.mult)
                P.act(EA[0:32, 0:n], ACS[0:32, 0:n], AF.Exp)
                for c in range(NC):
                    sl = slice(c * Lc, (c + 1) * Lc)
                    P.act(DTE[0:32, sl], ACS[0:32, sl], AF.Exp, bias=ACS[0:32, (c + 1) * Lc - 1:(c + 1) * Lc], scale=-1.0)
                for c in range(NC):
                    sl = slice(c * Lc, (c + 1) * Lc)
                    P.transpose(PS[7][0:Lc, 0:32], DTs[0:32, sl], IDF[0:32, 0:32])
                    P.transpose(PS[7][0:Lc, 32:64], DTE[0:32, sl], IDF[0:32, 0:32])
                    P.copy(DTK[0:Lc, :], PS[7][0:Lc, 0:64])
                    for k in range(32):
                        o_ = PS[k // 8][0:Lc, (k % 8) * Lc:(k % 8 + 1) * Lc]
                        sel_s = V(IDF, IDF.t[0:32, k:k + 1].to_broadcast([32, Lc]))
                        P.mm(o_, sel_s, ACS[0:32, sl], start=True, stop=False)
                        P.mm(o_, NACS[0:32, sl], sel_s, start=False, stop=False)
                        P.mm(o_, IDF[0:Lc, 0:Lc], MASKNEG[0:Lc, 0:Lc], start=False, stop=True)
                    for q in range(4):
                        P.act(ES[0:Lc, q * 512:q * 512 + 8 * Lc], PS[q][0:Lc, 0:8 * Lc], AF.Exp)
                    for g in range(8):
                        P.mm(PS[6][0:Lc, g * Lc:(g + 1) * Lc], XN[:, g, sl], XN[:, 8 + g, sl])
                    for q in range(4):
                        es4 = V(ES, ES.t[0:Lc, q * 512:q * 512 + 8 * Lc].rearrange("p (g k l) -> p g k l", g=2, k=4))
                        st4 = V(STT, STT.t[0:Lc, q * 512:q * 512 + 8 * Lc].rearrange("p (g k l) -> p g k l", g=2, k=4))
                        cb4 = V(PS[6], PS[6].t[0:Lc, q * 2 * Lc:(q * 2 + 2) * Lc].rearrange("p (g l) -> p g l", g=2).unsqueeze(2).to_broadcast([Lc, 2, 4, Lc]))
                        P.tt(st4, es4, cb4, ALU.mult)
                    for kb in range(16):
                        P.transpose(PS[kb // 4][0:Lc, (kb % 4) * 128:(kb % 4 + 1) * 128], XA[:, kb, sl], IDF[:, :])
                    for q in range(4):
                        px = V(PS[q], PS[q].t[0:Lc, :].rearrange("p (k d) -> p k d", k=8))
                        xd = V(XDT, XDT.t[0:Lc, q * 512:(q + 1) * 512].rearrange("p (k d) -> p k d", k=8))
                        xw = V(XW, XW.t[0:Lc, q * 512:(q + 1) * 512].rearrange("p (k d) -> p k d", k=8))
                        dtb = V(DTK, DTK.t[0:Lc, q * 8:(q + 1) * 8].unsqueeze(2).to_broadcast([Lc, 8, 64]))
                        deb = V(DTK, DTK.t[0:Lc, 32 + q * 8:32 + (q + 1) * 8].unsqueeze(2).to_broadcast([Lc, 8, 64]))
                        P.tt(xd, px, dtb, ALU.mult)
                        P.tt(xw, xd, deb, ALU.mult)
                    for k in range(32):
                        sel_p = V(IDF, IDF.t[0:32, k:k + 1].to_broadcast([32, 128]))
                        P.mm(PS[k // 8][:, (k % 8) * Lc:(k % 8 + 1) * Lc], sel_p, EA[0:32, sl])
                    for q in range(4):
                        pe4 = V(PS[q], PS[q].t[:, 0:8 * Lc].rearrange("p (g k l) -> p g k l", g=2, k=4))
                        cd4 = V(CDEC, CDEC.t[:, q * 512:q * 512 + 8 * Lc].rearrange("p (g k l) -> p g k l", g=2, k=4))
                        cc4 = V(XN, XN.t[:, 8 + 2 * q:10 + 2 * q, sl].unsqueeze(2).to_broadcast([128, 2, 4, Lc]))
                        P.tt(cd4, cc4, pe4, ALU.mult)
                        P.copy(EAEND[:, q * 8:(q + 1) * 8],
                               V(PS[q], PS[q].t[:, 0:8 * Lc].rearrange("p (k l) -> p k l", k=8)[:, :, Lc - 1]))
                    for g in range(8):
                        P.mm(PS[4 + g // 4][0:Lc, (g % 4) * 128:(g % 4 + 1) * 128], XN[:, g, sl], IDB[:, :])
                    P.copy(BTK[0:Lc, 0:512], PS[4][0:Lc, :], eng="act")
                    P.copy(BTK[0:Lc, 512:1024], PS[5][0:Lc, :], eng="act")
                    for k in range(32):
                        kb = k // 2
                        o_ = PS[4 + kb // 8][(k % 2) * 64:(k % 2) * 64 + 64, (kb % 8) * Lc:(kb % 8 + 1) * Lc]
                        col = (k // 8) * 512 + (k % 8) * Lc
                        P.mm(o_, XDT[0:Lc, k * 64:(k + 1) * 64], STT[0:Lc, col:col + Lc], start=True, stop=False)
                        P.mm(o_, SB[:, k * 64:(k + 1) * 64], CDEC[:, col:col + Lc], start=False, stop=True)
                    for kb in range(16):
                        P.stt(XA[:, kb, sl], XA[:, kb, sl], SMALL[:, c0 + 2 + kb:c0 + 3 + kb],
                              PS[4 + kb // 8][:, (kb % 8) * Lc:(kb % 8 + 1) * Lc], ALU.mult, ALU.add)
                    for g in range(8):
                        P.mm(PS[g // 2][:, (g % 2) * 256:(g % 2 + 1) * 256], BTK[0:Lc, g * 128:(g + 1) * 128],
                             XW[0:Lc, g * 256:(g + 1) * 256])
                    s3 = V(ST32, ST32.t[:, 0:2048].rearrange("p (k d) -> p k d", k=32))
                    P.tt(s3, s3, V(EAEND, EAEND.t[:, :].unsqueeze(2).to_broadcast([128, 32, 64])), ALU.mult)
                    for q in range(4):
                        P.tt(S32[:, q * 512:(q + 1) * 512], S32[:, q * 512:(q + 1) * 512], PS[q][:, :], ALU.add)
                    P.copy(SB, S32, eng="act")
                zo = SEGOFF["z"]
                for kb in range(16):
                    st = stage()
                    P.dma(st[:, 0:n], preg(j, zo + kb * 128, zo + (kb + 1) * 128))
                    P.tt(XA[:, kb, 0:n], XA[:, kb, 0:n], st[:, 0:n], ALU.mult)
                for g in range(8):
                    rstd_from_blocks([XA[:, 2 * g, 0:n], XA[:, 2 * g + 1, 0:n]], n, 256, RSTD)
                    for kb in (2 * g, 2 * g + 1):
                        yb = YB[kb % 2]
                        P.stt(yb[:, 0:n], XA[:, kb, 0:n], chv(L, "ssd_norm_w", kb), RSTD[:, 0:n], ALU.mult, ALU.mult)
                        P.dma(P.region(("ybr", j), ybr[0, kb * 128:(kb + 1) * 128, t0:t0 + n]), yb[:, 0:n])
            P.fence([BIG] + subs)


        GSM = P.sbuf("gsm", [128, 272], F32)

        def gla(L):
            QD = subbuf("qd", BIG, 0, 4096, BF16)
            KD = subbuf("kd", BIG, 4096, 8192, BF16)
            QDEC = subbuf("qdec", BIG, 8192, 12288, BF16)
            KDEC = subbuf("kdec", BIG, 12288, 16384, BF16)
            VTK = subbuf("vtk", BIG, 16384, 18432, BF16)
            ATT = subbuf("att", BIG, 18432, 18688, BF16)
            KDT = subbuf("kdt", BIG, 18688, 19712, BF16)
            W2 = subbuf("w2", BIG, 19712, 21760, F32)
            subs = [QD, KD, QDEC, KDEC, VTK, ATT, KDT, W2]
            P.fence([BIG] + subs)
            GCS = Buf("gcs", XN.t[:, :, :].rearrange("p a b -> p (a b)").bitcast(F32).rearrange("p (a b) -> p a b", a=8))
            P.fence([XN, GCS])
            P.dma(W2[0:16, :], V(ext, wi["gla_w2"][L]))
            for d in range(8):
                P.ts(GSM[:, 256 + d:257 + d], chv(L, "gla_b", d), -1.0, ALU.mult)
            S3 = V(ST32, ST32.t[:, :].rearrange("p (d v) -> p d v", d=8))
            SB3 = V(STB, STB.t[:, :].rearrange("p (d v) -> p d v", d=8))
            P.memset(ST32[:, :], 0.0); P.memset(STB[:, :], 0.0)
            gsm4 = GSM.t[:, 0:256].rearrange("p (a d c) -> p a d c", a=4, d=8)
            qo, ko, vo, ro, go = (SEGOFF[x_] for x_ in ("q", "k", "v", "r", "glr"))

            def q4(ap3, n):
                return ap3[:, :, 0:n].rearrange("p d (c l) -> p d c l", l=min(64, n))
            for j in range(NT):
                t0, n = tiles[j]
                Lc = min(64, n); NC = n // Lc
                gl = CV[0]
                P.dma(gl[0:16, 0:n], preg(j, go, go + 16))
                for d in range(8):
                    ps = PS[d % 4]
                    P.mm(ps[:, 0:n], W2[0:16, d * 128:(d + 1) * 128], gl[0:16, 0:n])
                    st = stage()
                    P.act(st[:, 0:n], ps[:, 0:n], AF.Exp, bias=GSM[:, 256 + d:257 + d], scale=-1.0)
                    P.act(st[:, 0:n], st[:, 0:n], AF.Ln, bias=1.0)
                    P.ts(st[:, 0:n], st[:, 0:n], -1.0 / 16.0, ALU.mult)
                    P.scan(GCS[:, d, 0:n], MASK64[:, 0:n], st[:, 0:n], 0.0)
                g4 = q4(GCS.t, n)
                if n >= 64:
                    P.ts(V(GSM, gsm4[:, 0, :, 0:NC]), V(GCS, g4[:, :, :, 31]), -1.0, ALU.mult)
                    P.copy(V(GSM, gsm4[:, 1, :, 0:NC]), V(GCS, g4[:, :, :, 31]))
                else:
                    P.memset(V(GSM, gsm4[:, 0:2, :, 0:NC]), 0.0, eng="dve")
                P.copy(V(GSM, gsm4[:, 2, :, 0:NC]), V(GCS, g4[:, :, :, Lc - 1]))
                P.act(V(GSM, gsm4[:, 3, :, 0:NC]), V(GSM, gsm4[:, 2, :, 0:NC]), AF.Exp)
                for d in range(8):
                    qs = stage()
                    P.dma(qs[:, 0:n], preg(j, qo + d * 128, qo + (d + 1) * 128))
                    e1 = stage()
                    P.act(e1[:, 0:n], GCS[:, d, 0:n], AF.Exp)
                    P.tt(QDEC[:, d * 512:d * 512 + n], qs[:, 0:n], e1[:, 0:n], ALU.mult)
                    e2 = stage()
                    for c in range(NC):
                        sl = slice(c * Lc, (c + 1) * Lc)
                        P.act(e2[:, sl], GCS[:, d, sl], AF.Exp, bias=V(GSM, gsm4[:, 0, d, c:c + 1]))
                    P.tt(QD[:, d * 512:d * 512 + n], qs[:, 0:n], e2[:, 0:n], ALU.mult)
                    ks = stage()
                    P.dma(ks[:, 0:n], preg(j, ko + d * 128, ko + (d + 1) * 128))
                    e3 = stage()
                    for c in range(NC):
                        sl = slice(c * Lc, (c + 1) * Lc)
                        P.act(e3[:, sl], GCS[:, d, sl], AF.Exp, bias=V(GSM, gsm4[:, 1, d, c:c + 1]), scale=-1.0)
                    P.tt(KD[:, d * 512:d * 512 + n], ks[:, 0:n], e3[:, 0:n], ALU.mult)
                    e4 = stage()
                    for c in range(NC):
                        sl = slice(c * Lc, (c + 1) * Lc)
                        P.act(e4[:, sl], GCS[:, d, sl], AF.Exp, bias=V(GSM, gsm4[:, 2, d, c:c + 1]), scale=-1.0)
                    P.tt(KDEC[:, d * 512:d * 512 + n], ks[:, 0:n], e4[:, 0:n], ALU.mult)
                for c in range(NC):
                    sl = slice(c * Lc, (c + 1) * Lc)
                    vst3 = V(VST, VST.t[:, :].rearrange("p (kb l) -> p kb l", kb=16)[:, :, 0:Lc])
                    P.dma(vst3, P.region(("proj", j), projS["v"][:, t0 + c * Lc:t0 + (c + 1) * Lc].rearrange("(kb p) l -> p kb l", p=128)))
                    for kb in range(16):
                        P.transpose(PS[kb // 4][0:Lc, (kb % 4) * 128:(kb % 4 + 1) * 128], vst3[:, kb, :], IDF[:, :])
                    for q in range(4):
                        P.copy(VTK[0:Lc, q * 512:(q + 1) * 512], PS[q][0:Lc, :], eng="act" if q % 2 else "dve")
                    for h in range(4):
                        for d2 in range(2):
                            d = h * 2 + d2
                            P.mm(PS[4][0:Lc, h * Lc:(h + 1) * Lc], KD[:, d * 512 + c * Lc:d * 512 + (c + 1) * Lc],
                                 QD[:, d * 512 + c * Lc:d * 512 + (c + 1) * Lc], start=(d2 == 0), stop=(d2 == 1))
                    P.tt(V(ATT, ATT.t[0:Lc, 0:4 * Lc].rearrange("p (h l) -> p h l", h=4)),
                         V(PS[4], PS[4].t[0:Lc, 0:4 * Lc].rearrange("p (h l) -> p h l", h=4)),
                         V(CAUS, CAUS.t[0:Lc, 0:Lc].unsqueeze(1).to_broadcast([Lc, 4, Lc])), ALU.mult)
                    for d in range(8):
                        P.mm(PS[5 + d // 4][0:Lc, (d % 4) * 128:(d % 4 + 1) * 128], KDEC[:, d * 512 + c * Lc:d * 512 + (c + 1) * Lc], IDB[:, :])
                    P.copy(KDT[0:Lc, 0:512], PS[5][0:Lc, :], eng="act")
                    P.copy(KDT[0:Lc, 512:1024], PS[6][0:Lc, :])
                    for vb in range(16):
                        h = vb // 4
                        o_ = PS[vb // 8][:, (vb % 8) * Lc:(vb % 8 + 1) * Lc]
                        P.mm(o_, VTK[0:Lc, vb * 128:(vb + 1) * 128], ATT[0:Lc, h * Lc:(h + 1) * Lc], start=True, stop=False)
                        for d2 in range(2):
                            d = h * 2 + d2
                            P.mm(o_, SB3[:, d, (vb % 4) * 128:(vb % 4 + 1) * 128], QDEC[:, d * 512 + c * Lc:d * 512 + (c + 1) * Lc],
                                 start=False, stop=(d2 == 1))
                    for hf in range(2):
                        P.copy(XA[:, hf * 8:(hf + 1) * 8, sl], V(PS[hf], PS[hf].t[:, 0:8 * Lc].rearrange("p (a l) -> p a l", a=8)),
                               eng="act" if hf else "dve")
                    for d in range(8):
                        h = d // 2
                        ps = PS[2 + d % 2]
                        P.mm(ps[:, :], KDT[0:Lc, d * 128:(d + 1) * 128], VTK[0:Lc, h * 512:(h + 1) * 512])
                        P.stt(S3[:, d, :], S3[:, d, :], V(GSM, gsm4[:, 3, d, c:c + 1]), ps[:, :], ALU.mult, ALU.add)
                        P.copy(SB3[:, d, :], S3[:, d, :], eng="act")
                for h in range(4):
                    rstd_from_blocks([XA[:, 4 * h + i, 0:n] for i in range(4)], n, 512, RSTD)
                    for vb in range(4 * h, 4 * h + 4):
                        rs = stage()
                        P.dma(rs[:, 0:n], preg(j, ro + vb * 128, ro + (vb + 1) * 128))
                        tm = stage()
                        P.stt(tm[:, 0:n], XA[:, vb, 0:n], chv(L, "gla_norm_w", vb), RSTD[:, 0:n], ALU.mult, ALU.mult)
                        yb = YB[vb % 2]
                        P.tt(yb[:, 0:n], tm[:, 0:n], rs[:, 0:n], ALU.mult)
                        P.dma(P.region(("ybr", j), ybr[1, vb * 128:(vb + 1) * 128, t0:t0 + n]), yb[:, 0:n])
            P.fence([BIG] + subs)
            P.fence([XN, GCS])


        IOTA = P.sbuf("iota", [128, 512], F32)
        S5S = P.sbuf("s5s", [128, 16, 64], F32)
        TWO_PI = float(2 * np.pi); PI = float(np.pi)
        P.op("pool", lambda e: e.iota(IOTA.t[:, :], pattern=[[1, 512]], base=1, channel_multiplier=0,
                                      allow_small_or_imprecise_dtypes=True), reads=[], writes=[IOTA[:, :]])

        def wrap_pm_pi(r, tmp, lower=True):
            P.ts(tmp, r, PI, ALU.is_gt)
            P.stt(r, tmp, -TWO_PI, r, ALU.mult, ALU.add)
            if lower:
                P.ts(tmp, r, -PI, ALU.is_lt)
                P.stt(r, tmp, TWO_PI, r, ALU.mult, ALU.add)

        def sincos(sin_out, cos_out, ang, W):
            ti = stage(); tf = stage(); tm = stage()
            tiv = V(ti, ti.t[:, 0:W].bitcast(I32))
            P.ts(tiv, ang, 1.0 / TWO_PI, ALU.mult)
            P.copy(tf[:, 0:W], tiv)
            P.stt(tf[:, 0:W], tf[:, 0:W], -TWO_PI, ang, ALU.mult, ALU.add)
            wrap_pm_pi(tf[:, 0:W], tm[:, 0:W])
            P.act(sin_out, tf[:, 0:W], AF.Sin)
            P.ts(tf[:, 0:W], tf[:, 0:W], PI / 2, ALU.add)
            wrap_pm_pi(tf[:, 0:W], tm[:, 0:W], lower=False)
            P.act(cos_out, tf[:, 0:W], AF.Sin)
            return tf

        def s5(L):
            NS = 4
            W = 256
            sets = []
            subs = []
            for k in range(NS):
                d = {}
                base = k * 5376
                for i, nm in enumerate(["cos", "sin", "are", "aim", "magt", "vre", "vim", "wre", "wim"]):
                    d[nm] = subbuf("s5_%s%d" % (nm, k), BIG, base + i * 512, base + (i + 1) * 512, F32)
                for i, nm in enumerate(["xreb", "ximb", "ub"]):
                    d[nm] = subbuf("s5_%s%d" % (nm, k), BIG, base + 4608 + i * 256, base + 4608 + (i + 1) * 256, BF16)
                sets.append(d); subs += list(d.values())
            P.fence([BIG] + subs)
            BTR = subbuf("s5_btr", XN, 0, 2048, BF16); BTI = subbuf("s5_bti", XN, 2048, 4096, BF16)
            CTR = subbuf("s5_ctr", XN, 4096, 6144, BF16); CTI = subbuf("s5_cti", XN, 6144, 8192, BF16)
            P.fence([XN, BTR, BTI, CTR, CTI])
            st16 = ST32.t[:, :].bitcast(BF16)
            BT3R = Buf("s5_bt3r", st16[:, 0:2048]); BT3I = Buf("s5_bt3i", st16[:, 2048:4096])
            P.fence([ST32, BT3R, BT3I])
            LAMRE, LAMIM, DTV, MAG, THR, CORE, COIM, CARRE, CARIM, TA, TB, TC, TD, TE = (S5S[:, i, :] for i in range(14))
            for dst, nm in ((LAMRE, "s5_lam_re"), (LAMIM, "s5_lam_im")):
                src = wi[nm][L].rearrange("(q two) p -> q two p", two=2)
                P.dma(NAT[0:64, 0, 0:64], V(ext, src[:, 0, :]))
                P.dma(NAT[0:64, 0, 64:128], V(ext, src[:, 1, :]))
                P.transpose(PS[0][:, 0:64], NAT[0:64, 0, :], IDF[0:64, 0:64])
                P.copy(dst, PS[0][:, 0:64])
            P.dma(NAT[0:64, 1, 0:2], V(ext, wi["s5_log_dt"][L].rearrange("(q two) -> q two", two=2)))
            P.copy(V(NAT, NAT.t[0:64, 2, :].rearrange("p (two d) -> p two d", two=2)),
                   V(NAT, NAT.t[0:64, 1, 0:2].unsqueeze(2).to_broadcast([64, 2, 64])))
            P.transpose(PS[0][:, 0:64], NAT[0:64, 2, :], IDF[0:64, 0:64])
            P.act(DTV, PS[0][:, 0:64], AF.Exp)
            P.tt(TA, LAMRE, DTV, ALU.mult)
            P.act(MAG, TA, AF.Exp)
            P.tt(THR, LAMIM, DTV, ALU.mult)
            sincos(TA, TB, THR, 64)
            ti = stage(); tm = stage()
            tiv = V(ti, ti.t[:, 0:64].bitcast(I32))
            P.ts(tiv, THR, 1.0 / TWO_PI, ALU.mult)
            P.copy(TC, tiv)
            P.stt(THR, TC, -TWO_PI, THR, ALU.mult, ALU.add)
            wrap_pm_pi(THR, tm[:, 0:64])
            P.tt(TC, MAG, TB, ALU.mult)
            P.tt(TD, MAG, TA, ALU.mult)
            P.ts(TC, TC, -1.0, ALU.add)
            P.tt(TA, LAMRE, LAMRE, ALU.mult)
            P.tt(TB, LAMIM, LAMIM, ALU.mult)
            P.tt(TA, TA, TB, ALU.add)
            P.recip(TA, TA)
            P.tt(CORE, TC, LAMRE, ALU.mult)
            P.tt(TB, TD, LAMIM, ALU.mult)
            P.tt(CORE, CORE, TB, ALU.add)
            P.tt(CORE, CORE, TA, ALU.mult)
            P.tt(COIM, TD, LAMRE, ALU.mult)
            P.tt(TB, TC, LAMIM, ALU.mult)
            P.tt(COIM, COIM, TB, ALU.subtract)
            P.tt(COIM, COIM, TA, ALU.mult)
            P.memset(CARRE, 0.0, eng="dve"); P.memset(CARIM, 0.0, eng="dve")
            xaf = XA.t[:, :, :].rearrange("p a b -> p (a b)")
            P.memset(XA[:, :, :], 0.0, eng="dve")
            for i, nm in enumerate(("s5_b_re", "s5_b_im")):
                bn = xaf[:, i * 2048:(i + 1) * 2048].rearrange("p (q c) -> p q c", c=32)
                src = wi[nm][L].rearrange("(q two) p j -> two p q j", two=2)
                P.dma(V(XA, bn[0:64, :, 0:16]), V(ext, src[0]))
                P.dma(V(XA, bn[64:128, :, 16:32]), V(ext, src[1]))
            for i, nm in enumerate(("s5_c_re", "s5_c_im")):
                cn = xaf[:, (2 + i) * 2048:(3 + i) * 2048].rearrange("p (blk c) -> p blk c", c=128)
                src = wi[nm][L].rearrange("(blk q4 two) i p -> q4 two i blk p", q4=4, two=2)
                for q4 in range(4):
                    for g2 in range(2):
                        r0 = q4 * 32 + g2 * 16
                        P.dma(V(XA, cn[r0:r0 + 16, :, g2 * 64:(g2 + 1) * 64]), V(ext, src[q4, g2]))
            for blk in range(16):
                for i, dstb in enumerate((BTR, BTI)):
                    bn = xaf[:, i * 2048 + blk * 128:i * 2048 + (blk + 1) * 128]
                    ps = PS[(2 * blk + i) % 8]
                    P.transpose(ps[:, 0:128], V(XA, bn), IDF[:, :])
                    P.copy(dstb[:, blk * 128:(blk + 1) * 128], ps[:, 0:128], eng="act" if i else "dve")
                    ps3 = PS[(2 * blk + i + 2) % 8]
                    P.transpose(ps3[0:32, 0:128], V(XA, bn[:, 96:128]), IDF[:, :])
                    P.copy((BT3R, BT3I)[i][0:32, blk * 128:(blk + 1) * 128], ps3[0:32, 0:128], eng="dve" if i else "act")
                for i, dstb in enumerate((CTR, CTI)):
                    cn = xaf[:, (2 + i) * 2048 + blk * 128:(2 + i) * 2048 + (blk + 1) * 128]
                    ps = PS[(2 * blk + i + 4) % 8]
                    P.transpose(ps[:, 0:128], V(XA, cn), IDF[:, :])
                    if i == 0:
                        P.copy(dstb[:, blk * 128:(blk + 1) * 128], ps[:, 0:128], eng="act")
                    else:
                        P.ts(dstb[:, blk * 128:(blk + 1) * 128], ps[:, 0:128], -1.0, ALU.mult)
            so = SEGOFF["s5u"]
            steps = []
            for j in range(NT):
                t0, n = tiles[j]
                for w0 in range(0, n, W):
                    steps.append((j, t0 + w0, min(W, n - w0)))

            def pair_gen(q, k):
                d = sets[k]
                COS, SIN, ARE, AIM, MAGT = d["cos"], d["sin"], d["are"], d["aim"], d["magt"]
                VRE, VIM, WRE, WIM = d["vre"], d["vim"], d["wre"], d["wim"]
                XREb, XIMb, UB = d["xreb"], d["ximb"], d["ub"]
                blk, q4 = divmod(q, 4)
                rows = slice(q4 * 32, q4 * 32 + 32) if q4 < 3 else slice(0, 32)
                btr = BTR[rows, blk * 128:(blk + 1) * 128] if q4 < 3 else BT3R[0:32, blk * 128:(blk + 1) * 128]
                bti = BTI[rows, blk * 128:(blk + 1) * 128] if q4 < 3 else BT3I[0:32, blk * 128:(blk + 1) * 128]
                pb, py = PS[2 * k], PS[2 * k + 1]
                ang, tiF, tf, tm = WRE, VRE, VIM, WIM
                tiv = V(tiF, tiF.t[:, 0:W].bitcast(I32))
                P.ts(ang[:, 0:W], IOTA[:, 0:W], THR[:, q:q + 1], ALU.mult); yield
                P.ts(tiv, ang[:, 0:W], 1.0 / TWO_PI, ALU.mult); yield
                P.copy(tf[:, 0:W], tiv); yield
                P.stt(tf[:, 0:W], tf[:, 0:W], -TWO_PI, ang[:, 0:W], ALU.mult, ALU.add); yield
                P.ts(tm[:, 0:W], tf[:, 0:W], PI, ALU.is_gt); yield
                P.stt(tf[:, 0:W], tm[:, 0:W], -TWO_PI, tf[:, 0:W], ALU.mult, ALU.add); yield
                P.ts(tm[:, 0:W], tf[:, 0:W], -PI, ALU.is_lt); yield
                P.stt(tf[:, 0:W], tm[:, 0:W], TWO_PI, tf[:, 0:W], ALU.mult, ALU.add); yield
                P.act(SIN[:, 0:W], tf[:, 0:W], AF.Sin); yield
                P.ts(tf[:, 0:W], tf[:, 0:W], PI / 2, ALU.add); yield
                P.ts(tm[:, 0:W], tf[:, 0:W], PI, ALU.is_gt); yield
                P.stt(tf[:, 0:W], tm[:, 0:W], -TWO_PI, tf[:, 0:W], ALU.mult, ALU.add); yield
                P.act(COS[:, 0:W], tf[:, 0:W], AF.Sin); yield
                P.ts(ARE[:, 0:W], COS[:, 0:W], CORE[:, q:q + 1], ALU.mult); yield
                P.stt(ARE[:, 0:W], SIN[:, 0:W], COIM[:, q:q + 1], ARE[:, 0:W], ALU.mult, ALU.add); yield
                P.ts(AIM[:, 0:W], COS[:, 0:W], COIM[:, q:q + 1], ALU.mult); yield
                P.ts(tm[:, 0:W], SIN[:, 0:W], CORE[:, q:q + 1], ALU.mult); yield
                P.tt(AIM[:, 0:W], AIM[:, 0:W], tm[:, 0:W], ALU.subtract); yield
                P.ts(MAGT[:, 0:W], IOTA[:, 0:W], 0.0, ALU.mult, MAG[:, q:q + 1], ALU.add); yield
                r0 = so + blk * 128 + q4 * 32
                for (j, c0, n) in steps:
                    P.dma(WRE[rows, 0:n], P.region(("proj", j), proj_ap(r0, r0 + 32, c0, c0 + n))); yield
                    P.copy(UB[rows, 0:n], WRE[rows, 0:n], eng="act"); yield
                    P.mm(pb[:, 0:n], btr, UB[rows, 0:n]); yield
                    P.mm(pb[:, 256:256 + n], bti, UB[rows, 0:n]); yield
                    pr, pi_ = pb[:, 0:n], pb[:, 256:256 + n]
                    P.tt(VRE[:, 0:n], pr, ARE[:, 0:n], ALU.mult); yield
                    P.tt(WRE[:, 0:n], pi_, AIM[:, 0:n], ALU.mult); yield
                    P.tt(VRE[:, 0:n], VRE[:, 0:n], WRE[:, 0:n], ALU.subtract); yield
                    P.tt(VIM[:, 0:n], pi_, ARE[:, 0:n], ALU.mult); yield
                    P.tt(WIM[:, 0:n], pr, AIM[:, 0:n], ALU.mult); yield
                    P.tt(VIM[:, 0:n], VIM[:, 0:n], WIM[:, 0:n], ALU.add); yield
                    P.scan(WRE[:, 0:n], MAGT[:, 0:n], VRE[:, 0:n], CARRE[:, q:q + 1]); yield
                    P.scan(WIM[:, 0:n], MAGT[:, 0:n], VIM[:, 0:n], CARIM[:, q:q + 1]); yield
                    P.tt(VRE[:, 0:n], WRE[:, 0:n], COS[:, 0:n], ALU.mult, eng="pool"); yield
                    P.tt(VIM[:, 0:n], WIM[:, 0:n], SIN[:, 0:n], ALU.mult, eng="pool"); yield
                    P.tt(CARRE[:, q:q + 1], VRE[:, n - 1:n], VIM[:, n - 1:n], ALU.subtract, eng="pool"); yield
                    P.tt(XREb[:, 0:n], VRE[:, 0:n], VIM[:, 0:n], ALU.subtract, eng="pool"); yield
                    P.tt(VRE[:, 0:n], WRE[:, 0:n], SIN[:, 0:n], ALU.mult, eng="pool"); yield
                    P.tt(VIM[:, 0:n], WIM[:, 0:n], COS[:, 0:n], ALU.mult, eng="pool"); yield
                    P.tt(CARIM[:, q:q + 1], VRE[:, n - 1:n], VIM[:, n - 1:n], ALU.add, eng="pool"); yield
                    P.tt(XIMb[:, 0:n], VRE[:, 0:n], VIM[:, 0:n], ALU.add, eng="pool"); yield
                    P.mm(py[0:32, 0:n], CTR[:, blk * 128 + q4 * 32:blk * 128 + q4 * 32 + 32], XREb[:, 0:n], start=True, stop=False)
                    P.mm(py[0:32, 0:n], CTI[:, blk * 128 + q4 * 32:blk * 128 + q4 * 32 + 32], XIMb[:, 0:n], start=False, stop=True); yield
                    P.copy(VRE[0:32, 0:n], py[0:32, 0:n], eng="act"); yield
                    P.dma(P.region(("ys", j), ysT[blk * 128 + q4 * 32:blk * 128 + q4 * 32 + 32, c0:c0 + n]), VRE[0:32, 0:n]); yield

            pending = list(range(64))
            active = []
            for k in range(NS):
                active.append(pair_gen(pending.pop(0), k))
            slot_of = {id(g): k for k, g in enumerate(active)}
            for k, g in enumerate(active):
                for _ in range(S5STAG * k):
                    next(g)
            while active:
                for g in list(active):
                    try:
                        next(g)
                    except StopIteration:
                        k = slot_of.pop(id(g))
                        idx = active.index(g)
                        if pending:
                            ng = pair_gen(pending.pop(0), k)
                            slot_of[id(ng)] = k
                            active[idx] = ng
                        else:
                            active.remove(g)
            P.fence([BIG] + subs)
            P.fence([XN, BTR, BTI, CTR, CTI])
            P.fence([ST32, BT3R, BT3I])
            so = SEGOFF["s5u"]
            for j in range(NT):
                t0, n = tiles[j]
                for kb in range(16):
                    ysb = stage(); ub = stage(); t2 = stage()
                    P.dma(ysb[:, 0:n], P.region(("ys", j), ysT[kb * 128:(kb + 1) * 128, t0:t0 + n]))
                    P.dma(ub[:, 0:n], preg(j, so + kb * 128, so + (kb + 1) * 128))
                    P.stt(ysb[:, 0:n], ub[:, 0:n], chv(L, "s5_d", kb), ysb[:, 0:n], ALU.mult, ALU.add)
                    P.act(t2[:, 0:n], ysb[:, 0:n], AF.Square)
                    P.ts(t2[:, 0:n], t2[:, 0:n], 0.044715, ALU.mult, 1.0, ALU.add)
                    P.tt(t2[:, 0:n], t2[:, 0:n], ysb[:, 0:n], ALU.mult)
                    P.act(t2[:, 0:n], t2[:, 0:n], AF.Sigmoid, scale=1.5957691216057308)
                    P.tt(XA[:, kb, 0:n], ysb[:, 0:n], t2[:, 0:n], ALU.mult)
                    P.copy(XN[:, kb, 0:n], XA[:, kb, 0:n], eng="act")

                def cons(cb, ps, cw):
                    sg = stage()
                    P.act(sg[:, 0:n], ps, AF.Sigmoid, bias=chv(L, "s5_glu_b", cb))
                    yb = YB[cb % 2]
                    P.tt(yb[:, 0:n], XA[:, cb, 0:n], sg[:, 0:n], ALU.mult)
                    P.dma(P.region(("ybr", j), ybr[2, cb * 128:(cb + 1) * 128, t0:t0 + n]), yb[:, 0:n])
                linear(WM[L, "glu"], lambda kb: XN[:, kb, 0:n], n, cons)


        def merge(L):
            go = SEGOFF["gate"]
            for j in range(NT):
                t0, n = tiles[j]
                for b in range(3):
                    P.dma(XN[:, :, 0:n], P.region(("ybr", j), ybr[b, :, t0:t0 + n].rearrange("(kb p) n -> p kb n", p=128)))

                    def cons(cb, ps, cw, b=b):
                        g = stage()
                        r0 = go + b * 2048 + cb * 128
                        P.dma(g[:, 0:n], preg(j, r0, r0 + 128))
                        if b == 0:
                            P.tt(XA[:, cb, 0:n], ps, g[:, 0:n], ALU.mult)
                        else:
                            P.tt(g[:, 0:n], ps, g[:, 0:n], ALU.mult)
                            P.tt(XA[:, cb, 0:n], XA[:, cb, 0:n], g[:, 0:n], ALU.add)
                    linear(WM[L, "br%d" % b], lambda kb: XN[:, kb, 0:n], n, cons)
                for kb in range(NKB):
                    P.copy(XN[:, kb, 0:n], XA[:, kb, 0:n], eng="act" if kb % 2 else "dve")

                def cons_o(cb, ps, cw):
                    P.copy(XA[:, cb, 0:n], ps, eng="act")
                linear(WM[L, "out"], lambda kb: XN[:, kb, 0:n], n, cons_o)
                rstd_from_blocks([XA[:, kb, 0:n] for kb in range(NKB)], n, D, RSTD)
                for kb in range(NKB):
                    hb = HB[kb % 2]
                    P.dma(hb[:, 0:n], hreg(j, kb))
                    t1 = T1[kb % 3]
                    P.stt(t1[:, 0:n], XA[:, kb, 0:n], chv(L, "ln_mix_post", kb), RSTD[:, 0:n], ALU.mult, ALU.mult)
                    P.tt(hb[:, 0:n], hb[:, 0:n], t1[:, 0:n], ALU.add)
                    P.dma(hreg(j, kb), hb[:, 0:n], eng="sp")

        class _sub:
            def __init__(s, wm, p0, p1):
                s.wm = wm; s.NP = p1 - p0; s.NC = wm.NC; s.nkb = wm.nkb; s.N = (p1 - p0) * 512; s.grp = wm.grp
                s.t = wm.t[p0:p1]

        for L in range(depth):
            if not skip_ffn:
                ffn(L, "f1", "ln_f1_pre", "ln_f1_post")
            if stop_after == "ffn1":
                break
            layer_smalls(L)
            inproj(L)
            if stop_after == "inproj":
                break
            if not skip_ssd:
                ssd(L)
            if stop_after == "ssd":
                break
            if not skip_gla:
                gla(L)
            if stop_after == "gla":
                break
            s5(L)
            if stop_after == "s5":
                break
            merge(L)
            if stop_after == "mix":
                break
            if not skip_ffn:
                ffn(L, "f2", "ln_f2_pre", "ln_f2_post")

        finals = []
        for j in range(1, NT):
            t0, n = tiles[j]
            P.dma(XA[:, :, 0:n], hreg(j))
            for s0 in range(0, n, 128):
                tm = V(XN, XN.t[:, 0:8, :].bitcast(F32).rearrange("p a b -> p (a b)"))
                for kb in range(NKB):
                    ps = PS[kb % 8]
                    P.transpose(ps[:, 0:128], XA[:, kb, s0:s0 + 128], IDF[:, :])
                    P.copy(tm[:, kb * 128:(kb + 1) * 128], ps[:, 0:128], eng="act" if kb % 2 else "dve")
                r0 = t0 - NMETA + s0
                finals.append(P.dma(P.region(("out", j), out_d[r0:r0 + 128, :]), tm, eng="sp"))
        st = P.emit(finals)
        print("ops/waits per engine:", st, "sems:", P.nsem)
    return nc


N_CORES = 8
S5VAR = "split"
S5STAG = 11


def kernel(**inputs):
    seq = inputs["x"].shape[1]
    depth = inputs["w_in"].shape[0]
    nc = build_program(seq, depth)
    in_maps = []
    for c in range(N_CORES):
        b = c % inputs["x"].shape[0]
        m = {k: np.ascontiguousarray(v) for k, v in inputs.items() if k != "x"}
        m["x"] = np.ascontiguousarray(inputs["x"][b])
        in_maps.append(m)
    res = run_bass_kernel_spmd(nc, in_maps, core_ids=list(range(N_CORES)))
    out = np.stack([res.results[b]["out"] for b in range(inputs["x"].shape[0])], axis=0)
    return out.astype(np.float32)
```

```python
import numpy as np
import concourse.bass as bass
import concourse.mybir as mybir
from concourse.bass_utils import run_bass_kernel_spmd
from contextlib import ExitStack

F32 = mybir.dt.float32; BF16 = mybir.dt.bfloat16; I32 = mybir.dt.int32
AF = mybir.ActivationFunctionType; ALU = mybir.AluOpType

D = 2048; NKB = 16; DFF = 5632; NFB = 44; NMETA = 16; EPS = 1e-6
SEGS = [("z", 2048), ("xbc", 4096), ("dt", 32), ("q", 1024), ("k", 1024), ("v", 2048), ("r", 2048),
        ("glr", 16), ("s5u", 2048), ("gate", 6144)]
DIN = sum(w for _, w in SEGS)
SEGOFF = {}
_o = 0
for _n, _w in SEGS:
    SEGOFF[_n] = _o; _o += _w


class Buf:
    def __init__(s, name, t):
        s.name = name; s.t = t
        s.last_w = None
        s.readers = {}
        s.sem = None; s.dma_cnt = 0

    def __getitem__(s, idx):
        return V(s, s.t[idx])


class V:
    def __init__(s, buf, ap):
        s.buf = buf; s.ap = ap

    def __getitem__(s, idx):
        return V(s.buf, s.ap[idx])

    def re(s, pat, **kw):
        return V(s.buf, s.ap.rearrange(pat, **kw))


class Op:
    __slots__ = ("eng", "fn", "deps", "dma_dst", "needed", "tok")

    def __init__(s, eng, fn):
        s.eng = eng; s.fn = fn; s.deps = []; s.dma_dst = None; s.needed = False; s.tok = None


class Prog:
    ENGS = ("pe", "dve", "act", "pool", "sp")

    def __init__(s, nc, es):
        s.nc = nc; s.es = es
        s.ops = {e: [] for e in s.ENGS}
        s.esem = {e: es.enter_context(nc.semaphore("s_" + e)) for e in ("pe", "dve", "act", "pool")}
        s.nsem = 4
        s.bufs = {}

    def sbuf(s, name, shape, dt):
        return Buf(name, s.es.enter_context(s.nc.sbuf_tensor(name, list(shape), dt)))

    def psum(s, name, shape, dt=F32):
        return Buf(name, s.es.enter_context(s.nc.psum_tensor(name, list(shape), dt)))

    def dram(s, name, shape, dt, kind="Internal"):
        return Buf(name, s.nc.dram_tensor(name, list(shape), dt, kind=kind).ap())

    def region(s, key, ap):
        if key not in s.bufs:
            s.bufs[key] = Buf(str(key), None)
        return V(s.bufs[key], ap)

    def op(s, eng, fn, reads=(), writes=(), dma=False):
        o = Op(eng, fn)
        deps = {}

        def add(d):
            if d is not None and d is not o:
                deps[id(d)] = d
        for v in reads:
            add(v.buf.last_w)
        for v in writes:
            add(v.buf.last_w)
            for r in v.buf.readers.values():
                add(r)
        for v in writes:
            v.buf.last_w = o; v.buf.readers = {}
        for v in reads:
            if v.buf.last_w is o:
                continue
            key = ("dma", len(v.buf.readers)) if dma else eng
            v.buf.readers[key] = o
        for d in deps.values():
            if d.eng == "pe" and eng == "pe" and d.dma_dst is None and not dma:
                continue
            d.needed = True
            o.deps.append(d)
        if dma:
            b = writes[0].buf
            if b.sem is None:
                b.sem = s.es.enter_context(s.nc.semaphore("d_%d" % s.nsem)); s.nsem += 1
            b.dma_cnt += 1
            o.dma_dst = b
            o.tok = (b.sem, 16 * b.dma_cnt)
        s.ops[eng].append(o)
        return o

    def dma(s, out, in_, eng="sp", extra_reads=()):
        return s.op(eng, lambda e: e.dma_start(out=out.ap, in_=in_.ap), reads=[in_] + list(extra_reads), writes=[out], dma=True)

    def fence(s, bufs):
        vs = [V(b, None) for b in bufs]
        return s.op("dve", lambda e: e.memset(s.fence_ap, 0.0), reads=vs, writes=vs + [V(s.fence_buf, None)])

    def mm(s, out, lhsT, rhs, start=True, stop=True):
        rw = [] if start else [out]
        return s.op("pe", lambda e: e.matmul(out.ap, lhsT.ap, rhs.ap, start=start, stop=stop),
                    reads=[lhsT, rhs] + rw, writes=[out])

    def transpose(s, out, in_, ident):
        return s.op("pe", lambda e: e.transpose(out.ap, in_.ap, ident.ap), reads=[in_, ident], writes=[out])

    def act(s, out, in_, func, bias=None, scale=1.0):
        kw = {}
        rd = [in_]
        if bias is not None:
            if isinstance(bias, V):
                kw["bias"] = bias.ap; rd.append(bias)
            else:
                kw["bias"] = bias
        if isinstance(scale, V):
            rd.append(scale); sc = scale.ap
        else:
            sc = scale
        return s.op("act", lambda e: e.activation(out=out.ap, in_=in_.ap, func=func, scale=sc, **kw),
                    reads=rd, writes=[out])

    def tt(s, out, a, b, op, eng="dve"):
        return s.op(eng, lambda e: e.tensor_tensor(out=out.ap, in0=a.ap, in1=b.ap, op=op), reads=[a, b], writes=[out])

    def ts(s, out, a, s1, op0, s2=None, op1=None, eng="dve"):
        rd = [a]
        s1a = s1.ap if isinstance(s1, V) else s1
        s2a = s2.ap if isinstance(s2, V) else s2
        if isinstance(s1, V): rd.append(s1)
        if isinstance(s2, V): rd.append(s2)
        if op1 is not None:
            return s.op(eng, lambda e: e.tensor_scalar(out=out.ap, in0=a.ap, scalar1=s1a, scalar2=s2a, op0=op0, op1=op1),
                        reads=rd, writes=[out])
        return s.op(eng, lambda e: e.tensor_scalar(out=out.ap, in0=a.ap, scalar1=s1a, scalar2=None, op0=op0),
                    reads=rd, writes=[out])

    def stt(s, out, a, sc, b, op0, op1):
        rd = [a, b]
        sca = sc.ap if isinstance(sc, V) else sc
        if isinstance(sc, V): rd.append(sc)
        return s.op("dve", lambda e: e.scalar_tensor_tensor(out=out.ap, in0=a.ap, scalar=sca, in1=b.ap, op0=op0, op1=op1),
                    reads=rd, writes=[out])

    def scan(s, out, d0, d1, init, op0=ALU.mult, op1=ALU.add):
        rd = [d0, d1]
        ia = init.ap if isinstance(init, V) else init
        if isinstance(init, V): rd.append(init)
        return s.op("dve", lambda e: e.tensor_tensor_scan(out=out.ap, data0=d0.ap, data1=d1.ap, initial=ia, op0=op0, op1=op1),
                    reads=rd, writes=[out])

    def recip(s, out, in_):
        return s.op("dve", lambda e: e.reciprocal(out=out.ap, in_=in_.ap), reads=[in_], writes=[out])

    def copy(s, out, in_, eng="dve"):
        if eng == "act":
            return s.op("act", lambda e: e.copy(out=out.ap, in_=in_.ap), reads=[in_], writes=[out])
        return s.op(eng, lambda e: e.tensor_copy(out=out.ap, in_=in_.ap), reads=[in_], writes=[out])

    def memset(s, out, val, eng="pool"):
        return s.op(eng, lambda e: e.memset(out.ap, val), reads=[], writes=[out])

    def emit(s, final_ops):
        nc = s.nc
        for e in ("pe", "dve", "act", "pool"):
            c = 0
            for o in s.ops[e]:
                if o.dma_dst is None and o.needed:
                    c += 1
                    o.tok = (s.esem[e], c)
        engmap = {"pe": "tensor", "dve": "vector", "act": "scalar", "pool": "gpsimd", "sp": "sync"}
        stats = {}
        with nc.Block() as block:
            for e in s.ENGS:
                def body(eng, ops=s.ops[e], e=e):
                    waited = {}
                    nw = 0
                    for o in ops:
                        need = {}
                        for d in o.deps:
                            sem, val = d.tok
                            k = id(sem)
                            if waited.get(k, 0) >= val:
                                continue
                            if k not in need or need[k][1] < val:
                                need[k] = (sem, val)
                        for k, (sem, val) in need.items():
                            eng.wait_ge(sem, val); waited[k] = val; nw += 1
                        ins = o.fn(eng)
                        if o.dma_dst is not None:
                            ins.then_inc(o.dma_dst.sem, 16)
                        elif o.needed:
                            ins.then_inc(s.esem[e], 1)
                    if e == "sp":
                        for o in final_ops:
                            sem, val = o.tok
                            eng.wait_ge(sem, val)
                    stats[e] = (len(ops), nw)
                getattr(block, engmap[e])(body)
        return stats


class WMat:
    def __init__(s, P, name, src_ap, K, N, grp):
        s.K = K; s.N = N
        s.nkb = K // 128
        s.NC = (s.nkb + 15) // 16
        s.NP = (N + 511) // 512
        s.grp = grp
        s.t = P.nc.dram_tensor(name, [s.NP, s.NC, 128, 16, 512], BF16, kind="Internal").ap()
        for pn in range(s.NP):
            w = min(512, N - pn * 512)
            for kc in range(s.NC):
                nk = min(16, s.nkb - kc * 16)
                src = src_ap[kc * 2048:kc * 2048 + nk * 128, pn * 512:pn * 512 + w].rearrange("(kb p) c -> p kb c", p=128)
                P.dma(V(grp, s.t[pn, kc, :, 0:nk, 0:w]), V(P.wsrc, src), eng="pool")


def build_program(seq, depth, dbg=None, stop_after=None, skip_ffn=False, skip_gla=False, skip_ssd=False):
    T = NMETA + seq
    tiles = [(0, NMETA)] + [(NMETA + 512 * i, 512) for i in range(seq // 512)]
    assert seq % 512 == 0
    NT = len(tiles)
    nc = bass.Bass("TRN2", target_bir_lowering=False)
    es = ExitStack()
    with es:
        P = Prog(nc, es)
        dbg = dbg or []

        def din(name, shape):
            return nc.dram_tensor(name, list(shape), F32, kind="ExternalInput").ap()
        x_in = din("x", [seq, D]); meta_in = din("meta", [NMETA, D])
        wi = {}
        for nm, shp in [("ln_f1_pre", [depth, D]), ("w_f1_gu", [depth, D, 2 * DFF]), ("w_f1_down", [depth, DFF, D]),
                        ("ln_f1_post", [depth, D]), ("ln_mix_pre", [depth, D]), ("w_in", [depth, D, DIN]),
                        ("conv_w", [depth, 4, 4096]), ("conv_b", [depth, 4096]), ("dt_bias", [depth, 32]),
                        ("a_log", [depth, 32]), ("ssd_d", [depth, 32]), ("ssd_norm_w", [depth, D]),
                        ("gla_w2", [depth, 16, 1024]), ("gla_b", [depth, 1024]), ("gla_norm_w", [depth, D]),
                        ("s5_lam_re", [depth, 128, 64]), ("s5_lam_im", [depth, 128, 64]), ("s5_log_dt", [depth, 128]),
                        ("s5_b_re", [depth, 128, 64, 16]), ("s5_b_im", [depth, 128, 64, 16]),
                        ("s5_c_re", [depth, 128, 16, 64]), ("s5_c_im", [depth, 128, 16, 64]),
                        ("s5_d", [depth, D]), ("s5_glu_w", [depth, D, D]), ("s5_glu_b", [depth, D]),
                        ("w_branch", [depth, 3, D, D]), ("w_out", [depth, D, D]), ("ln_mix_post", [depth, D]),
                        ("ln_f2_pre", [depth, D]), ("w_f2_gu", [depth, D, 2 * DFF]), ("w_f2_down", [depth, DFF, D]),
                        ("ln_f2_post", [depth, D])]:
            wi[nm] = din(nm, shp)
        out_d = nc.dram_tensor("out", [seq, D], F32, kind="ExternalOutput").ap()
        P.wsrc = Buf("wsrc", None)
        ext = P.wsrc

        def scratch(name, shape, dt):
            kind = "ExternalOutput" if name in dbg else "Internal"
            return nc.dram_tensor(name, list(shape), dt, kind=kind).ap()
        hT = scratch("hT", [D, T], F32)

        def hreg(j, kb=None):
            t0, n = tiles[j]
            if kb is None:
                return P.region(("h", j), hT[:, t0:t0 + n].rearrange("(kb p) n -> p kb n", p=128))
            return P.region(("h", j), hT[kb * 128:(kb + 1) * 128, t0:t0 + n])

        WT = [P.sbuf("wt%d" % i, [128, 16, 512], BF16) for i in range(2)]
        XA = P.sbuf("xa", [128, 16, 512], F32)
        XN = P.sbuf("xn", [128, 16, 512], BF16)
        BIG = P.sbuf("big", [128, 44, 512], BF16)
        SQ = [P.sbuf("sq%d" % i, [128, 512], BF16) for i in range(2)]
        T1 = [P.sbuf("t1_%d" % i, [128, 512], F32) for i in range(3)]
        RSTD = P.sbuf("rstd", [128, 512], F32)
        HB = [P.sbuf("hb%d" % i, [128, 512], F32) for i in range(2)]
        ONES = P.sbuf("ones", [128, 128], BF16)
        IDF = P.sbuf("idf", [128, 128], F32)
        NAT = P.sbuf("nat", [128, 3, 128], F32)
        CHV = P.sbuf("chv", [128, depth, 384], F32)
        PS = [P.psum("ps%d" % i, [128, 512]) for i in range(8)]
        IDB = P.sbuf("idb", [128, 128], BF16)
        CV = [P.sbuf("cv%d" % i, [128, 520], F32) for i in range(2)]
        YB = [P.sbuf("yb%d" % i, [128, 512], BF16) for i in range(2)]
        ST32 = P.sbuf("st32", [128, 4096], F32)
        STB = P.sbuf("stb", [128, 4096], BF16)
        MASK64 = P.sbuf("mask64", [128, 512], F32)
        MASKNEG = P.sbuf("maskneg", [64, 64], F32)
        CAUS = P.sbuf("caus", [64, 64], F32)
        SMALL = P.sbuf("small", [128, 256], F32)
        DTK = P.sbuf("dtk", [64, 64], F32)
        EAEND = P.sbuf("eaend", [128, 32], F32)
        VST = P.sbuf("vst", [128, 1024], F32)

        def subbuf(name, base, e0, e1, dt, shape=None):
            ap = base.t[:, :, :].rearrange("p a b -> p (a b)")[:, e0:e1]
            if dt == F32:
                ap = ap.bitcast(F32)
            return Buf(name, ap)
        wt_i = [0]

        P.memset(ONES[:, :], 1.0)
        P.memset(MASK64[:, :], 1.0)
        P.memset(V(MASK64, MASK64.t[:, :].rearrange("p (c l) -> p c l", l=64)[:, :, 0:1]), 0.0)
        P.memset(MASKNEG[:, :], 0.0)
        P.op("pool", lambda e: e.affine_select(out=MASKNEG.t[:, :], in_=MASKNEG.t[:, :], pattern=[[1, 64]],
                                               compare_op=ALU.is_ge, fill=-30000.0, base=0, channel_multiplier=-1),
             reads=[MASKNEG[:, :]], writes=[MASKNEG[:, :]])
        P.memset(CAUS[:, :], 1.0)
        P.op("pool", lambda e: e.affine_select(out=CAUS.t[:, :], in_=CAUS.t[:, :], pattern=[[1, 64]],
                                               compare_op=ALU.is_ge, fill=0.0, base=0, channel_multiplier=-1),
             reads=[CAUS[:, :]], writes=[CAUS[:, :]])
        P.memset(IDF[:, :], 1.0)
        P.op("pool", lambda e: e.affine_select(out=IDF.t[:, :], in_=IDF.t[:, :], pattern=[[-1, 128]],
                                               compare_op=ALU.is_equal, fill=0.0, base=0, channel_multiplier=1),
             reads=[IDF[:, :]], writes=[IDF[:, :]])
        CH_LAYOUT = [("ln_f1_pre", 16), ("ln_f1_post", 16), ("ln_mix_pre", 16), ("ln_mix_post", 16), ("ln_f2_pre", 16),
                     ("ln_f2_post", 16), ("ssd_norm_w", 16), ("gla_norm_w", 16), ("s5_d", 16), ("s5_glu_b", 16),
                     ("conv_b", 32), ("conv_w0", 32), ("conv_w1", 32), ("conv_w2", 32), ("conv_w3", 32), ("gla_b", 8)]
        CHO = {}
        _c = 0
        for nm, m_ in CH_LAYOUT:
            CHO[nm] = _c; _c += m_
        NCH = 384
        for L in range(depth):
            nat = NAT
            P.memset(nat[:, :, :], 0.0)
            for nm, m_ in CH_LAYOUT:
                if nm.startswith("conv_w"):
                    src = wi["conv_w"][L, int(nm[-1])]
                else:
                    src = wi[nm][L]
                c0 = CHO[nm]
                r = 0
                while r < m_:
                    g, p0 = divmod(c0 + r, 128)
                    cnt = min(m_ - r, 128 - p0)
                    P.dma(nat[p0:p0 + cnt, g, :], V(ext, src[r * 128:(r + cnt) * 128].rearrange("(kb p) -> kb p", p=128)))
                    r += cnt
            for g in range(3):
                P.transpose(PS[g][:, 0:128], nat[:, g, :], IDF[:, :])
                P.copy(CHV[:, L, g * 128:(g + 1) * 128], PS[g][:, 0:128])

        P.copy(IDB[:, :], IDF[:, :])

        def chv(L, nm, kb):
            c = CHO[nm] + kb
            return CHV[:, L, c:c + 1]
        WM = {}
        for L in range(depth):
            g1 = Buf("wg_f1_%d" % L, None); gm = Buf("wg_mx_%d" % L, None); g2 = Buf("wg_f2_%d" % L, None)
            WM[L, "f1_gu"] = WMat(P, "wb_f1gu%d" % L, wi["w_f1_gu"][L], D, 2 * DFF, g1)
            WM[L, "f1_dn"] = WMat(P, "wb_f1dn%d" % L, wi["w_f1_down"][L], DFF, D, g1)
            if stop_after != "ffn1":
                for nm, w in SEGS:
                    o = SEGOFF[nm]
                    WM[L, "in_" + nm] = WMat(P, "wb_in_%s%d" % (nm, L), wi["w_in"][L][:, o:o + w], D, w, gm)
                for b in range(3):
                    WM[L, "br%d" % b] = WMat(P, "wb_br%d_%d" % (b, L), wi["w_branch"][L, b], D, D, gm)
                WM[L, "out"] = WMat(P, "wb_out%d" % L, wi["w_out"][L], D, D, gm)
                WM[L, "glu"] = WMat(P, "wb_glu%d" % L, wi["s5_glu_w"][L], D, D, gm)
                WM[L, "f2_gu"] = WMat(P, "wb_f2gu%d" % L, wi["w_f2_gu"][L], D, 2 * DFF, g2)
                WM[L, "f2_dn"] = WMat(P, "wb_f2dn%d" % L, wi["w_f2_down"][L], DFF, D, g2)

        def load_transposed(j):
            t0, n = tiles[j]
            for s0 in range(0, n, 128):
                m = min(128, n - s0)
                if j == 0:
                    src = meta_in[s0:s0 + m, :]
                else:
                    src = x_in[t0 - NMETA + s0:t0 - NMETA + s0 + m, :]
                tm = V(XA, XA.t[0:m, 0:4, :].rearrange("p a b -> p (a b)"))
                P.dma(tm, V(ext, src))
                for kb in range(NKB):
                    ps = PS[kb % 8]
                    P.transpose(ps[:, 0:m], tm[:, kb * 128:(kb + 1) * 128], IDF[0:m, 0:m])
                    hb = HB[kb % 2]
                    P.copy(hb[:, 0:m], ps[:, 0:m], eng="act" if kb % 2 else "dve")
                    P.dma(P.region(("h", j), hT[kb * 128:(kb + 1) * 128, t0 + s0:t0 + s0 + m]), hb[:, 0:m], eng="sp")
        for j in range(NT):
            load_transposed(j)

        def rstd_from_blocks(blocks, n, nelem, out_rstd):
            ps = PS[7]
            nb = len(blocks)
            for i, b in enumerate(blocks):
                sq = SQ[i % 2]
                P.act(sq[:, 0:n], b, AF.Square)
                P.mm(ps[:, 0:n], ONES[:, :], sq[:, 0:n], start=(i == 0), stop=(i == nb - 1))
            P.act(out_rstd[:, 0:n], ps[:, 0:n], AF.Sqrt, bias=EPS, scale=1.0 / nelem)
            P.recip(out_rstd[:, 0:n], out_rstd[:, 0:n])

        def linear(wm, rhs_of_kb, n, consume, K_nkb=None):
            bank = 0
            for pn in range(wm.NP):
                w = min(512, wm.N - pn * 512)
                ncb = (w + 127) // 128
                banks = [PS[(pn % 2) * 4 + c] for c in range(ncb)] if wm.NC > 1 else None
                for kc in range(wm.NC):
                    nk = min(16, wm.nkb - kc * 16)
                    wt = WT[wt_i[0] % 2]; wt_i[0] += 1
                    P.dma(wt[:, 0:nk, 0:w], V(wm.grp, wm.t[pn, kc, :, 0:nk, 0:w]))
                    for c in range(ncb):
                        cw = min(128, w - c * 128)
                        if wm.NC > 1:
                            ps = banks[c]
                        else:
                            ps = PS[bank % 7]
                        for kb in range(nk):
                            P.mm(ps[0:cw, 0:n], wt[:, kb, c * 128:c * 128 + cw], rhs_of_kb(kc * 16 + kb),
                                 start=(kc == 0 and kb == 0), stop=(kc == wm.NC - 1 and kb == nk - 1))
                        if wm.NC == 1:
                            consume(pn * 4 + c, ps[0:cw, 0:n], cw)
                            bank += 1
                if wm.NC > 1:
                    for c in range(ncb):
                        cw = min(128, w - c * 128)
                        consume(pn * 4 + c, banks[c][0:cw, 0:n], cw)

        def ffn(L, which, ln_pre, ln_post):
            wgu = WM[L, which + "_gu"]; wdn = WM[L, which + "_dn"]
            for j in range(NT):
                t0, n = tiles[j]
                P.dma(XA[:, :, 0:n], hreg(j))
                rstd_from_blocks([XA[:, kb, 0:n] for kb in range(NKB)], n, D, RSTD)
                for kb in range(NKB):
                    P.stt(XN[:, kb, 0:n], XA[:, kb, 0:n], chv(L, ln_pre, kb), RSTD[:, 0:n], ALU.mult, ALU.mult)
                def cons_gate(cb, ps, cw):
                    P.act(BIG[:, cb, 0:n], ps, AF.Silu)

                def cons_up(cb, ps, cw):
                    P.tt(BIG[:, cb, 0:n], BIG[:, cb, 0:n], ps, ALU.mult)
                linear(_sub(wgu, 0, NFB // 4), lambda kb: XN[:, kb, 0:n], n, cons_gate)
                linear(_sub(wgu, NFB // 4, NFB // 2), lambda kb: XN[:, kb, 0:n], n, cons_up)
                def cons_dn(cb, ps, cw):
                    P.copy(XA[:, cb, 0:n], ps, eng="act")
                linear(wdn, lambda kb: BIG[:, kb, 0:n], n, cons_dn)
                rstd_from_blocks([XA[:, kb, 0:n] for kb in range(NKB)], n, D, RSTD)
                for kb in range(NKB):
                    hb = HB[kb % 2]
                    P.dma(hb[:, 0:n], hreg(j, kb))
                    t1 = T1[kb % 3]
                    P.stt(t1[:, 0:n], XA[:, kb, 0:n], chv(L, ln_post, kb), RSTD[:, 0:n], ALU.mult, ALU.mult)
                    P.stt(hb[:, 0:n], t1[:, 0:n], 0.5, hb[:, 0:n], ALU.mult, ALU.add)
                    P.dma(hreg(j, kb), hb[:, 0:n], eng="sp")


        projS = {nm: scratch("proj_" + nm, [w, T], F32) for nm, w in SEGS}

        def proj_ap(r0, r1, a, b):
            for nm, w in SEGS:
                o = SEGOFF[nm]
                if o <= r0 < o + w:
                    assert r1 <= o + w
                    return projS[nm][r0 - o:r1 - o, a:b]
            raise AssertionError
        ybr = scratch("ybr", [3, D, T], BF16)
        ysT = scratch("ysT", [D, T], F32)
        FEN = P.sbuf("fence", [128, 8], F32)
        P.fence_ap = FEN.t[:, 0:1]; P.fence_buf = FEN

        def preg(j, r0, r1, c0=None, c1=None):
            t0, n = tiles[j]
            a = t0 if c0 is None else c0
            b = t0 + n if c1 is None else c1
            return P.region(("proj", j), proj_ap(r0, r1, a, b))

        stage_i = [0]

        def stage():
            bufs = T1 + HB
            b = bufs[stage_i[0] % 5]; stage_i[0] += 1
            return b

        def layer_smalls(L):
            c0 = L * 64
            P.dma(SMALL[0:32, c0:c0 + 1], V(ext, wi["dt_bias"][L].rearrange("(k o) -> k o", o=1)))
            P.dma(SMALL[0:32, c0 + 1:c0 + 2], V(ext, wi["a_log"][L].rearrange("(k o) -> k o", o=1)))
            P.act(SMALL[0:32, c0 + 1:c0 + 2], SMALL[0:32, c0 + 1:c0 + 2], AF.Exp)
            P.ts(SMALL[0:32, c0 + 1:c0 + 2], SMALL[0:32, c0 + 1:c0 + 2], -1.0, ALU.mult)
            P.dma(SMALL[:, c0 + 18:c0 + 50], V(ext, wi["ssd_d"][L].partition_broadcast(128)))
            raw = SMALL.t[:, c0 + 18:c0 + 50].rearrange("p (kb two) -> p kb two", two=2)
            P.copy(SMALL[0:64, c0 + 2:c0 + 18], V(SMALL, raw[0:64, :, 0]))
            P.copy(SMALL[64:128, c0 + 2:c0 + 18], V(SMALL, raw[64:128, :, 1]))

        def inproj(L):
            c0 = L * 64
            for j in range(NT):
                t0, n = tiles[j]
                P.dma(XA[:, :, 0:n], hreg(j))
                rstd_from_blocks([XA[:, kb, 0:n] for kb in range(NKB)], n, D, RSTD)
                for kb in range(NKB):
                    P.stt(XN[:, kb, 0:n], XA[:, kb, 0:n], chv(L, "ln_mix_pre", kb), RSTD[:, 0:n], ALU.mult, ALU.mult)
                for nm, w in SEGS:
                    off = SEGOFF[nm]
                    cnt = [0]

                    def cons(cb, ps, cw, nm=nm, off=off, cnt=cnt):
                        st = stage()
                        sv = st[0:cw, 0:n]
                        if nm in ("z", "r"):
                            P.act(sv, ps, AF.Silu)
                        elif nm == "gate":
                            P.act(sv, ps, AF.Sigmoid)
                        elif nm == "dt":
                            P.act(sv, ps, AF.Exp, bias=SMALL[0:32, c0:c0 + 1])
                            P.act(sv, sv, AF.Ln, bias=1.0)
                        elif nm == "q":
                            P.ts(sv, ps, 1.0 / 16.0, ALU.mult)
                        else:
                            cnt[0] += 1
                            P.copy(sv, ps, eng="act" if cnt[0] % 2 else "dve")
                        P.dma(preg(j, off + cb * 128, off + cb * 128 + cw), sv, eng="act")
                    linear(WM[L, "in_" + nm], lambda kb: XN[:, kb, 0:n], n, cons)

        def ssd(L):
            c0 = L * 64
            AHEAD = SMALL[0:32, c0 + 1:c0 + 2]
            XDT = subbuf("xdt", BIG, 0, 4096, BF16)
            XW = subbuf("xw", BIG, 4096, 8192, BF16)
            BTK = subbuf("btk", BIG, 8192, 10240, BF16)
            ES = subbuf("es", BIG, 10240, 12288, BF16)
            STT = subbuf("stt", BIG, 12288, 14336, BF16)
            CDEC = subbuf("cdec", BIG, 14336, 16384, BF16)
            ACS = subbuf("acs", BIG, 16384, 17408, F32)
            NACS = subbuf("nacs", BIG, 17408, 18432, F32)
            EA = subbuf("ea", BIG, 18432, 19456, F32)
            DTE = subbuf("dte", BIG, 19456, 20480, F32)
            DTs = subbuf("dts", BIG, 20480, 21504, F32)
            subs = [XDT, XW, BTK, ES, STT, CDEC, ACS, NACS, EA, DTE, DTs]
            P.fence([BIG] + subs)
            S32 = V(ST32, ST32.t[:, 0:2048]); SB = V(STB, STB.t[:, 0:2048])
            P.memset(S32, 0.0); P.memset(SB, 0.0)
            xo = SEGOFF["xbc"]
            for j in range(NT):
                t0, n = tiles[j]
                Lc = min(64, n); NC = n // Lc
                for cb in range(32):
                    cv = CV[cb % 2]
                    r0 = xo + cb * 128
                    if j == 0:
                        P.memset(cv[:, 0:3], 0.0)
                        P.dma(cv[:, 3:3 + n], preg(j, r0, r0 + 128))
                    else:
                        P.dma(cv[:, 0:n + 3], preg(j, r0, r0 + 128, t0 - 3, t0 + n), extra_reads=[preg(j - 1, r0, r0 + 128)])
                    acc = stage()
                    P.ts(acc[:, 0:n], cv[:, 3:3 + n], chv(L, "conv_w3", cb), ALU.mult, chv(L, "conv_b", cb), ALU.add)
                    for k in (2, 1, 0):
                        P.stt(acc[:, 0:n], cv[:, k:k + n], chv(L, "conv_w%d" % k, cb), acc[:, 0:n], ALU.mult, ALU.add)
                    if cb < 16:
                        P.act(XA[:, cb, 0:n], acc[:, 0:n], AF.Silu)
                    else:
                        P.act(XN[:, cb - 16, 0:n], acc[:, 0:n], AF.Silu)
                do = SEGOFF["dt"]
                P.dma(DTs[0:32, 0:n], preg(j, do, do + 32))
                P.ts(NACS[0:32, 0:n], DTs[0:32, 0:n], AHEAD, ALU.mult)
                P.scan(ACS[0:32, 0:n], MASK64[0:32, 0:n], NACS[0:32, 0:n], 0.0)
                P.ts(NACS[0:32, 0:n], ACS[0:32, 0:n], -1.0, ALU.mult)
                P.act(EA[0:32, 0:n], ACS[0:32, 0:n], AF.Exp)
                for c in range(NC):
                    sl = slice(c * Lc, (c + 1) * Lc)
                    P.act(DTE[0:32, sl], ACS[0:32, sl], AF.Exp, bias=ACS[0:32, (c + 1) * Lc - 1:(c + 1) * Lc], scale=-1.0)
                for c in range(NC):
                    sl = slice(c * Lc, (c + 1) * Lc)
                    P.transpose(PS[7][0:Lc, 0:32], DTs[0:32, sl], IDF[0:32, 0:32])
                    P.transpose(PS[7][0:Lc, 32:64], DTE[0:32, sl], IDF[0:32, 0:32])
                    P.copy(DTK[0:Lc, :], PS[7][0:Lc, 0:64])
                    for k in range(32):
                        o_ = PS[k // 8][0:Lc, (k % 8) * Lc:(k % 8 + 1) * Lc]
                        sel_s = V(IDF, IDF.t[0:32, k:k + 1].to_broadcast([32, Lc]))
                        P.mm(o_, sel_s, ACS[0:32, sl], start=True, stop=False)
                        P.mm(o_, NACS[0:32, sl], sel_s, start=False, stop=False)
                        P.mm(o_, IDF[0:Lc, 0:Lc], MASKNEG[0:Lc, 0:Lc], start=False, stop=True)
                    for q in range(4):
                        P.act(ES[0:Lc, q * 512:q * 512 + 8 * Lc], PS[q][0:Lc, 0:8 * Lc], AF.Exp)
                    for g in range(8):
                        P.mm(PS[6][0:Lc, g * Lc:(g + 1) * Lc], XN[:, g, sl], XN[:, 8 + g, sl])
                    for q in range(4):
                        es4 = V(ES, ES.t[0:Lc, q * 512:q * 512 + 8 * Lc].rearrange("p (g k l) -> p g k l", g=2, k=4))
                        st4 = V(STT, STT.t[0:Lc, q * 512:q * 512 + 8 * Lc].rearrange("p (g k l) -> p g k l", g=2, k=4))
                        cb4 = V(PS[6], PS[6].t[0:Lc, q * 2 * Lc:(q * 2 + 2) * Lc].rearrange("p (g l) -> p g l", g=2).unsqueeze(2).to_broadcast([Lc, 2, 4, Lc]))
                        P.tt(st4, es4, cb4, ALU.mult)
                    for kb in range(16):
                        P.transpose(PS[kb // 4][0:Lc, (kb % 4) * 128:(kb % 4 + 1) * 128], XA[:, kb, sl], IDF[:, :])
                    for q in range(4):
                        px = V(PS[q], PS[q].t[0:Lc, :].rearrange("p (k d) -> p k d", k=8))
                        xd = V(XDT, XDT.t[0:Lc, q * 512:(q + 1) * 512].rearrange("p (k d) -> p k d", k=8))
                        xw = V(XW, XW.t[0:Lc, q * 512:(q + 1) * 512].rearrange("p (k d) -> p k d", k=8))
                        dtb = V(DTK, DTK.t[0:Lc, q * 8:(q + 1) * 8].unsqueeze(2).to_broadcast([Lc, 8, 64]))
                        deb = V(DTK, DTK.t[0:Lc, 32 + q * 8:32 + (q + 1) * 8].unsqueeze(2).to_broadcast([Lc, 8, 64]))
                        P.tt(xd, px, dtb, ALU.mult)
                        P.tt(xw, xd, deb, ALU.mult)
                    for k in range(32):
                        sel_p = V(IDF, IDF.t[0:32, k:k + 1].to_broadcast([32, 128]))
                        P.mm(PS[k // 8][:, (k % 8) * Lc:(k % 8 + 1) * Lc], sel_p, EA[0:32, sl])
                    for q in range(4):
                        pe4 = V(PS[q], PS[q].t[:, 0:8 * Lc].rearrange("p (g k l) -> p g k l", g=2, k=4))
                        cd4 = V(CDEC, CDEC.t[:, q * 512:q * 512 + 8 * Lc].rearrange("p (g k l) -> p g k l", g=2, k=4))
                        cc4 = V(XN, XN.t[:, 8 + 2 * q:10 + 2 * q, sl].unsqueeze(2).to_broadcast([128, 2, 4, Lc]))
                        P.tt(cd4, cc4, pe4, ALU.mult)
                        P.copy(EAEND[:, q * 8:(q + 1) * 8],
                               V(PS[q], PS[q].t[:, 0:8 * Lc].rearrange("p (k l) -> p k l", k=8)[:, :, Lc - 1]))
                    for g in range(8):
                        P.mm(PS[4 + g // 4][0:Lc, (g % 4) * 128:(g % 4 + 1) * 128], XN[:, g, sl], IDB[:, :])
                    P.copy(BTK[0:Lc, 0:512], PS[4][0:Lc, :], eng="act")
                    P.copy(BTK[0:Lc, 512:1024], PS[5][0:Lc, :], eng="act")
                    for k in range(32):
                        kb = k // 2
                        o_ = PS[4 + kb // 8][(k % 2) * 64:(k % 2) * 64 + 64, (kb % 8) * Lc:(kb % 8 + 1) * Lc]
                        col = (k // 8) * 512 + (k % 8) * Lc
                        P.mm(o_, XDT[0:Lc, k * 64:(k + 1) * 64], STT[0:Lc, col:col + Lc], start=True, stop=False)
                        P.mm(o_, SB[:, k * 64:(k + 1) * 64], CDEC[:, col:col + Lc], start=False, stop=True)
                    for kb in range(16):
                        P.stt(XA[:, kb, sl], XA[:, kb, sl], SMALL[:, c0 + 2 + kb:c0 + 3 + kb],
                              PS[4 + kb // 8][:, (kb % 8) * Lc:(kb % 8 + 1) * Lc], ALU.mult, ALU.add)
                    for g in range(8):
                        P.mm(PS[g // 2][:, (g % 2) * 256:(g % 2 + 1) * 256], BTK[0:Lc, g * 128:(g + 1) * 128],
                             XW[0:Lc, g * 256:(g + 1) * 256])
                    s3 = V(ST32, ST32.t[:, 0:2048].rearrange("p (k d) -> p k d", k=32))
                    P.tt(s3, s3, V(EAEND, EAEND.t[:, :].unsqueeze(2).to_broadcast([128, 32, 64])), ALU.mult)
                    for q in range(4):
                        P.tt(S32[:, q * 512:(q + 1) * 512], S32[:, q * 512:(q + 1) * 512], PS[q][:, :], ALU.add)
                    P.copy(SB, S32, eng="act")
                zo = SEGOFF["z"]
                for kb in range(16):
                    st = stage()
                    P.dma(st[:, 0:n], preg(j, zo + kb * 128, zo + (kb + 1) * 128))
                    P.tt(XA[:, kb, 0:n], XA[:, kb, 0:n], st[:, 0:n], ALU.mult)
                for g in range(8):
                    rstd_from_blocks([XA[:, 2 * g, 0:n], XA[:, 2 * g + 1, 0:n]], n, 256, RSTD)
                    for kb in (2 * g, 2 * g + 1):
                        yb = YB[kb % 2]
                        P.stt(yb[:, 0:n], XA[:, kb, 0:n], chv(L, "ssd_norm_w", kb), RSTD[:, 0:n], ALU.mult, ALU.mult)
                        P.dma(P.region(("ybr", j), ybr[0, kb * 128:(kb + 1) * 128, t0:t0 + n]), yb[:, 0:n])
            P.fence([BIG] + subs)


        GSM = P.sbuf("gsm", [128, 272], F32)

        def gla(L):
            QD = subbuf("qd", BIG, 0, 4096, BF16)
            KD = subbuf("kd", BIG, 4096, 8192, BF16)
            QDEC = subbuf("qdec", BIG, 8192, 12288, BF16)
            KDEC = subbuf("kdec", BIG, 12288, 16384, BF16)
            VTK = subbuf("vtk", BIG, 16384, 18432, BF16)
            ATT = subbuf("att", BIG, 18432, 18688, BF16)
            KDT = subbuf("kdt", BIG, 18688, 19712, BF16)
            W2 = subbuf("w2", BIG, 19712, 21760, F32)
            subs = [QD, KD, QDEC, KDEC, VTK, ATT, KDT, W2]
            P.fence([BIG] + subs)
            GCS = Buf("gcs", XN.t[:, :, :].rearrange("p a b -> p (a b)").bitcast(F32).rearrange("p (a b) -> p a b", a=8))
            P.fence([XN, GCS])
            P.dma(W2[0:16, :], V(ext, wi["gla_w2"][L]))
            for d in range(8):
                P.ts(GSM[:, 256 + d:257 + d], chv(L, "gla_b", d), -1.0, ALU.mult)
            S3 = V(ST32, ST32.t[:, :].rearrange("p (d v) -> p d v", d=8))
            SB3 = V(STB, STB.t[:, :].rearrange("p (d v) -> p d v", d=8))
            P.memset(ST32[:, :], 0.0); P.memset(STB[:, :], 0.0)
            gsm4 = GSM.t[:, 0:256].rearrange("p (a d c) -> p a d c", a=4, d=8)
            qo, ko, vo, ro, go = (SEGOFF[x_] for x_ in ("q", "k", "v", "r", "glr"))

            def q4(ap3, n):
                return ap3[:, :, 0:n].rearrange("p d (c l) -> p d c l", l=min(64, n))
            for j in range(NT):
                t0, n = tiles[j]
                Lc = min(64, n); NC = n // Lc
                gl = CV[0]
                P.dma(gl[0:16, 0:n], preg(j, go, go + 16))
                for d in range(8):
                    ps = PS[d % 4]
                    P.mm(ps[:, 0:n], W2[0:16, d * 128:(d + 1) * 128], gl[0:16, 0:n])
                    st = stage()
                    P.act(st[:, 0:n], ps[:, 0:n], AF.Exp, bias=GSM[:, 256 + d:257 + d], scale=-1.0)
                    P.act(st[:, 0:n], st[:, 0:n], AF.Ln, bias=1.0)
                    P.ts(st[:, 0:n], st[:, 0:n], -1.0 / 16.0, ALU.mult)
                    P.scan(GCS[:, d, 0:n], MASK64[:, 0:n], st[:, 0:n], 0.0)
                g4 = q4(GCS.t, n)
                if n >= 64:
                    P.ts(V(GSM, gsm4[:, 0, :, 0:NC]), V(GCS, g4[:, :, :, 31]), -1.0, ALU.mult)
                    P.copy(V(GSM, gsm4[:, 1, :, 0:NC]), V(GCS, g4[:, :, :, 31]))
                else:
                    P.memset(V(GSM, gsm4[:, 0:2, :, 0:NC]), 0.0, eng="dve")
                P.copy(V(GSM, gsm4[:, 2, :, 0:NC]), V(GCS, g4[:, :, :, Lc - 1]))
                P.act(V(GSM, gsm4[:, 3, :, 0:NC]), V(GSM, gsm4[:, 2, :, 0:NC]), AF.Exp)
                for d in range(8):
                    qs = stage()
                    P.dma(qs[:, 0:n], preg(j, qo + d * 128, qo + (d + 1) * 128))
                    e1 = stage()
                    P.act(e1[:, 0:n], GCS[:, d, 0:n], AF.Exp)
                    P.tt(QDEC[:, d * 512:d * 512 + n], qs[:, 0:n], e1[:, 0:n], ALU.mult)
                    e2 = stage()
                    for c in range(NC):
                        sl = slice(c * Lc, (c + 1) * Lc)
                        P.act(e2[:, sl], GCS[:, d, sl], AF.Exp, bias=V(GSM, gsm4[:, 0, d, c:c + 1]))
                    P.tt(QD[:, d * 512:d * 512 + n], qs[:, 0:n], e2[:, 0:n], ALU.mult)
                    ks = stage()
                    P.dma(ks[:, 0:n], preg(j, ko + d * 128, ko + (d + 1) * 128))
                    e3 = stage()
                    for c in range(NC):
                        sl = slice(c * Lc, (c + 1) * Lc)
                        P.act(e3[:, sl], GCS[:, d, sl], AF.Exp, bias=V(GSM, gsm4[:, 1, d, c:c + 1]), scale=-1.0)
                    P.tt(KD[:, d * 512:d * 512 + n], ks[:, 0:n], e3[:, 0:n], ALU.mult)
                    e4 = stage()
                    for c in range(NC):
                        sl = slice(c * Lc, (c + 1) * Lc)
                        P.act(e4[:, sl], GCS[:, d, sl], AF.Exp, bias=V(GSM, gsm4[:, 2, d, c:c + 1]), scale=-1.0)
                    P.tt(KDEC[:, d * 512:d * 512 + n], ks[:, 0:n], e4[:, 0:n], ALU.mult)
                for c in range(NC):
                    sl = slice(c * Lc, (c + 1) * Lc)
                    vst3 = V(VST, VST.t[:, :].rearrange("p (kb l) -> p kb l", kb=16)[:, :, 0:Lc])
                    P.dma(vst3, P.region(("proj", j), projS["v"][:, t0 + c * Lc:t0 + (c + 1) * Lc].rearrange("(kb p) l -> p kb l", p=128)))
                    for kb in range(16):
                        P.transpose(PS[kb // 4][0:Lc, (kb % 4) * 128:(kb % 4 + 1) * 128], vst3[:, kb, :], IDF[:, :])
                    for q in range(4):
                        P.copy(VTK[0:Lc, q * 512:(q + 1) * 512], PS[q][0:Lc, :], eng="act" if q % 2 else "dve")
                    for h in range(4):
                        for d2 in range(2):
                            d = h * 2 + d2
                            P.mm(PS[4][0:Lc, h * Lc:(h + 1) * Lc], KD[:, d * 512 + c * Lc:d * 512 + (c + 1) * Lc],
                                 QD[:, d * 512 + c * Lc:d * 512 + (c + 1) * Lc], start=(d2 == 0), stop=(d2 == 1))
                    P.tt(V(ATT, ATT.t[0:Lc, 0:4 * Lc].rearrange("p (h l) -> p h l", h=4)),
                         V(PS[4], PS[4].t[0:Lc, 0:4 * Lc].rearrange("p (h l) -> p h l", h=4)),
                         V(CAUS, CAUS.t[0:Lc, 0:Lc].unsqueeze(1).to_broadcast([Lc, 4, Lc])), ALU.mult)
                    for d in range(8):
                        P.mm(PS[5 + d // 4][0:Lc, (d % 4) * 128:(d % 4 + 1) * 128], KDEC[:, d * 512 + c * Lc:d * 512 + (c + 1) * Lc], IDB[:, :])
                    P.copy(KDT[0:Lc, 0:512], PS[5][0:Lc, :], eng="act")
                    P.copy(KDT[0:Lc, 512:1024], PS[6][0:Lc, :])
                    for vb in range(16):
                        h = vb // 4
                        o_ = PS[vb // 8][:, (vb % 8) * Lc:(vb % 8 + 1) * Lc]
                        P.mm(o_, VTK[0:Lc, vb * 128:(vb + 1) * 128], ATT[0:Lc, h * Lc:(h + 1) * Lc], start=True, stop=False)
                        for d2 in range(2):
                            d = h * 2 + d2
                            P.mm(o_, SB3[:, d, (vb % 4) * 128:(vb % 4 + 1) * 128], QDEC[:, d * 512 + c * Lc:d * 512 + (c + 1) * Lc],
                                 start=False, stop=(d2 == 1))
                    for hf in range(2):
                        P.copy(XA[:, hf * 8:(hf + 1) * 8, sl], V(PS[hf], PS[hf].t[:, 0:8 * Lc].rearrange("p (a l) -> p a l", a=8)),
                               eng="act" if hf else "dve")
                    for d in range(8):
                        h = d // 2
                        ps = PS[2 + d % 2]
                        P.mm(ps[:, :], KDT[0:Lc, d * 128:(d + 1) * 128], VTK[0:Lc, h * 512:(h + 1) * 512])
                        P.stt(S3[:, d, :], S3[:, d, :], V(GSM, gsm4[:, 3, d, c:c + 1]), ps[:, :], ALU.mult, ALU.add)
                        P.copy(SB3[:, d, :], S3[:, d, :], eng="act")
                for h in range(4):
                    rstd_from_blocks([XA[:, 4 * h + i, 0:n] for i in range(4)], n, 512, RSTD)
                    for vb in range(4 * h, 4 * h + 4):
                        rs = stage()
                        P.dma(rs[:, 0:n], preg(j, ro + vb * 128, ro + (vb + 1) * 128))
                        tm = stage()
                        P.stt(tm[:, 0:n], XA[:, vb, 0:n], chv(L, "gla_norm_w", vb), RSTD[:, 0:n], ALU.mult, ALU.mult)
                        yb = YB[vb % 2]
                        P.tt(yb[:, 0:n], tm[:, 0:n], rs[:, 0:n], ALU.mult)
                        P.dma(P.region(("ybr", j), ybr[1, vb * 128:(vb + 1) * 128, t0:t0 + n]), yb[:, 0:n])
            P.fence([BIG] + subs)
            P.fence([XN, GCS])


        IOTA = P.sbuf("iota", [128, 512], F32)
        S5S = P.sbuf("s5s", [128, 16, 64], F32)
        TWO_PI = float(2 * np.pi); PI = float(np.pi)
        P.op("pool", lambda e: e.iota(IOTA.t[:, :], pattern=[[1, 512]], base=1, channel_multiplier=0,
                                      allow_small_or_imprecise_dtypes=True), reads=[], writes=[IOTA[:, :]])

        def wrap_pm_pi(r, tmp, lower=True):
            P.ts(tmp, r, PI, ALU.is_gt)
            P.stt(r, tmp, -TWO_PI, r, ALU.mult, ALU.add)
            if lower:
                P.ts(tmp, r, -PI, ALU.is_lt)
                P.stt(r, tmp, TWO_PI, r, ALU.mult, ALU.add)

        def sincos(sin_out, cos_out, ang, W):
            ti = stage(); tf = stage(); tm = stage()
            tiv = V(ti, ti.t[:, 0:W].bitcast(I32))
            P.ts(tiv, ang, 1.0 / TWO_PI, ALU.mult)
            P.copy(tf[:, 0:W], tiv)
            P.stt(tf[:, 0:W], tf[:, 0:W], -TWO_PI, ang, ALU.mult, ALU.add)
            wrap_pm_pi(tf[:, 0:W], tm[:, 0:W])
            P.act(sin_out, tf[:, 0:W], AF.Sin)
            P.ts(tf[:, 0:W], tf[:, 0:W], PI / 2, ALU.add)
            wrap_pm_pi(tf[:, 0:W], tm[:, 0:W], lower=False)
            P.act(cos_out, tf[:, 0:W], AF.Sin)
            return tf

        def s5(L):
            NS = 4
            W = 256
            sets = []
            subs = []
            for k in range(NS):
                d = {}
                base = k * 5376
                for i, nm in enumerate(["cos", "sin", "are", "aim", "magt", "vre", "vim", "wre", "wim"]):
                    d[nm] = subbuf("s5_%s%d" % (nm, k), BIG, base + i * 512, base + (i + 1) * 512, F32)
                for i, nm in enumerate(["xreb", "ximb", "ub"]):
                    d[nm] = subbuf("s5_%s%d" % (nm, k), BIG, base + 4608 + i * 256, base + 4608 + (i + 1) * 256, BF16)
                sets.append(d); subs += list(d.values())
            P.fence([BIG] + subs)
            BTR = subbuf("s5_btr", XN, 0, 2048, BF16); BTI = subbuf("s5_bti", XN, 2048, 4096, BF16)
            CTR = subbuf("s5_ctr", XN, 4096, 6144, BF16); CTI = subbuf("s5_cti", XN, 6144, 8192, BF16)
            P.fence([XN, BTR, BTI, CTR, CTI])
            st16 = ST32.t[:, :].bitcast(BF16)
            BT3R = Buf("s5_bt3r", st16[:, 0:2048]); BT3I = Buf("s5_bt3i", st16[:, 2048:4096])
            P.fence([ST32, BT3R, BT3I])
            LAMRE, LAMIM, DTV, MAG, THR, CORE, COIM, CARRE, CARIM, TA, TB, TC, TD, TE = (S5S[:, i, :] for i in range(14))
            for dst, nm in ((LAMRE, "s5_lam_re"), (LAMIM, "s5_lam_im")):
                src = wi[nm][L].rearrange("(q two) p -> q two p", two=2)
                P.dma(NAT[0:64, 0, 0:64], V(ext, src[:, 0, :]))
                P.dma(NAT[0:64, 0, 64:128], V(ext, src[:, 1, :]))
                P.transpose(PS[0][:, 0:64], NAT[0:64, 0, :], IDF[0:64, 0:64])
                P.copy(dst, PS[0][:, 0:64])
            P.dma(NAT[0:64, 1, 0:2], V(ext, wi["s5_log_dt"][L].rearrange("(q two) -> q two", two=2)))
            P.copy(V(NAT, NAT.t[0:64, 2, :].rearrange("p (two d) -> p two d", two=2)),
                   V(NAT, NAT.t[0:64, 1, 0:2].unsqueeze(2).to_broadcast([64, 2, 64])))
            P.transpose(PS[0][:, 0:64], NAT[0:64, 2, :], IDF[0:64, 0:64])
            P.act(DTV, PS[0][:, 0:64], AF.Exp)
            P.tt(TA, LAMRE, DTV, ALU.mult)
            P.act(MAG, TA, AF.Exp)
            P.tt(THR, LAMIM, DTV, ALU.mult)
            sincos(TA, TB, THR, 64)
            ti = stage(); tm = stage()
            tiv = V(ti, ti.t[:, 0:64].bitcast(I32))
            P.ts(tiv, THR, 1.0 / TWO_PI, ALU.mult)
            P.copy(TC, tiv)
            P.stt(THR, TC, -TWO_PI, THR, ALU.mult, ALU.add)
            wrap_pm_pi(THR, tm[:, 0:64])
            P.tt(TC, MAG, TB, ALU.mult)
            P.tt(TD, MAG, TA, ALU.mult)
            P.ts(TC, TC, -1.0, ALU.add)
            P.tt(TA, LAMRE, LAMRE, ALU.mult)
            P.tt(TB, LAMIM, LAMIM, ALU.mult)
            P.tt(TA, TA, TB, ALU.add)
            P.recip(TA, TA)
            P.tt(CORE, TC, LAMRE, ALU.mult)
            P.tt(TB, TD, LAMIM, ALU.mult)
            P.tt(CORE, CORE, TB, ALU.add)
            P.tt(CORE, CORE, TA, ALU.mult)
            P.tt(COIM, TD, LAMRE, ALU.mult)
            P.tt(TB, TC, LAMIM, ALU.mult)
            P.tt(COIM, COIM, TB, ALU.subtract)
            P.tt(COIM, COIM, TA, ALU.mult)
            P.memset(CARRE, 0.0, eng="dve"); P.memset(CARIM, 0.0, eng="dve")
            xaf = XA.t[:, :, :].rearrange("p a b -> p (a b)")
            P.memset(XA[:, :, :], 0.0, eng="dve")
            for i, nm in enumerate(("s5_b_re", "s5_b_im")):
                bn = xaf[:, i * 2048:(i + 1) * 2048].rearrange("p (q c) -> p q c", c=32)
                src = wi[nm][L].rearrange("(q two) p j -> two p q j", two=2)
                P.dma(V(XA, bn[0:64, :, 0:16]), V(ext, src[0]))
                P.dma(V(XA, bn[64:128, :, 16:32]), V(ext, src[1]))
            for i, nm in enumerate(("s5_c_re", "s5_c_im")):
                cn = xaf[:, (2 + i) * 2048:(3 + i) * 2048].rearrange("p (blk c) -> p blk c", c=128)
                src = wi[nm][L].rearrange("(blk q4 two) i p -> q4 two i blk p", q4=4, two=2)
                for q4 in range(4):
                    for g2 in range(2):
                        r0 = q4 * 32 + g2 * 16
                        P.dma(V(XA, cn[r0:r0 + 16, :, g2 * 64:(g2 + 1) * 64]), V(ext, src[q4, g2]))
            for blk in range(16):
                for i, dstb in enumerate((BTR, BTI)):
                    bn = xaf[:, i * 2048 + blk * 128:i * 2048 + (blk + 1) * 128]
                    ps = PS[(2 * blk + i) % 8]
                    P.transpose(ps[:, 0:128], V(XA, bn), IDF[:, :])
                    P.copy(dstb[:, blk * 128:(blk + 1) * 128], ps[:, 0:128], eng="act" if i else "dve")
                    ps3 = PS[(2 * blk + i + 2) % 8]
                    P.transpose(ps3[0:32, 0:128], V(XA, bn[:, 96:128]), IDF[:, :])
                    P.copy((BT3R, BT3I)[i][0:32, blk * 128:(blk + 1) * 128], ps3[0:32, 0:128], eng="dve" if i else "act")
                for i, dstb in enumerate((CTR, CTI)):
                    cn = xaf[:, (2 + i) * 2048 + blk * 128:(2 + i) * 2048 + (blk + 1) * 128]
                    ps = PS[(2 * blk + i + 4) % 8]
                    P.transpose(ps[:, 0:128], V(XA, cn), IDF[:, :])
                    if i == 0:
                        P.copy(dstb[:, blk * 128:(blk + 1) * 128], ps[:, 0:128], eng="act")
                    else:
                        P.ts(dstb[:, blk * 128:(blk + 1) * 128], ps[:, 0:128], -1.0, ALU.mult)
            so = SEGOFF["s5u"]
            steps = []
            for j in range(NT):
                t0, n = tiles[j]
                for w0 in range(0, n, W):
                    steps.append((j, t0 + w0, min(W, n - w0)))

            def pair_gen(q, k):
                d = sets[k]
                COS, SIN, ARE, AIM, MAGT = d["cos"], d["sin"], d["are"], d["aim"], d["magt"]
                VRE, VIM, WRE, WIM = d["vre"], d["vim"], d["wre"], d["wim"]
                XREb, XIMb, UB = d["xreb"], d["ximb"], d["ub"]
                blk, q4 = divmod(q, 4)
                rows = slice(q4 * 32, q4 * 32 + 32) if q4 < 3 else slice(0, 32)
                btr = BTR[rows, blk * 128:(blk + 1) * 128] if q4 < 3 else BT3R[0:32, blk * 128:(blk + 1) * 128]
                bti = BTI[rows, blk * 128:(blk + 1) * 128] if q4 < 3 else BT3I[0:32, blk * 128:(blk + 1) * 128]
                pb, py = PS[2 * k], PS[2 * k + 1]
                ang, tiF, tf, tm = WRE, VRE, VIM, WIM
                tiv = V(tiF, tiF.t[:, 0:W].bitcast(I32))
                P.ts(ang[:, 0:W], IOTA[:, 0:W], THR[:, q:q + 1], ALU.mult); yield
                P.ts(tiv, ang[:, 0:W], 1.0 / TWO_PI, ALU.mult); yield
                P.copy(tf[:, 0:W], tiv); yield
                P.stt(tf[:, 0:W], tf[:, 0:W], -TWO_PI, ang[:, 0:W], ALU.mult, ALU.add); yield
                P.ts(tm[:, 0:W], tf[:, 0:W], PI, ALU.is_gt); yield
                P.stt(tf[:, 0:W], tm[:, 0:W], -TWO_PI, tf[:, 0:W], ALU.mult, ALU.add); yield
                P.ts(tm[:, 0:W], tf[:, 0:W], -PI, ALU.is_lt); yield
                P.stt(tf[:, 0:W], tm[:, 0:W], TWO_PI, tf[:, 0:W], ALU.mult, ALU.add); yield
                P.act(SIN[:, 0:W], tf[:, 0:W], AF.Sin); yield
                P.ts(tf[:, 0:W], tf[:, 0:W], PI / 2, ALU.add); yield
                P.ts(tm[:, 0:W], tf[:, 0:W], PI, ALU.is_gt); yield
                P.stt(tf[:, 0:W], tm[:, 0:W], -TWO_PI, tf[:, 0:W], ALU.mult, ALU.add); yield
                P.act(COS[:, 0:W], tf[:, 0:W], AF.Sin); yield
                P.ts(ARE[:, 0:W], COS[:, 0:W], CORE[:, q:q + 1], ALU.mult); yield
                P.stt(ARE[:, 0:W], SIN[:, 0:W], COIM[:, q:q + 1], ARE[:, 0:W], ALU.mult, ALU.add); yield
                P.ts(AIM[:, 0:W], COS[:, 0:W], COIM[:, q:q + 1], ALU.mult); yield
                P.ts(tm[:, 0:W], SIN[:, 0:W], CORE[:, q:q + 1], ALU.mult); yield
                P.tt(AIM[:, 0:W], AIM[:, 0:W], tm[:, 0:W], ALU.subtract); yield
                P.ts(MAGT[:, 0:W], IOTA[:, 0:W], 0.0, ALU.mult, MAG[:, q:q + 1], ALU.add); yield
                r0 = so + blk * 128 + q4 * 32
                for (j, c0, n) in steps:
                    P.dma(WRE[rows, 0:n], P.region(("proj", j), proj_ap(r0, r0 + 32, c0, c0 + n))); yield
                    P.copy(UB[rows, 0:n], WRE[rows, 0:n], eng="act"); yield
                    P.mm(pb[:, 0:n], btr, UB[rows, 0:n]); yield
                    P.mm(pb[:, 256:256 + n], bti, UB[rows, 0:n]); yield
                    pr, pi_ = pb[:, 0:n], pb[:, 256:256 + n]
                    P.tt(VRE[:, 0:n], pr, ARE[:, 0:n], ALU.mult); yield
                    P.tt(WRE[:, 0:n], pi_, AIM[:, 0:n], ALU.mult); yield
                    P.tt(VRE[:, 0:n], VRE[:, 0:n], WRE[:, 0:n], ALU.subtract); yield
                    P.tt(VIM[:, 0:n], pi_, ARE[:, 0:n], ALU.mult); yield
                    P.tt(WIM[:, 0:n], pr, AIM[:, 0:n], ALU.mult); yield
                    P.tt(VIM[:, 0:n], VIM[:, 0:n], WIM[:, 0:n], ALU.add); yield
                    P.scan(WRE[:, 0:n], MAGT[:, 0:n], VRE[:, 0:n], CARRE[:, q:q + 1]); yield
                    P.scan(WIM[:, 0:n], MAGT[:, 0:n], VIM[:, 0:n], CARIM[:, q:q + 1]); yield
                    P.tt(VRE[:, 0:n], WRE[:, 0:n], COS[:, 0:n], ALU.mult, eng="pool"); yield
                    P.tt(VIM[:, 0:n], WIM[:, 0:n], SIN[:, 0:n], ALU.mult, eng="pool"); yield
                    P.tt(CARRE[:, q:q + 1], VRE[:, n - 1:n], VIM[:, n - 1:n], ALU.subtract, eng="pool"); yield
                    P.tt(XREb[:, 0:n], VRE[:, 0:n], VIM[:, 0:n], ALU.subtract, eng="pool"); yield
                    P.tt(VRE[:, 0:n], WRE[:, 0:n], SIN[:, 0:n], ALU.mult, eng="pool"); yield
                    P.tt(VIM[:, 0:n], WIM[:, 0:n], COS[:, 0:n], ALU.mult, eng="pool"); yield
                    P.tt(CARIM[:, q:q + 1], VRE[:, n - 1:n], VIM[:, n - 1:n], ALU.add, eng="pool"); yield
                    P.tt(XIMb[:, 0:n], VRE[:, 0:n], VIM[:, 0:n], ALU.add, eng="pool"); yield
                    P.mm(py[0:32, 0:n], CTR[:, blk * 128 + q4 * 32:blk * 128 + q4 * 32 + 32], XREb[:, 0:n], start=True, stop=False)
                    P.mm(py[0:32, 0:n], CTI[:, blk * 128 + q4 * 32:blk * 128 + q4 * 32 + 32], XIMb[:, 0:n], start=False, stop=True); yield
                    P.copy(VRE[0:32, 0:n], py[0:32, 0:n], eng="act"); yield
                    P.dma(P.region(("ys", j), ysT[blk * 128 + q4 * 32:blk * 128 + q4 * 32 + 32, c0:c0 + n]), VRE[0:32, 0:n]); yield

            pending = list(range(64))
            active = []
            for k in range(NS):
                active.append(pair_gen(pending.pop(0), k))
            slot_of = {id(g): k for k, g in enumerate(active)}
            for k, g in enumerate(active):
                for _ in range(S5STAG * k):
                    next(g)
            while active:
                for g in list(active):
                    try:
                        next(g)
                    except StopIteration:
                        k = slot_of.pop(id(g))
                        idx = active.index(g)
                        if pending:
                            ng = pair_gen(pending.pop(0), k)
                            slot_of[id(ng)] = k
                            active[idx] = ng
                        else:
                            active.remove(g)
            P.fence([BIG] + subs)
            P.fence([XN, BTR, BTI, CTR, CTI])
            P.fence([ST32, BT3R, BT3I])
            so = SEGOFF["s5u"]
            for j in range(NT):
                t0, n = tiles[j]
                for kb in range(16):
                    ysb = stage(); ub = stage(); t2 = stage()
                    P.dma(ysb[:, 0:n], P.region(("ys", j), ysT[kb * 128:(kb + 1) * 128, t0:t0 + n]))
                    P.dma(ub[:, 0:n], preg(j, so + kb * 128, so + (kb + 1) * 128))
                    P.stt(ysb[:, 0:n], ub[:, 0:n], chv(L, "s5_d", kb), ysb[:, 0:n], ALU.mult, ALU.add)
                    P.act(t2[:, 0:n], ysb[:, 0:n], AF.Square)
                    P.ts(t2[:, 0:n], t2[:, 0:n], 0.044715, ALU.mult, 1.0, ALU.add)
                    P.tt(t2[:, 0:n], t2[:, 0:n], ysb[:, 0:n], ALU.mult)
                    P.act(t2[:, 0:n], t2[:, 0:n], AF.Sigmoid, scale=1.5957691216057308)
                    P.tt(XA[:, kb, 0:n], ysb[:, 0:n], t2[:, 0:n], ALU.mult)
                    P.copy(XN[:, kb, 0:n], XA[:, kb, 0:n], eng="act")

                def cons(cb, ps, cw):
                    sg = stage()
                    P.act(sg[:, 0:n], ps, AF.Sigmoid, bias=chv(L, "s5_glu_b", cb))
                    yb = YB[cb % 2]
                    P.tt(yb[:, 0:n], XA[:, cb, 0:n], sg[:, 0:n], ALU.mult)
                    P.dma(P.region(("ybr", j), ybr[2, cb * 128:(cb + 1) * 128, t0:t0 + n]), yb[:, 0:n], eng="act")
                linear(WM[L, "glu"], lambda kb: XN[:, kb, 0:n], n, cons)


        def merge(L):
            go = SEGOFF["gate"]
            for j in range(NT):
                t0, n = tiles[j]
                for b in range(3):
                    P.dma(XN[:, :, 0:n], P.region(("ybr", j), ybr[b, :, t0:t0 + n].rearrange("(kb p) n -> p kb n", p=128)))

                    def cons(cb, ps, cw, b=b):
                        g = stage()
                        r0 = go + b * 2048 + cb * 128
                        P.dma(g[:, 0:n], preg(j, r0, r0 + 128))
                        if b == 0:
                            P.tt(XA[:, cb, 0:n], ps, g[:, 0:n], ALU.mult)
                        else:
                            P.tt(g[:, 0:n], ps, g[:, 0:n], ALU.mult)
                            P.tt(XA[:, cb, 0:n], XA[:, cb, 0:n], g[:, 0:n], ALU.add)
                    linear(WM[L, "br%d" % b], lambda kb: XN[:, kb, 0:n], n, cons)
                for kb in range(NKB):
                    P.copy(XN[:, kb, 0:n], XA[:, kb, 0:n], eng="act" if kb % 2 else "dve")

                def cons_o(cb, ps, cw):
                    P.copy(XA[:, cb, 0:n], ps, eng="act")
                linear(WM[L, "out"], lambda kb: XN[:, kb, 0:n], n, cons_o)
                rstd_from_blocks([XA[:, kb, 0:n] for kb in range(NKB)], n, D, RSTD)
                for kb in range(NKB):
                    hb = HB[kb % 2]
                    P.dma(hb[:, 0:n], hreg(j, kb))
                    t1 = T1[kb % 3]
                    P.stt(t1[:, 0:n], XA[:, kb, 0:n], chv(L, "ln_mix_post", kb), RSTD[:, 0:n], ALU.mult, ALU.mult)
                    P.tt(hb[:, 0:n], hb[:, 0:n], t1[:, 0:n], ALU.add)
                    P.dma(hreg(j, kb), hb[:, 0:n], eng="sp")

        class _sub:
            def __init__(s, wm, p0, p1):
                s.wm = wm; s.NP = p1 - p0; s.NC = wm.NC; s.nkb = wm.nkb; s.N = (p1 - p0) * 512; s.grp = wm.grp
                s.t = wm.t[p0:p1]

        for L in range(depth):
            if not skip_ffn:
                ffn(L, "f1", "ln_f1_pre", "ln_f1_post")
            if stop_after == "ffn1":
                break
            layer_smalls(L)
            inproj(L)
            if stop_after == "inproj":
                break
            if not skip_ssd:
                ssd(L)
            if stop_after == "ssd":
                break
            if not skip_gla:
                gla(L)
            if stop_after == "gla":
                break
            s5(L)
            if stop_after == "s5":
                break
            merge(L)
            if stop_after == "mix":
                break
            if not skip_ffn:
                ffn(L, "f2", "ln_f2_pre", "ln_f2_post")

        finals = []
        for j in range(1, NT):
            t0, n = tiles[j]
            P.dma(XA[:, :, 0:n], hreg(j))
            for s0 in range(0, n, 128):
                tm = V(XN, XN.t[:, 0:8, :].bitcast(F32).rearrange("p a b -> p (a b)"))
                for kb in range(NKB):
                    ps = PS[kb % 8]
                    P.transpose(ps[:, 0:128], XA[:, kb, s0:s0 + 128], IDF[:, :])
                    P.copy(tm[:, kb * 128:(kb + 1) * 128], ps[:, 0:128], eng="act" if kb % 2 else "dve")
                r0 = t0 - NMETA + s0
                finals.append(P.dma(P.region(("out", j), out_d[r0:r0 + 128, :]), tm, eng="sp"))
        st = P.emit(finals)
        print("ops/waits per engine:", st, "sems:", P.nsem)
    return nc


N_CORES = 8
S5VAR = "split"
S5STAG = 11


def kernel(**inputs):
    seq = inputs["x"].shape[1]
    depth = inputs["w_in"].shape[0]
    nc = build_program(seq, depth)
    in_maps = []
    for c in range(N_CORES):
        b = c % inputs["x"].shape[0]
        m = {k: np.ascontiguousarray(v) for k, v in inputs.items() if k != "x"}
        m["x"] = np.ascontiguousarray(inputs["x"][b])
        in_maps.append(m)
    res = run_bass_kernel_spmd(nc, in_maps, core_ids=list(range(N_CORES)))
    out = np.stack([res.results[b]["out"] for b in range(inputs["x"].shape[0])], axis=0)
    return out.astype(np.float32)
```
